# Optimizing a Trainium2 kernel written in Bass

```python
import jax, jax.numpy as jnp
from jax import lax
import numpy as np

D_MODEL = 2048
BATCH = 1
SEQ = 16384
DEPTH = 1

D_MIX = 2 * D_MODEL
D_ATTN = D_MODEL // 1
D_SSM = D_MIX - D_ATTN

N_HEADS = 16
HEAD_DIM = D_ATTN // N_HEADS
N_KV_HEADS = 4
Q_PER_KV = N_HEADS // N_KV_HEADS
KV_DIM = N_KV_HEADS * HEAD_DIM
CMP_LEN = 32
CMP_STRIDE = 16
CMP_HIDDEN = 256
SEL_BLOCK = 64
N_SEL = 16
WINDOW = 512
Q_BLOCK = 128

SSM_HEAD_DIM = 64
SSM_HEADS = D_SSM // SSM_HEAD_DIM
SSM_GROUPS = 4
SSM_STATE = 128
SSM_CONV = 4
SSM_CHUNK = 256
D_CONV_CH = D_SSM + 2 * SSM_GROUPS * SSM_STATE

D_FF = 5632
FFN_CONV = 3

IN_SIZES = (D_SSM, D_CONV_CH, SSM_HEADS, D_ATTN, KV_DIM, KV_DIM, KV_DIM, KV_DIM, KV_DIM, KV_DIM, 3 * N_HEADS)
D_IN = D_SSM + D_CONV_CH + SSM_HEADS + D_ATTN + 6 * KV_DIM + 3 * N_HEADS

NORM_EPS = 1e-6
NEG_INF = -1e30
FORCE_SCORE = 1e4

kernel_name = 'hymba_ssd_nsa_convffn_block'


def rms_norm(x, w):
    xf = x.astype(jnp.float32)
    y = xf * lax.rsqrt(jnp.mean(xf * xf, axis=-1, keepdims=True) + NORM_EPS)
    return (y * w.astype(jnp.float32)).astype(x.dtype)


def masked_softmax(s, mask):
    p = jax.nn.softmax(jnp.where(mask, s, NEG_INF), axis=-1)
    return jnp.where(mask, p, 0.0)


def causal_depthwise_conv(u, w, b):
    k, c = w.shape
    y = lax.conv_general_dilated(u, w[:, None, :].astype(u.dtype), window_strides=(1,),
                                 padding=[(k - 1, 0)], dimension_numbers=('NWC', 'WIO', 'NWC'),
                                 feature_group_count=c)
    return y + b.astype(u.dtype)


def alibi_slopes(n):
    return 2.0 ** (-8.0 * jnp.arange(1, n + 1, dtype=jnp.float32) / n)


def ssd_chunked(xs, dt, a_log, d_skip, b_mat, c_mat):
    bsz, L = xs.shape[:2]
    G, R, P, N, Q = SSM_GROUPS, SSM_HEADS // SSM_GROUPS, SSM_HEAD_DIM, SSM_STATE, SSM_CHUNK
    n_chunks = -(-L // Q)
    pad = n_chunks * Q - L
    pad_l = lambda u: jnp.pad(u, [(0, 0), (0, pad)] + [(0, 0)] * (u.ndim - 2))
    f32 = jnp.float32
    x = pad_l(xs).reshape(bsz, n_chunks, Q, G, R, P).astype(f32)
    dtc = pad_l(dt).reshape(bsz, n_chunks, Q, G, R)
    bc = pad_l(b_mat).reshape(bsz, n_chunks, Q, G, N).astype(f32)
    cc = pad_l(c_mat).reshape(bsz, n_chunks, Q, G, N).astype(f32)
    A = -jnp.exp(a_log.astype(f32)).reshape(G, R)
    a = (dtc * A).transpose(0, 1, 3, 4, 2)
    a_cum = jnp.cumsum(a, axis=-1)
    causal = jnp.tril(jnp.ones((Q, Q), dtype=bool))
    decay_in = jnp.exp(jnp.where(causal, a_cum[..., :, None] - a_cum[..., None, :], NEG_INF))
    xdt = x * dtc[..., None]
    cb = jnp.einsum('bclgn,bcsgn->bcgls', cc, bc)
    y_diag = jnp.einsum('bcgrls,bcsgrp->bclgrp', cb[:, :, :, None] * decay_in, xdt)
    decay_out = jnp.exp(a_cum[..., -1:] - a_cum).transpose(0, 1, 4, 2, 3)
    states = jnp.einsum('bclgn,bclgrp->bcgrpn', bc, xdt * decay_out[..., None])
    chunk_decay = jnp.exp(a_cum[..., -1])

    def step(h, inp):
        s_c, d_c = inp
        return h * d_c[..., None, None] + s_c, h

    h0 = jnp.zeros((bsz, G, R, P, N), f32)
    _, prev = lax.scan(step, h0, (jnp.moveaxis(states, 1, 0), jnp.moveaxis(chunk_decay, 1, 0)))
    prev = jnp.moveaxis(prev, 0, 1)
    decay_to = jnp.exp(a_cum).transpose(0, 1, 4, 2, 3)
    y_off = jnp.einsum('bclgn,bcgrpn->bclgrp', cc, prev) * decay_to[..., None]
    y = y_diag + y_off + x * d_skip.astype(f32).reshape(G, R)[:, :, None]
    return y.reshape(bsz, n_chunks * Q, G * R * P)[:, :L].astype(xs.dtype)


def compress_blocks(kv, pos, w1, w2):
    L = kv.shape[2]
    n_cmp = (L - CMP_LEN) // CMP_STRIDE + 1
    idx = jnp.arange(n_cmp)[:, None] * CMP_STRIDE + jnp.arange(CMP_LEN)[None, :]
    blk = kv[:, :, idx] + pos.astype(kv.dtype)
    flat = blk.reshape(blk.shape[:3] + (CMP_LEN * HEAD_DIM,))
    return jax.nn.gelu(flat @ w1) @ w2


def nsa_attention(q, kc_raw, vc_raw, ks, vs, kw, vw, gate_logits, pos_k, w1_k, w2_k, pos_v, w1_v, w2_v):
    f32 = jnp.float32
    bsz, L = q.shape[:2]
    G, R = N_KV_HEADS, Q_PER_KV
    qh = q.reshape(bsz, L, G, R, HEAD_DIM).transpose(0, 2, 3, 1, 4)
    to_g = lambda u: u.transpose(0, 2, 1, 3)
    kc = compress_blocks(to_g(kc_raw), pos_k, w1_k, w2_k)
    vc = compress_blocks(to_g(vc_raw), pos_v, w1_v, w2_v)
    n_cmp = kc.shape[2]
    n_blk = L // SEL_BLOCK
    n_sel = min(N_SEL, n_blk)
    ksb = to_g(ks).reshape(bsz, G, n_blk, SEL_BLOCK, HEAD_DIM)
    vsb = to_g(vs).reshape(bsz, G, n_blk, SEL_BLOCK, HEAD_DIM)
    win_pad = ((0, 0), (0, 0), (WINDOW, 0), (0, 0))
    kw_pad = jnp.pad(to_g(kw), win_pad)
    vw_pad = jnp.pad(to_g(vw), win_pad)
    slopes = alibi_slopes(N_HEADS).reshape(G, R)[None, :, :, None, None]
    scale = HEAD_DIM ** -0.5
    cmp_end = jnp.arange(n_cmp) * CMP_STRIDE + CMP_LEN - 1
    a_r, b_r = SEL_BLOCK // CMP_STRIDE, CMP_LEN // CMP_STRIDE
    jb = jnp.arange(n_blk)
    mn = (jnp.arange(a_r)[:, None] + jnp.arange(b_r)[None, :]).reshape(-1)
    imp_idx = a_r * (jb[:, None] + 1) - 1 - mn[None, :] + (b_r - 1)
    bi = jnp.arange(bsz)[:, None, None, None]
    gi = jnp.arange(G)[None, :, None, None]
    offs_sel = jnp.arange(SEL_BLOCK)
    offs_win = jnp.arange(Q_BLOCK + WINDOW) - WINDOW

    def scores(s, dist):
        return s.astype(f32) * scale - slopes * dist.astype(f32)

    def query_block(qb):
        t0 = qb * Q_BLOCK
        t = t0 + jnp.arange(Q_BLOCK)
        qblk = lax.dynamic_slice_in_dim(qh, t0, Q_BLOCK, axis=3)
        dist_c = t[:, None] - cmp_end[None, :]
        p_c = masked_softmax(scores(jnp.einsum('bgrqd,bgkd->bgrqk', qblk, kc), dist_c), dist_c >= 0)
        o_c = jnp.einsum('bgrqk,bgkd->bgrqd', p_c.astype(vc.dtype), vc)
        p_pad = jnp.pad(p_c.sum(axis=2), ((0, 0), (0, 0), (0, 0), (b_r - 1, b_r - 1)))
        imp = p_pad[..., imp_idx].sum(-1)
        jt = (t // SEL_BLOCK)[:, None]
        forced = (jb == 0) | (jb == jt) | (jb == jt - 1)
        imp = jnp.where(forced, FORCE_SCORE, jnp.where(jb <= jt, imp, -1.0))
        _, sel = lax.top_k(imp, n_sel)
        k_sel = ksb[bi, gi, sel].reshape(bsz, G, Q_BLOCK, n_sel * SEL_BLOCK, HEAD_DIM)
        v_sel = vsb[bi, gi, sel].reshape(bsz, G, Q_BLOCK, n_sel * SEL_BLOCK, HEAD_DIM)
        pos_sel = (sel[..., None] * SEL_BLOCK + offs_sel).reshape(bsz, G, Q_BLOCK, n_sel * SEL_BLOCK)
        dist_s = (t[:, None] - pos_sel)[:, :, None]
        p_s = masked_softmax(scores(jnp.einsum('bgrqd,bgqkd->bgrqk', qblk, k_sel), dist_s), dist_s >= 0)
        o_s = jnp.einsum('bgrqk,bgqkd->bgrqd', p_s.astype(v_sel.dtype), v_sel)
        k_win = lax.dynamic_slice_in_dim(kw_pad, t0, Q_BLOCK + WINDOW, axis=2)
        v_win = lax.dynamic_slice_in_dim(vw_pad, t0, Q_BLOCK + WINDOW, axis=2)
        pos_w = t0 + offs_win
        dist_w = t[:, None] - pos_w[None, :]
        valid_w = (dist_w >= 0) & (dist_w < WINDOW) & (pos_w[None, :] >= 0)
        p_w = masked_softmax(scores(jnp.einsum('bgrqd,bgkd->bgrqk', qblk, k_win), dist_w), valid_w)
        o_w = jnp.einsum('bgrqk,bgkd->bgrqd', p_w.astype(v_win.dtype), v_win)
        return o_c, o_s, o_w

    o_c, o_s, o_w = lax.map(query_block, jnp.arange(L // Q_BLOCK))
    merge = lambda o: o.transpose(1, 2, 3, 0, 4, 5).reshape(bsz, G, R, L, HEAD_DIM).astype(f32)
    g = jax.nn.sigmoid(gate_logits.astype(f32)).reshape(bsz, L, 3, G, R).transpose(2, 0, 3, 4, 1)[..., None]
    o = g[0] * merge(o_c) + g[1] * merge(o_s) + g[2] * merge(o_w)
    return o.transpose(0, 3, 1, 2, 4).reshape(bsz, L, N_HEADS * HEAD_DIM).astype(q.dtype)


def setup_inputs(seed: int = 0) -> dict:
    key = jax.random.key(seed)
    ks = jax.random.split(key, 24)
    nrm = lambda k, shape, s: jax.random.normal(k, shape, jnp.float32) * s
    gain = lambda k, shape: 1.0 + 0.02 * jax.random.normal(k, shape, jnp.float32)
    dt0 = jnp.exp(jax.random.uniform(ks[5], (DEPTH, SSM_HEADS), jnp.float32, np.log(1e-3), np.log(1e-1)))
    return {
        'x': nrm(ks[0], (BATCH, SEQ, D_MODEL), 1.0),
        'mix_norm_w': gain(ks[1], (DEPTH, D_MODEL)),
        'w_in': nrm(ks[2], (DEPTH, D_MODEL, D_IN), D_MODEL ** -0.5),
        'ssm_conv_w': nrm(ks[3], (DEPTH, SSM_CONV, D_CONV_CH), SSM_CONV ** -0.5),
        'ssm_conv_b': nrm(ks[4], (DEPTH, D_CONV_CH), 0.01),
        'ssm_dt_bias': dt0 + jnp.log(-jnp.expm1(-dt0)),
        'ssm_a_log': jnp.log(jax.random.uniform(ks[6], (DEPTH, SSM_HEADS), jnp.float32, 1.0, 16.0)),
        'ssm_d': 1.0 + 0.1 * jax.random.normal(ks[7], (DEPTH, SSM_HEADS), jnp.float32),
        'ssm_norm_w': gain(ks[8], (DEPTH, D_SSM)),
        'cmp_pos_k': nrm(ks[9], (DEPTH, CMP_LEN, HEAD_DIM), 0.1),
        'cmp_w1_k': nrm(ks[10], (DEPTH, CMP_LEN * HEAD_DIM, CMP_HIDDEN), (CMP_LEN * HEAD_DIM) ** -0.5),
        'cmp_w2_k': nrm(ks[11], (DEPTH, CMP_HIDDEN, HEAD_DIM), CMP_HIDDEN ** -0.5),
        'cmp_pos_v': nrm(ks[12], (DEPTH, CMP_LEN, HEAD_DIM), 0.1),
        'cmp_w1_v': nrm(ks[13], (DEPTH, CMP_LEN * HEAD_DIM, CMP_HIDDEN), (CMP_LEN * HEAD_DIM) ** -0.5),
        'cmp_w2_v': nrm(ks[14], (DEPTH, CMP_HIDDEN, HEAD_DIM), CMP_HIDDEN ** -0.5),
        'attn_norm_w': gain(ks[15], (DEPTH, D_ATTN)),
        'w_out': nrm(ks[16], (DEPTH, D_MIX, D_MODEL), D_MIX ** -0.5),
        'ffn_norm_w': gain(ks[17], (DEPTH, D_MODEL)),
        'w_up': nrm(ks[18], (DEPTH, D_MODEL, 2 * D_FF), D_MODEL ** -0.5),
        'ffn_conv_w': nrm(ks[19], (DEPTH, FFN_CONV, 2 * D_FF), FFN_CONV ** -0.5),
        'ffn_conv_b': nrm(ks[20], (DEPTH, 2 * D_FF), 0.01),
        'w_down': nrm(ks[21], (DEPTH, D_FF, D_MODEL), D_FF ** -0.5),
        'final_norm_w': gain(ks[22], (D_MODEL,)),
    }


def reference(x, mix_norm_w, w_in, ssm_conv_w, ssm_conv_b, ssm_dt_bias, ssm_a_log, ssm_d, ssm_norm_w,
              cmp_pos_k, cmp_w1_k, cmp_w2_k, cmp_pos_v, cmp_w1_v, cmp_w2_v, attn_norm_w, w_out,
              ffn_norm_w, w_up, ffn_conv_w, ffn_conv_b, w_down, final_norm_w):
    bsz, L, _ = x.shape
    split_at = [int(v) for v in np.cumsum(IN_SIZES)[:-1]]
    kv_shape = lambda u: u.reshape(bsz, L, N_KV_HEADS, HEAD_DIM)
    for l in range(DEPTH):
        h = rms_norm(x, mix_norm_w[l])
        proj = h @ w_in[l]
        z, xbc, dt_raw, q, kc, vc, ksl, vsl, kwn, vwn, gl = jnp.split(proj, split_at, axis=-1)
        xbc = jax.nn.silu(causal_depthwise_conv(xbc, ssm_conv_w[l], ssm_conv_b[l]))
        xs, bm, cm = jnp.split(xbc, [D_SSM, D_SSM + SSM_GROUPS * SSM_STATE], axis=-1)
        dt = jax.nn.softplus(dt_raw.astype(jnp.float32) + ssm_dt_bias[l].astype(jnp.float32))
        y_ssm = ssd_chunked(xs.reshape(bsz, L, SSM_HEADS, SSM_HEAD_DIM), dt, ssm_a_log[l], ssm_d[l],
                            bm.reshape(bsz, L, SSM_GROUPS, SSM_STATE), cm.reshape(bsz, L, SSM_GROUPS, SSM_STATE))
        y_ssm = rms_norm(y_ssm * jax.nn.silu(z), ssm_norm_w[l])
        y_attn = nsa_attention(q.reshape(bsz, L, N_HEADS, HEAD_DIM), kv_shape(kc), kv_shape(vc),
                               kv_shape(ksl), kv_shape(vsl), kv_shape(kwn), kv_shape(vwn),
                               gl.reshape(bsz, L, 3, N_HEADS), cmp_pos_k[l], cmp_w1_k[l], cmp_w2_k[l],
                               cmp_pos_v[l], cmp_w1_v[l], cmp_w2_v[l])
        y_attn = rms_norm(y_attn, attn_norm_w[l])
        x = x + jnp.concatenate([y_attn, y_ssm], axis=-1) @ w_out[l]
        h = rms_norm(x, ffn_norm_w[l])
        u = causal_depthwise_conv(h @ w_up[l], ffn_conv_w[l], ffn_conv_b[l])
        gate, val = jnp.split(u, 2, axis=-1)
        x = x + (jax.nn.silu(gate) * val) @ w_down[l]
    return rms_norm(x, final_norm_w)
```

```python
import numpy as np
from contextlib import ExitStack
import ml_dtypes
import concourse.bass as bass
import concourse.mybir as mybir
from concourse.bass_utils import run_bass_kernel_spmd

F32 = mybir.dt.float32
BF16 = mybir.dt.bfloat16
AF = mybir.ActivationFunctionType
ALU = mybir.AluOpType
AX = mybir.AxisListType

SAME_ENGINE_SYNC = True


class Tl:
    def __init__(self, k, name, shape, dt, space="sbuf"):
        self.k = k
        self.name = name
        if space == "sbuf":
            self.t = k.es.enter_context(k.nc.sbuf_tensor(name, list(shape), dt))
        else:
            self.t = k.es.enter_context(k.nc.psum_tensor(name, list(shape), dt))
        self.lw = {}
        self.rd = {}
        self.prd = {}
        self.dcnts = {}

    def __getitem__(self, idx):
        return self.t[idx]

    def dsem(self, q):
        kind = "sw" if q == "pool" else "hw"
        key = ("d", self.name, kind)
        if key not in self.k.sems:
            self.k.sems[key] = self.k.es.enter_context(self.k.nc.semaphore("d%s_%s" % (kind, self.name)))
            self.dcnts[key] = 0
        return key


class Alias:
    def __init__(self, base, dtype):
        self.base = base
        self.t = base.t.bitcast(dtype)
        self.name = base.name

    def __getitem__(self, idx):
        return self.t[idx]

    lw = property(lambda s: s.base.lw, lambda s, v: setattr(s.base, "lw", v))
    rd = property(lambda s: s.base.rd, lambda s, v: setattr(s.base, "rd", v))
    prd = property(lambda s: s.base.prd, lambda s, v: setattr(s.base, "prd", v))


class Dr:
    def __init__(self):
        self.lw = {}
        self.rd = {}


class KB:
    def __init__(self, nc, es):
        self.nc = nc
        self.es = es
        self.E = {"pe": nc.tensor, "act": nc.scalar, "dve": nc.vector, "pool": nc.gpsimd, "sp": nc.sync}
        self.sems = {}
        self.cnt = {}
        for e in ("pe", "act", "dve", "pool"):
            self.sems[e] = es.enter_context(nc.semaphore("s_" + e))
            self.cnt[e] = 0
        self.waited = {e: {} for e in self.E}
        self.n_ins = 0

    def sb(self, name, shape, dt):
        return Tl(self, name, shape, dt, "sbuf")

    def ps(self, name, shape, dt=F32):
        return Tl(self, name, shape, dt, "psum")

    def _wait(self, eng, deps):
        for key, v in deps.items():
            if key == eng and (eng == "pe" or not SAME_ENGINE_SYNC):
                continue
            if self.waited[eng].get(key, 0) < v:
                self.E[eng].wait_ge(self.sems[key], v)
                self.waited[eng][key] = v
                self.n_ins += 1

    @staticmethod
    def _addd(deps, d):
        for key, v in d.items():
            if deps.get(key, 0) < v:
                deps[key] = v

    def op(self, eng, fn, reads=(), writes=(), parts=()):
        deps = {}
        for r in reads:
            self._addd(deps, r.lw)
        for w in writes:
            self._addd(deps, w.lw)
            self._addd(deps, w.rd)
        for w in parts:
            self._addd(deps, w.rd)
            self._addd(deps, getattr(w, "prd", {}))
        self._wait(eng, deps)
        ins = fn()
        self.cnt[eng] += 1
        c = self.cnt[eng]
        ins.then_inc(self.sems[eng], 1)
        self.n_ins += 1
        for r in reads:
            if r.rd.get(eng, 0) < c:
                r.rd[eng] = c
        for w in writes:
            w.lw = {eng: c}
            w.prd = w.rd
            w.rd = {}
        for w in parts:
            w.lw[eng] = c
        return ins

    def dma(self, q, out, in_, dst=None, src=None, part=False, after=(), marks=(), **kw):
        deps = {}
        for d in after:
            self._addd(deps, d.lw)
        if src is not None:
            self._addd(deps, src.lw)
        if dst is not None:
            if not part:
                self._addd(deps, dst.lw)
            else:
                self._addd(deps, dst.prd)
            self._addd(deps, dst.rd)
        self._wait(q, deps)
        ins = self.E[q].dma_start(out=out, in_=in_, **kw)
        self.n_ins += 1
        assert not (dst is not None and src is not None)
        if dst is not None:
            key = dst.dsem(q)
            dst.dcnts[key] += 16
            ins.then_inc(self.sems[key], 16)
            if part:
                dst.lw[key] = dst.dcnts[key]
            else:
                dst.lw = {key: dst.dcnts[key]}
                dst.prd = dst.rd
                dst.rd = {}
        elif src is not None:
            key = src.dsem(q)
            src.dcnts[key] += 16
            ins.then_inc(self.sems[key], 16)
            src.rd[key] = src.dcnts[key]
            for d in marks:
                d.lw[key] = src.dcnts[key]
        return ins

    def release(self, tiles):
        deps = {}
        for t in tiles:
            self._addd(deps, t.lw)
            self._addd(deps, t.rd)
            self._addd(deps, t.prd)
        for e in ("pe", "act", "dve", "pool", "sp"):
            self._wait(e, deps)

    def finish(self, tiles):
        deps = {}
        for t in tiles:
            self._addd(deps, t.lw)
            self._addd(deps, t.rd)
        self._wait("sp", deps)


def rms_tile_to_hT(k, xt, nw_col, ident_b, hT, col0, D, eps, ps_tr, ss, xn):
    nc = k.nc
    KC = D // 128
    k.op("act", lambda: nc.scalar.activation(out=xn[:], in_=xt[:, 0:D], func=AF.Square, accum_out=ss[:, 0:1]),
         reads=[xt], writes=[xn, ss])
    k.op("dve", lambda: nc.vector.tensor_scalar(out=ss[:, 1:2], in0=ss[:, 0:1], scalar1=1.0 / D, scalar2=eps,
                                                op0=ALU.mult, op1=ALU.add), reads=[ss], writes=[ss])
    k.op("act", lambda: nc.scalar.activation(out=ss[:, 2:3], in_=ss[:, 1:2], func=AF.Sqrt), reads=[ss], writes=[ss])
    k.op("dve", lambda: nc.vector.reciprocal(out=ss[:, 3:4], in_=ss[:, 2:3]), reads=[ss], writes=[ss])
    k.op("dve", lambda: nc.vector.tensor_scalar(out=xn[:], in0=xt[:, 0:D], scalar1=ss[:, 3:4], scalar2=None,
                                                op0=ALU.mult), reads=[xt, ss], writes=[xn])
    for g in range(KC // 4):
        pt = ps_tr[g % 2]
        for j in range(4):
            kc = g * 4 + j
            k.op("pe", lambda: nc.tensor.transpose(pt[:, j * 128:(j + 1) * 128], xn[:, kc * 128:(kc + 1) * 128],
                                                   ident_b[:]), reads=[xn, ident_b],
                 parts=[pt] if j else [], writes=[] if j else [pt])
        for j in range(4):
            kc = g * 4 + j
            if g % 2 == 0:
                k.op("dve", lambda: nc.vector.tensor_scalar(
                    out=hT[:, kc, col0:col0 + 128], in0=pt[:, j * 128:(j + 1) * 128],
                    scalar1=nw_col[:, kc:kc + 1], scalar2=None, op0=ALU.mult), reads=[pt, nw_col], parts=[hT])
            else:
                k.op("act", lambda: nc.scalar.activation(
                    out=hT[:, kc, col0:col0 + 128], in_=pt[:, j * 128:(j + 1) * 128],
                    func=AF.Copy, scale=nw_col[:, kc:kc + 1]), reads=[pt, nw_col], parts=[hT])


def rmsnorm_to_hT(k, x_ap, nw_col, ident_b, hT, n_tok_tiles, D, eps, ps_tr, xt_tiles, tok0=0):
    for tt in range(n_tok_tiles):
        xt = xt_tiles["x"][tt % 2]
        k.dma("sp", xt[:], x_ap[tt * 128:(tt + 1) * 128, :], dst=xt)
        rms_tile_to_hT(k, xt, nw_col, ident_b, hT, tok0 + tt * 128, D, eps, ps_tr,
                       xt_tiles["ss"][tt % 2], xt_tiles["xn"][tt % 2])


D_MODEL = 2048
SEQ = 16384
NCORES = 8
EPS = 1e-6
C_Z, C_XBC, C_DT, C_Q, C_KC, C_VC, C_KS, C_VS, C_KW, C_VW, C_GL, C_END = (
    0, 2048, 5120, 5152, 7200, 7712, 8224, 8736, 9248, 9760, 10272, 10320)


def p1_blocks():
    blocks = []
    r = 0
    for c0, c1 in ((C_XBC, C_DT), (C_Q, C_KC), (C_KC, C_VC), (C_VC, C_KS), (C_KS, C_VS), (C_KW, C_VW)):
        for c in range(c0, c1, 128):
            blocks.append((c, 128, "f", "fm", r))
            r += 128
    nfm = r
    blocks.append((C_DT, 32, "f", "dtT", 0))
    for j in range(4):
        blocks.append((C_Z + 512 * j, 512, "t", "z", 512 * j))
    blocks.append((C_VS, 512, "t", "vtm", 0))
    blocks.append((C_VW, 512, "t", "vtm", 512))
    blocks.append((C_GL, 48, "t", "gl", 0))
    return blocks, nfm


def build_proj(D, NT, NCOLS, blocks, outs_spec):
    nc = bass.Bass("TRN2", target_bir_lowering=False)
    KC = D // 128
    x = nc.dram_tensor("x", [NT, D], F32, kind="ExternalInput").ap()
    nw = nc.dram_tensor("nw", [128, KC], F32, kind="ExternalInput").ap()
    wtot = sum(128 * KC * b[1] for b in blocks)
    w = nc.dram_tensor("w", [max(wtot, 1)], F32, kind="ExternalInput").ap()
    idb = nc.dram_tensor("identb", [128, 128], BF16, kind="ExternalInput").ap()
    outs = {n: nc.dram_tensor(n, list(s), d, kind="ExternalOutput").ap() for n, (s, d) in outs_spec.items()}
    woff = [0]
    with ExitStack() as es:
        k = KB(nc, es)
        ident_b = k.sb("ident_b", [128, 128], BF16)
        nw_col = k.sb("nw_col", [128, KC], F32)
        hT = k.sb("hT", [128, KC, NT], BF16)
        xt_tiles = {
            "x": [k.sb(f"xt{i}", [128, D], F32) for i in range(2)],
            "ss": [k.sb(f"ss{i}", [128, 4], F32) for i in range(2)],
            "xn": [k.sb(f"xn{i}", [128, D], BF16) for i in range(2)],
        }
        ps_tr = [k.ps(f"ps_tr{i}", [128, 512], BF16) for i in range(2)]
        ps_mm = [k.ps(f"ps_mm{i}", [128, 512], F32) for i in range(4)]
        wst = [k.sb(f"wst{i}", [128, KC, 512], F32) for i in range(2)]
        wbf = [k.sb(f"wbf{i}", [128, KC, 512], BF16) for i in range(2)]
        ob = [k.sb(f"ob{i}", [128, max(NT, 512)], F32) for i in range(2)]
        k.dma("sp", ident_b[:], idb[:, :], dst=ident_b)
        k.dma("sp", nw_col[:], nw[:, :], dst=nw_col)
        rmsnorm_to_hT(k, x, nw_col, ident_b, hT, NT // 128, D, EPS, ps_tr, xt_tiles)
        mmi = 0
        for bi, (c0, ncol, kind, oname, o0) in enumerate(blocks):
            ws, wb = wst[bi % 2], wbf[bi % 2]
            odt = outs_spec[oname][1]
            wblk = w[woff[0]:woff[0] + 128 * KC * ncol].rearrange("(p k c) -> p k c", p=128, k=KC)
            woff[0] += 128 * KC * ncol
            k.dma("pool" if bi % 2 else "sp", ws[:, :, 0:ncol], wblk, dst=ws)
            half = KC // 2
            k.op("dve", lambda: nc.vector.tensor_copy(out=wb[:, 0:half, 0:ncol], in_=ws[:, 0:half, 0:ncol]),
                 reads=[ws], writes=[wb])
            k.op("pool", lambda: nc.gpsimd.tensor_copy(out=wb[:, half:KC, 0:ncol], in_=ws[:, half:KC, 0:ncol]),
                 reads=[ws], parts=[wb])
            o = ob[bi % 2]
            if kind == "f":
                ov = o.t.bitcast(odt) if odt != F32 else o.t
                for ch in range(NT // 512):
                    pm = ps_mm[mmi % 4]
                    mmi += 1
                    for kc in range(KC):
                        k.op("pe", lambda: nc.tensor.matmul(pm[0:ncol, :], lhsT=wb[:, kc, 0:ncol],
                                                            rhs=hT[:, kc, ch * 512:(ch + 1) * 512],
                                                            start=(kc == 0), stop=(kc == KC - 1)),
                             reads=[wb, hT], writes=[pm] if kc == 0 else [], parts=[pm] if kc else [])
                    if ch % 2 == 0:
                        k.op("act", lambda: nc.scalar.copy(out=ov[0:ncol, ch * 512:(ch + 1) * 512], in_=pm[0:ncol, :]),
                             reads=[pm], writes=[o] if ch == 0 else [], parts=[o] if ch else [])
                    else:
                        k.op("dve", lambda: nc.vector.tensor_copy(out=ov[0:ncol, ch * 512:(ch + 1) * 512], in_=pm[0:ncol, :]),
                             reads=[pm], parts=[o])
                k.dma("pool", outs[oname][o0:o0 + ncol, :], ov[0:ncol, 0:NT], src=o)
            else:
                ov = o.t.bitcast(odt) if odt != F32 else o.t
                for tt in range(NT // 128):
                    pm = ps_mm[mmi % 4]
                    mmi += 1
                    for kc in range(KC):
                        k.op("pe", lambda: nc.tensor.matmul(pm[:, 0:ncol], lhsT=hT[:, kc, tt * 128:(tt + 1) * 128],
                                                            rhs=wb[:, kc, 0:ncol],
                                                            start=(kc == 0), stop=(kc == KC - 1)),
                             reads=[wb, hT], writes=[pm] if kc == 0 else [], parts=[pm] if kc else [])
                    o = ob[(bi + tt) % 2]
                    ov = o.t.bitcast(odt) if odt != F32 else o.t
                    if tt % 2 == 0:
                        k.op("act", lambda: nc.scalar.copy(out=ov[:, 0:ncol], in_=pm[:, 0:ncol]), reads=[pm], writes=[o])
                    else:
                        k.op("dve", lambda: nc.vector.tensor_copy(out=ov[:, 0:ncol], in_=pm[:, 0:ncol]), reads=[pm], writes=[o])
                    k.dma("pool", outs[oname][tt * 128:(tt + 1) * 128, o0:o0 + ncol], ov[:, 0:ncol], src=o)
        k.finish(ob)
        print("proj kernel instructions:", k.n_ins)
    return nc


def build_tail(D, NT, DFF, GT=512):
    nc = bass.Bass("TRN2", target_bir_lowering=False)
    KC = D // 128
    NTH = NT + 128
    NJ = DFF // 128
    x = nc.dram_tensor("x", [NTH, D], F32, kind="ExternalInput").ap()
    ya = nc.dram_tensor("ya", [NTH, D], F32, kind="ExternalInput").ap()
    ys = nc.dram_tensor("ys", [NTH, D], F32, kind="ExternalInput").ap()
    nws = {n: nc.dram_tensor(n, [128, KC], F32, kind="ExternalInput").ap() for n in ("nwa", "nws", "nwf")}
    fwb = nc.dram_tensor("fwb", [128, D], F32, kind="ExternalInput").ap()
    w_out = nc.dram_tensor("w_out", [D // 256, 128, 2 * KC, 256], F32, kind="ExternalInput").ap()
    w_up = nc.dram_tensor("w_up", [NJ, 128, KC, 256], F32, kind="ExternalInput").ap()
    w_down = nc.dram_tensor("w_down", [D // 256, 128, NJ, 256], F32, kind="ExternalInput").ap()
    cw = nc.dram_tensor("cw", [128, 2 * NJ, 3], F32, kind="ExternalInput").ap()
    cb = nc.dram_tensor("cb", [128, 2 * NJ], F32, kind="ExternalInput").ap()
    idb = nc.dram_tensor("identb", [128, 128], BF16, kind="ExternalInput").ap()
    out = nc.dram_tensor("out", [NT, D], F32, kind="ExternalOutput").ap()
    x1d = nc.dram_tensor("x1", [NTH, D], F32, kind="ExternalOutput").ap()
    h2d = nc.dram_tensor("h2T", [128, KC, NTH], BF16, kind="ExternalOutput").ap()
    with ExitStack() as es:
        k = KB(nc, es)
        ident_b = k.sb("ident_b", [128, 128], BF16)
        nwc = {n: k.sb(n + "_c", [128, KC], F32) for n in nws}
        cw_t = k.sb("cw_t", [128, 2 * NJ, 3], F32)
        cb_t = k.sb("cb_t", [128, 2 * NJ], F32)
        fw_t = k.sb("fw_t", [128, D], F32)
        big = [k.sb(f"big{i}", [128, D], F32) for i in range(4)]
        xn = [k.sb(f"xn{i}", [128, D], BF16) for i in range(2)]
        ss = [k.sb(f"ss{i}", [128, 4], F32) for i in range(3)]
        ps_tr = [k.ps(f"ps_tr{i}", [128, 512], BF16) for i in range(2)]
        ps_mm = [k.ps(f"ps_mm{i}", [128, 512], F32) for i in range(4)]
        ps_cv = [k.ps(f"ps_cv{i}", [128, 512], F32) for i in range(2)]
        ps_h2 = [Alias(t_, F32) for t_ in ps_tr]
        wst = [k.sb(f"wst{i}", [128, 16, 256], F32) for i in range(2)]
        wbf = [k.sb(f"wbf{i}", [128, 16, 256], BF16) for i in range(2)]
        k.dma("sp", ident_b[:], idb[:, :], dst=ident_b)
        for n in nws:
            k.dma("sp", nwc[n][:], nws[n][:, :], dst=nwc[n])
        k.dma("sp", cw_t[:], cw[:, :, :], dst=cw_t)
        k.dma("sp", cb_t[:], cb[:, :], dst=cb_t)
        k.dma("sp", fw_t[:], fwb[:, :], dst=fw_t)
        x1_reg = [Dr() for _ in range(NTH // 128)]
        h2_reg = [Dr() for _ in range(NTH // 128)]
        wload = [0]

        def load_w(view_ap, nk, ncol):
            i = wload[0] % 2
            wload[0] += 1
            ws, wb = wst[i], wbf[i]
            h = nk // 2
            k.dma("sp", ws[:, 0:h, 0:ncol], view_ap[:, 0:h, :], dst=ws)
            k.dma("pool", ws[:, h:nk, 0:ncol], view_ap[:, h:nk, :], dst=ws, part=True)
            k.op("dve", lambda: nc.vector.tensor_copy(out=wb[:, 0:h, 0:ncol], in_=ws[:, 0:h, 0:ncol]),
                 reads=[ws], writes=[wb])
            k.op("pool", lambda: nc.gpsimd.tensor_copy(out=wb[:, h:nk, 0:ncol], in_=ws[:, h:nk, 0:ncol]),
                 reads=[ws], parts=[wb])
            return wb

        TA = 4
        tiles = list(range(NTH // 128))
        esA = ExitStack()
        es_outer, k.es = k.es, esA
        hTy_all = k.sb("hTy_all", [128, 2 * KC, TA * 128], BF16)
        hT2t = k.sb("hT2t", [128, KC, 128], BF16)
        for g0 in range(0, len(tiles), TA):
            grp = tiles[g0:g0 + TA]
            for ti, tt in enumerate(grp):
                for half, (src, nwn) in enumerate(((ya, "nwa"), (ys, "nws"))):
                    yt = big[(2 * ti + half) % 4]
                    k.dma("sp", yt[:], src[tt * 128:(tt + 1) * 128, :], dst=yt)
                    hv = hTy_all
                    rms_tile_to_hT(k, yt, nwc[nwn], ident_b, _KView(hTy_all, half * KC), ti * 128, D, EPS, ps_tr,
                                   ss[half], xn[half])
            for ti, tt in enumerate(grp):
                xt = big[ti % 4]
                k.dma("sp", xt[:], x[tt * 128:(tt + 1) * 128, :], dst=xt)
            mmi = 0
            for dc in range(D // 256):
                wbs = []
                for pc in range(2 * KC // 16):
                    wbs.append((pc, load_w(w_out[dc, :, pc * 16:(pc + 1) * 16, :], 16, 256)))
                    pcc, wb = wbs[-1]
                    for ti, tt in enumerate(grp):
                        pm = ps_mm[ti]
                        for kk in range(16):
                            kc = pcc * 16 + kk
                            first = (pcc == 0 and kk == 0)
                            last = (pcc == 2 * KC // 16 - 1 and kk == 15)
                            k.op("pe", lambda: nc.tensor.matmul(pm[:, 0:256], lhsT=hTy_all[:, kc, ti * 128:(ti + 1) * 128],
                                                                rhs=wb[:, kk, 0:256], start=first, stop=last),
                                 reads=[wb, hTy_all], writes=[pm] if first else [], parts=[] if first else [pm])
                for ti, tt in enumerate(grp):
                    xt = big[ti % 4]
                    pm = ps_mm[ti]
                    k.op("dve", lambda: nc.vector.tensor_tensor(out=xt[:, dc * 256:(dc + 1) * 256],
                                                                in0=pm[:, 0:256], in1=xt[:, dc * 256:(dc + 1) * 256],
                                                                op=ALU.add), reads=[pm, xt], parts=[xt])
            for ti, tt in enumerate(grp):
                xt = big[ti % 4]
                k.dma("pool", x1d[tt * 128:(tt + 1) * 128, :], xt[:], src=xt, marks=[x1_reg[tt]])
                rms_tile_to_hT(k, xt, nwc["nwf"], ident_b, _KView(hT2t, 0), 0, D, EPS, ps_tr, ss[2], xn[ti % 2])
                k.dma("pool", h2d[:, :, tt * 128:(tt + 1) * 128], hT2t[:], src=hT2t, marks=[h2_reg[tt]])

        k.release([hTy_all, hT2t])
        esA.close()
        esB = ExitStack()
        k.es = esB
        hT2 = k.sb("hT2", [128, KC, GT + 2], BF16)
        gT = k.sb("gT", [128, NJ, GT], BF16)
        ub = [k.sb(f"ub{i}", [128, GT + 2], BF16) for i in range(4)]
        uhalo = k.sb("uhalo", [128, 2 * NJ, 2], BF16)
        dg = [k.sb(f"dg{i}", [128, 6, 128], BF16) for i in range(2)]
        ga = [k.sb(f"ga{i}", [128, GT], F32) for i in range(2)]
        for gi in range(NT // GT):
            t0 = 128 + gi * GT
            regs = [h2_reg[i] for i in range((t0 - 2) // 128, (t0 + GT - 1) // 128 + 1)]
            k.dma("sp", hT2[:], h2d[:, :, t0 - 2:t0 + GT], dst=hT2, after=regs)
            def ffn_front(j):
                i = wload[0] % 2
                wload[0] += 1
                ws, wb = wst[i], wbf[i]
                h = KC // 2
                k.dma("sp", ws[:, 0:h, :], w_up[j, :, 0:h, :], dst=ws)
                k.dma("pool", ws[:, h:KC, :], w_up[j, :, h:KC, :], dst=ws, part=True)
                k.op("dve", lambda: nc.vector.tensor_copy(out=wb[:, 0:h, :], in_=ws[:, 0:h, :]), reads=[ws], writes=[wb])
                k.op("pool", lambda: nc.gpsimd.tensor_copy(out=wb[:, h:KC, :], in_=ws[:, h:KC, :]), reads=[ws], parts=[wb])
                dgt = dg[j % 2]
                for half in range(2):
                    cj = half * NJ + j
                    for kk in range(3):
                        k.op("pool", lambda: nc.gpsimd.tensor_scalar(out=dgt[:, half * 3 + kk, :], in0=ident_b[:],
                                                                     scalar1=cw_t[:, cj, kk:kk + 1], scalar2=None,
                                                                     op0=ALU.mult),
                             reads=[ident_b, cw_t], writes=[dgt] if (half == 0 and kk == 0) else [],
                             parts=[] if (half == 0 and kk == 0) else [dgt])
                for half in range(2):
                    cj = half * NJ + j
                    pm = ps_mm[(2 * j + half) % 4]
                    u = ub[2 * (j % 2) + half]
                    for kc in range(KC):
                        k.op("pe", lambda: nc.tensor.matmul(pm[:, 0:GT], lhsT=wb[:, kc, half * 128:(half + 1) * 128],
                                                            rhs=hT2[:, kc, 2:2 + GT], start=(kc == 0), stop=(kc == KC - 1)),
                             reads=[wb, hT2], writes=[pm] if kc == 0 else [], parts=[pm] if kc else [])
                    if gi == 0:
                        pm2 = ps_h2[half]
                        for kc in range(KC):
                            k.op("pe", lambda: nc.tensor.matmul(pm2[:, 0:2], lhsT=wb[:, kc, half * 128:(half + 1) * 128],
                                                                rhs=hT2[:, kc, 0:2], start=(kc == 0), stop=(kc == KC - 1)),
                                 reads=[wb, hT2], writes=[pm2] if kc == 0 else [], parts=[pm2] if kc else [])
                        k.op("act", lambda: nc.scalar.copy(out=u[:, 0:2], in_=pm2[:, 0:2]), reads=[pm2], writes=[u])
                    else:
                        k.op("act", lambda: nc.scalar.copy(out=u[:, 0:2], in_=uhalo[:, cj, :]), reads=[uhalo], writes=[u])
                    k.op("act", lambda: nc.scalar.copy(out=u[:, 2:2 + GT], in_=pm[:, 0:GT]), reads=[pm], parts=[u])
                    k.op("dve", lambda: nc.vector.tensor_copy(out=uhalo[:, cj, :], in_=u[:, GT:GT + 2]), reads=[u], parts=[uhalo])

            def ffn_back(j):
                dgt = dg[j % 2]
                for half in range(2):
                    u = ub[2 * (j % 2) + half]
                    pc = ps_cv[half]
                    for kk in range(3):
                        k.op("pe", lambda: nc.tensor.matmul(pc[:, 0:GT], lhsT=dgt[:, half * 3 + kk, :], rhs=u[:, kk:kk + GT],
                                                            start=(kk == 0), stop=(kk == 2)),
                             reads=[dgt, u], writes=[pc] if kk == 0 else [], parts=[pc] if kk else [])
                gat = ga[j % 2]
                k.op("act", lambda: nc.scalar.activation(out=gat[:], in_=ps_cv[0][:, 0:GT], func=AF.Silu,
                                                         bias=cb_t[:, j:j + 1]), reads=[ps_cv[0], cb_t], writes=[gat])
                k.op("dve", lambda: nc.vector.scalar_tensor_tensor(out=gT[:, j, :], in0=ps_cv[1][:, 0:GT],
                                                                   scalar=cb_t[:, NJ + j:NJ + j + 1], in1=gat[:],
                                                                   op0=ALU.add, op1=ALU.mult),
                     reads=[ps_cv[1], cb_t, gat], parts=[gT] if j else [], writes=[] if j else [gT])
            ffn_front(0)
            for j in range(NJ):
                if j + 1 < NJ:
                    ffn_front(j + 1)
                ffn_back(j)
            ntt = GT // 128
            for ti in range(ntt):
                xt = big[ti]
                r0 = t0 + ti * 128
                k.dma("sp", xt[:], x1d[r0:r0 + 128, :], dst=xt, after=[x1_reg[r0 // 128]])
            for dc in range(D // 256):
                pieces = [(p0, min(16, NJ - p0)) for p0 in range(0, NJ, 16)]
                for pi, (p0, npc) in enumerate(pieces):
                    wb = load_w(w_down[dc, :, p0:p0 + npc, :], npc, 256)
                    for ti in range(ntt):
                        pm = ps_mm[ti]
                        for kk in range(npc):
                            first = (pi == 0 and kk == 0)
                            last = (pi == len(pieces) - 1 and kk == npc - 1)
                            k.op("pe", lambda: nc.tensor.matmul(pm[:, 0:256], lhsT=gT[:, p0 + kk, ti * 128:(ti + 1) * 128],
                                                                rhs=wb[:, kk, 0:256], start=first, stop=last),
                                 reads=[wb, gT], writes=[pm] if first else [], parts=[] if first else [pm])
                for ti in range(ntt):
                    xt = big[ti]
                    pm = ps_mm[ti]
                    k.op("dve", lambda: nc.vector.tensor_tensor(out=xt[:, dc * 256:(dc + 1) * 256], in0=pm[:, 0:256],
                                                                in1=xt[:, dc * 256:(dc + 1) * 256], op=ALU.add),
                         reads=[pm, xt], parts=[xt])
            for ti in range(ntt):
                xt = big[ti]
                s_ = ss[ti % 3]
                xq = xn[ti % 2]
                k.op("act", lambda: nc.scalar.activation(out=xq[:], in_=xt[:], func=AF.Square, accum_out=s_[:, 0:1]),
                     reads=[xt], writes=[xq, s_])
                k.op("dve", lambda: nc.vector.tensor_scalar(out=s_[:, 1:2], in0=s_[:, 0:1], scalar1=1.0 / D, scalar2=EPS,
                                                            op0=ALU.mult, op1=ALU.add), reads=[s_], writes=[s_])
                k.op("act", lambda: nc.scalar.activation(out=s_[:, 2:3], in_=s_[:, 1:2], func=AF.Sqrt), reads=[s_], writes=[s_])
                k.op("dve", lambda: nc.vector.reciprocal(out=s_[:, 3:4], in_=s_[:, 2:3]), reads=[s_], writes=[s_])
                k.op("dve", lambda: nc.vector.scalar_tensor_tensor(out=xt[:], in0=xt[:], scalar=s_[:, 3:4], in1=fw_t[:],
                                                                   op0=ALU.mult, op1=ALU.mult),
                     reads=[xt, s_, fw_t], writes=[xt])
                r0 = gi * GT + ti * 128
                k.dma("pool", out[r0:r0 + 128, :], xt[:], src=xt)
        k.finish(big)
        k.release([hT2, gT, uhalo] + ub + dg + ga)
        esB.close()
        k.es = es_outer
        print("tail kernel instructions:", k.n_ins)
    return nc


class _KView:
    def __init__(self, tl, off):
        self.tl = tl
        self.off = off
        self.lw = tl.lw
        self.rd = tl.rd
        self.prd = tl.prd

    def __getitem__(self, idx):
        p, kc, c = idx
        return self.tl.t[p, kc + self.off, c]


NEG = -30000.0


def ssd_consts():
    c = {}
    c["identb"] = np.eye(128).astype(ml_dtypes.bfloat16)
    c["identf"] = np.eye(128).astype(np.float32)
    l = np.arange(256)[None, :]
    s = np.arange(128)[:, None]
    c["causneg"] = np.where(l >= s, 0.0, NEG).astype(np.float32)
    sel = np.zeros((4, 4, 128), np.float32)
    for h in range(4):
        sel[h, h, :] = 1.0
    c["sel4"] = sel
    c["ones4"] = np.ones((4, 128), np.float32)
    return c


def ssd_body(nc, k, L, io):
    NCH = L // 256
    with ExitStack() as es2:
        k.es, es_old = es2, k.es
        identb = k.sb("s_identb", [128, 128], BF16)
        identf = k.sb("s_identf", [128, 128], F32)
        causneg = k.sb("s_causneg", [128, 256], F32)
        sel4 = k.sb("s_sel4", [4, 4, 128], F32)
        ones4 = k.sb("s_ones4", [4, 128], F32)
        cw_t = k.sb("s_cw", [128, 4, 4], F32)
        cb_t = k.sb("s_cb", [128, 4], F32)
        cbrow = k.sb("s_cbrow", [1, 512], F32)
        drow = k.sb("s_drow", [128, 256], F32)
        dtb = k.sb("s_dtb", [4, 1], F32)
        alog = k.sb("s_alog", [4, 2], F32)
        for t, n in ((identb, "identb"), (identf, "identf"), (causneg, "causneg"), (sel4, "sel4"), (ones4, "ones4"),
                     (cw_t, "cw"), (cb_t, "cb"), (cbrow, "cbrow"), (drow, "drow"), (dtb, "dtb")):
            k.dma("sp", t[:], io[n], dst=t)
        k.dma("sp", alog[:, 0:1], io["alog"], dst=alog)
        dtt = k.sb("s_dtt", [4, L], F32)
        acum = k.sb("s_acum", [4, L], F32)
        PCS = min(L, 2048)
        t1 = k.sb("s_t1", [4, PCS], F32)
        t2 = k.sb("s_t2", [4, PCS], F32)
        rst = k.sb("s_rst", [4, PCS], F32)
        V, S_ = nc.vector, nc.scalar
        k.op("act", lambda: S_.activation(out=alog[:, 1:2], in_=alog[:, 0:1], func=AF.Exp), reads=[alog], writes=[alog])
        k.op("dve", lambda: V.memset(rst[:], 1.0), writes=[rst])
        k.op("dve", lambda: V.memset(rst[:].rearrange("p (c q) -> p c q", q=256)[:, :, 0:1], 0.0), writes=[rst])
        for pc in range(L // PCS):
            sl = slice(pc * PCS, (pc + 1) * PCS)
            k.dma("sp", t1[:], io["dtT"][:, sl], dst=t1)
            k.op("dve", lambda: V.tensor_scalar(out=t1[:], in0=t1[:], scalar1=dtb[:, 0:1], scalar2=None, op0=ALU.add),
                 reads=[dtb], writes=[t1])
            k.op("act", lambda: S_.activation(out=t2[:], in_=t1[:], func=AF.Abs), reads=[t1], writes=[t2])
            k.op("act", lambda: S_.activation(out=t2[:], in_=t2[:], func=AF.Exp, scale=-1.0), reads=[t2], writes=[t2])
            k.op("act", lambda: S_.activation(out=t2[:], in_=t2[:], func=AF.Ln, bias=1.0), reads=[t2], writes=[t2])
            k.op("dve", lambda: V.tensor_scalar(out=t1[:], in0=t1[:], scalar1=0.0, scalar2=None, op0=ALU.max), reads=[t1], writes=[t1])
            k.op("dve", lambda: V.tensor_tensor(out=dtt[:, sl], in0=t1[:], in1=t2[:], op=ALU.add), reads=[t1, t2], parts=[dtt])
            k.op("dve", lambda: V.tensor_scalar(out=t1[:], in0=dtt[:, sl], scalar1=alog[:, 1:2], scalar2=-1.0, op0=ALU.mult,
                                                op1=ALU.mult), reads=[dtt, alog], writes=[t1])
            k.op("dve", lambda: V.tensor_tensor_scan(out=acum[:, sl], data0=rst[:], data1=t1[:], initial=0.0, op0=ALU.mult,
                                                     op1=ALU.add), reads=[t1, rst], parts=[acum])
        dgc = k.sb("s_dgc", [128, 4, 4, 128], BF16)
        for t in range(4):
            for kk in range(4):
                k.op("dve", lambda: V.tensor_scalar(out=dgc[:, t, kk, :], in0=identb[:], scalar1=cw_t[:, t, kk:kk + 1],
                                                    scalar2=None, op0=ALU.mult), reads=[identb, cw_t], parts=[dgc])
        xbc = [k.sb(f"s_xbc{i}", [128, 4, 3 + 256], BF16) for i in range(2)]
        zt = [k.sb(f"s_z{i}", [128, 256], F32) for i in range(2)]
        BT = k.sb("s_BT", [128, 256], BF16)
        CT = k.sb("s_CT", [128, 256], BF16)
        xsf = [k.sb(f"s_xsf{i}", [128, 256], F32) for i in range(2)]
        Btok = k.sb("s_Btok", [128, 2, 128], BF16)
        xdt = [k.sb(f"s_xdt{i}", [128, 256], BF16) for i in range(2)]
        xdd = [k.sb(f"s_xdd{i}", [128, 256], BF16) for i in range(2)]
        sc = [k.sb(f"s_sc{i}", [128, 16], F32) for i in range(2)]
        cdt = k.sb("s_cd", [128, 8], F32)
        R4 = k.sb("s_R4", [4, 4], F32)
        tmpE = [k.sb(f"s_tmpE{i}", [128, 256], F32) for i in range(2)]
        MT = [[k.sb(f"s_MT{h}_{i}", [128, 256], BF16) for i in range(2)] for h in range(4)]
        H = k.sb("s_H", [128, 256], F32)
        Hb = k.sb("s_Hb", [128, 256], BF16)
        yos = k.sb("s_yos", [128, 256], F32)
        yv = [k.sb(f"s_yv{i}", [128, 256], F32) for i in range(2)]
        p_conv = k.ps("sp_conv", [128, 512], F32)
        p_small = k.ps("sp_small", [128, 512], F32)
        p_cb = [k.ps(f"sp_cb{i}", [128, 512], F32) for i in range(2)]
        p_D = [k.ps(f"sp_D{i}", [128, 512], F32) for i in range(2)]
        p_y = k.ps("sp_y", [128, 512], F32)
        p_st = k.ps("sp_st", [128, 512], F32)
        k.op("dve", lambda: V.memset(H[:], 0.0), writes=[H])
        k.op("dve", lambda: V.memset(Hb[:], 0.0), writes=[Hb])
        di = 0
        for c in range(NCH):
            T0 = c * 256
            xb = xbc[c % 2]
            k.dma("sp", xb[:], io["xbcT"][:, T0:T0 + 259].rearrange("(t p) n -> p t n", p=128), dst=xb)
            k.op("dve", lambda: V.tensor_scalar(out=R4[:], in0=identf[0:4, 0:4], scalar1=acum[:, T0 + 255:T0 + 256],
                                                scalar2=None, op0=ALU.mult), reads=[identf, acum], writes=[R4])
            k.op("pe", lambda: nc.tensor.matmul(p_small[:, 0:4], lhsT=ones4[:], rhs=R4[:], start=True, stop=True),
                 reads=[ones4, R4], writes=[p_small])
            k.op("dve", lambda: V.tensor_copy(out=cdt[:, 0:4], in_=p_small[:, 0:4]), reads=[p_small], writes=[cdt])
            k.op("act", lambda: S_.activation(out=cdt[:, 4:8], in_=cdt[:, 0:4], func=AF.Exp), reads=[cdt], parts=[cdt])
            for t, dst in ((2, BT), (3, CT)):
                for kk in range(4):
                    k.op("pe", lambda: nc.tensor.matmul(p_conv[:, 0:256], lhsT=dgc[:, t, kk, :], rhs=xb[:, t, kk:kk + 256],
                                                        start=(kk == 0), stop=(kk == 3)),
                         reads=[dgc, xb], writes=[p_conv] if kk == 0 else [], parts=[p_conv] if kk else [])
                k.op("act", lambda: S_.activation(out=dst[:], in_=p_conv[:, 0:256], func=AF.Silu, bias=cb_t[:, t:t + 1]),
                     reads=[p_conv, cb_t], writes=[dst])
            for i in range(2):
                s_ = sc[i]
                k.op("pe", lambda: nc.tensor.matmul(p_small[:, 8:12], lhsT=dtt[:, T0 + 128 * i:T0 + 128 * i + 128],
                                                    rhs=identf[0:4, 0:4], start=True, stop=True),
                     reads=[dtt, identf], writes=[p_small])
                k.op("pe", lambda: nc.tensor.matmul(p_small[:, 12:16], lhsT=acum[:, T0 + 128 * i:T0 + 128 * i + 128],
                                                    rhs=identf[0:4, 0:4], start=True, stop=True),
                     reads=[acum, identf], parts=[p_small])
                k.op("dve", lambda: V.tensor_copy(out=s_[:, 0:8], in_=p_small[:, 8:16]), reads=[p_small], writes=[s_])
                k.op("act", lambda: S_.activation(out=s_[:, 8:12], in_=s_[:, 4:8], func=AF.Exp), reads=[s_], parts=[s_])
                k.op("dve", lambda: V.tensor_tensor(out=s_[:, 12:16], in0=cdt[:, 0:4], in1=s_[:, 4:8], op=ALU.subtract),
                     reads=[cdt, s_], parts=[s_])
                k.op("act", lambda: S_.activation(out=s_[:, 12:16], in_=s_[:, 12:16], func=AF.Exp), reads=[s_], parts=[s_])
                k.op("dve", lambda: V.tensor_tensor(out=s_[:, 12:16], in0=s_[:, 12:16], in1=s_[:, 0:4], op=ALU.mult),
                     reads=[s_], parts=[s_])
                for t in range(3):
                    for kk in range(4):
                        k.op("pe", lambda: nc.tensor.matmul(p_conv[:, 256:384], lhsT=xb[:, t, kk + 128 * i:kk + 128 * i + 128],
                                                            rhs=dgc[:, t, kk, :], start=(kk == 0), stop=False),
                             reads=[dgc, xb], writes=[p_conv] if kk == 0 else [], parts=[p_conv] if kk else [])
                    k.op("pe", lambda: nc.tensor.matmul(p_conv[:, 256:384], lhsT=ones4[0:1, :], rhs=cbrow[0:1, t * 128:(t + 1) * 128],
                                                        start=False, stop=True), reads=[ones4, cbrow], parts=[p_conv])
                    if t < 2:
                        k.op("act", lambda: S_.activation(out=xsf[i][:, t * 128:(t + 1) * 128], in_=p_conv[:, 256:384],
                                                          func=AF.Silu), reads=[p_conv], writes=[xsf[i]] if t == 0 else [],
                             parts=[xsf[i]] if t else [])
                    else:
                        k.op("act", lambda: S_.activation(out=Btok[:, i, :], in_=p_conv[:, 256:384], func=AF.Silu),
                             reads=[p_conv], writes=[Btok] if i == 0 else [], parts=[Btok] if i else [])
                for h in range(4):
                    k.op("dve", lambda: V.tensor_scalar(out=xdt[i][:, 64 * h:64 * h + 64], in0=xsf[i][:, 64 * h:64 * h + 64],
                                                        scalar1=s_[:, h:h + 1], scalar2=None, op0=ALU.mult),
                         reads=[xsf[i], s_], writes=[xdt[i]] if h == 0 else [], parts=[xdt[i]] if h else [])
                    k.op("dve", lambda: V.tensor_scalar(out=xdd[i][:, 64 * h:64 * h + 64], in0=xsf[i][:, 64 * h:64 * h + 64],
                                                        scalar1=s_[:, 12 + h:13 + h], scalar2=None, op0=ALU.mult),
                         reads=[xsf[i], s_], writes=[xdd[i]] if h == 0 else [], parts=[xdd[i]] if h else [])
            for i in range(2):
                k.op("pe", lambda: nc.tensor.matmul(p_cb[i][:, 0:256], lhsT=BT[:, 128 * i:128 * i + 128], rhs=CT[:, :],
                                                    start=True, stop=True), reads=[BT, CT], writes=[p_cb[i]])
            for h in range(4):
                for i in range(2):
                    pd = p_D[di % 2]
                    te = tmpE[di % 2]
                    di += 1
                    l0 = 128 * i
                    k.op("pe", lambda: nc.tensor.matmul(pd[:, l0:256], lhsT=sel4[:, h, :], rhs=acum[:, T0 + l0:T0 + 256],
                                                        start=True, stop=True), reads=[sel4, acum], writes=[pd])
                    k.op("dve", lambda: V.scalar_tensor_tensor(out=te[:, l0:256], in0=pd[:, l0:256], scalar=sc[i][:, 4 + h:5 + h],
                                                               in1=causneg[:, 0:256 - l0], op0=ALU.subtract, op1=ALU.add),
                         reads=[pd, sc[i], causneg], writes=[te])
                    k.op("act", lambda: S_.activation(out=te[:, l0:256], in_=te[:, l0:256], func=AF.Exp), reads=[te], writes=[te])
                    k.op("dve", lambda: V.tensor_tensor(out=MT[h][i][:, l0:256], in0=p_cb[i][:, l0:256], in1=te[:, l0:256],
                                                        op=ALU.mult), reads=[p_cb[i], te], writes=[MT[h][i]])
            for li in range(2):
                z_ = zt[li]
                k.dma("sp", z_[:], io["z"][T0 + 128 * li:T0 + 128 * li + 128, :], dst=z_)
                for h in range(4):
                    for i in range(li + 1):
                        k.op("pe", lambda: nc.tensor.matmul(p_y[:, 64 * h:64 * h + 64], lhsT=MT[h][i][:, 128 * li:128 * li + 128],
                                                            rhs=xdt[i][:, 64 * h:64 * h + 64], start=(i == 0), stop=(i == li)),
                             reads=[MT[h][i], xdt[i]], writes=[p_y] if (h == 0 and i == 0) else [],
                             parts=[] if (h == 0 and i == 0) else [p_y])
                for h in range(4):
                    k.op("pe", lambda: nc.tensor.matmul(p_y[:, 256 + 64 * h:256 + 64 * h + 64], lhsT=CT[:, 128 * li:128 * li + 128],
                                                        rhs=Hb[:, 64 * h:64 * h + 64], start=True, stop=True),
                         reads=[CT, Hb], parts=[p_y])
                for h in range(4):
                    k.op("dve", lambda: V.tensor_scalar(out=yos[:, 64 * h:64 * h + 64], in0=p_y[:, 256 + 64 * h:256 + 64 * h + 64],
                                                        scalar1=sc[li][:, 8 + h:9 + h], scalar2=None, op0=ALU.mult),
                         reads=[p_y, sc[li]], writes=[yos] if h == 0 else [], parts=[yos] if h else [])
                y_ = yv[li]
                k.op("dve", lambda: V.tensor_tensor(out=y_[:], in0=p_y[:, 0:256], in1=yos[:], op=ALU.add),
                     reads=[p_y, yos], writes=[y_])
                k.op("dve", lambda: V.tensor_tensor(out=yos[:], in0=xsf[li][:], in1=drow[:], op=ALU.mult),
                     reads=[xsf[li], drow], writes=[yos])
                k.op("dve", lambda: V.tensor_tensor(out=y_[:], in0=y_[:], in1=yos[:], op=ALU.add), reads=[yos], writes=[y_])
                k.op("act", lambda: S_.activation(out=z_[:], in_=z_[:], func=AF.Silu), reads=[z_], writes=[z_])
                k.op("dve", lambda: V.tensor_tensor(out=y_[:], in0=y_[:], in1=z_[:], op=ALU.mult), reads=[z_], writes=[y_])
                k.dma("sp", io["ys"][T0 + 128 * li:T0 + 128 * li + 128, :], y_[:], src=y_)
            for i in range(2):
                k.op("pe", lambda: nc.tensor.matmul(p_st[:, 0:256], lhsT=Btok[:, i, :], rhs=xdd[i][:], start=(i == 0), stop=(i == 1)),
                     reads=[Btok, xdd[i]], writes=[p_st] if i == 0 else [], parts=[p_st] if i else [])
            for h in range(4):
                k.op("dve", lambda: V.scalar_tensor_tensor(out=H[:, 64 * h:64 * h + 64], in0=H[:, 64 * h:64 * h + 64],
                                                           scalar=cdt[:, 4 + h:5 + h], in1=p_st[:, 64 * h:64 * h + 64],
                                                           op0=ALU.mult, op1=ALU.add), reads=[cdt, p_st], writes=[H])
            k.op("dve", lambda: V.tensor_copy(out=Hb[:], in_=H[:]), reads=[H], writes=[Hb])
        k.finish(yv)
        k.es = es_old


def build_ssd(L):
    nc = bass.Bass("TRN2", target_bir_lowering=False)
    shapes = {"xbcT": ([512, 3 + L], BF16), "cw": ([128, 4, 4], F32), "cb": ([128, 4], F32), "cbrow": ([1, 512], F32),
              "dtT": ([4, L], F32), "dtb": ([4, 1], F32), "alog": ([4, 1], F32), "drow": ([128, 256], F32),
              "z": ([L, 256], F32), "identb": ([128, 128], BF16), "identf": ([128, 128], F32),
              "causneg": ([128, 256], F32), "sel4": ([4, 4, 128], F32), "ones4": ([4, 128], F32)}
    io = {}
    for n, (s, d) in shapes.items():
        t = nc.dram_tensor(n, s, d, kind="ExternalInput").ap()
        io[n] = t[tuple(slice(None) for _ in s)]
    io["xbcT"] = nc_ap(io["xbcT"])
    io["ys"] = nc.dram_tensor("ys", [L, 256], F32, kind="ExternalOutput").ap()
    with ExitStack() as es:
        k = KB(nc, es)
        ssd_body(nc, k, L, io)
        print("ssd kernel instructions:", k.n_ins)
    return nc


def nc_ap(a):
    return a


SCALE = 128 ** -0.5
TINY = 1e-30


def attn_consts(L):
    NB = L // 64
    NCT = max(1, (L // 16) // 128)
    c = {}
    j = np.arange(128)[:, None].astype(np.float64)
    i512 = np.arange(512)[None, :]
    c["R0"] = (i512 - j).astype(np.float32)
    c["R0c"] = (np.arange(128)[None, :] - 16 * j - 31).astype(np.float32)
    x = np.arange(2176)[None, :]
    c["Mc"] = np.where(x - 16 * j - 31 >= 0, 0.0, NEG).astype(ml_dtypes.bfloat16)
    x = np.arange(1408)[None, :]
    d = x - 384 - j
    c["Mw"] = np.where((d >= 0) & (d < 512), 0.0, NEG).astype(ml_dtypes.bfloat16)
    x = np.arange(896)[None, :]
    c["Mcaus"] = np.where(x - 384 - j >= 0, 0.0, NEG).astype(ml_dtypes.bfloat16)
    nbr = min(128, NB)
    es = np.zeros((nbr, 64, 128), np.float32)
    for kt in range(64):
        for jj in range(128):
            b = 2 * kt + jj // 64
            if b < nbr:
                es[b, kt, jj] = 1.0
    c["Esel"] = es.astype(ml_dtypes.bfloat16)
    A = np.zeros((NCT * 128, NB + 1), np.float32)
    for b in range(NB):
        for off, wgt in ((3, 1.0), (2, 2.0), (1, 2.0), (0, 2.0), (-1, 1.0)):
            r = 4 * b + off
            if 0 <= r < NCT * 128:
                A[r, b] += wgt
    A[:, NB] = 1.0
    c["Aext"] = A.reshape(NCT, 128, NB + 1).transpose(1, 0, 2).astype(ml_dtypes.bfloat16).copy()
    p = np.arange(128)[:, None]
    xx = np.arange(2 * NB)[None, :] - NB
    hi = (p >= 64).astype(np.int64)
    cand = (xx <= hi)
    c["cand"] = cand.astype(np.float32)
    c["negm"] = cand.astype(np.float32) - 1.0
    c["force"] = np.where((xx == hi) | (xx == hi - 1), 1e4, -2.0).astype(np.float32)
    mm = 128.0 * (np.arange(132) - 3)[None, :]
    c["Dtab"] = (mm - j).astype(np.float32)
    c["Dtab3"] = (mm - 16 * j - 31).astype(np.float32)
    c["irow"] = np.tile(np.arange(512, dtype=np.float32)[None, :], (128, 1))
    c["identb"] = np.eye(128).astype(ml_dtypes.bfloat16)
    return c


def attn_head_params(heads):
    sl = np.array([2.0 ** (-8.0 * (h + 1) / 16.0) for h in heads])
    row = np.concatenate([-sl / SCALE, -sl]).astype(np.float32)
    return np.tile(row[None, :], (128, 1))


def attn_body(nc, k, L, io):
    NB = L // 64
    NBR = min(128, NB)
    NBH = max(1, NB // 128)
    NCT = max(1, (L // 16) // 128)
    NKT = L // 128
    W = 512
    V, S_, PE = nc.vector, nc.scalar, nc.tensor
    with ExitStack() as es2:
        k.es, es_old = es2, k.es
        cst = {}
        for n, shp, dt in (("R0", [128, 512], F32), ("R0c", [128, 128], F32), ("Mc", [128, 2176], BF16),
                           ("Mw", [128, 1408], BF16), ("Mcaus", [128, 896], BF16), ("Esel", [NBR, 64, 128], BF16),
                           ("Aext", [128, NCT, NB + 1], BF16), ("cand", [128, 2 * NB], F32), ("negm", [128, 2 * NB], F32),
                           ("force", [128, 2 * NB], F32), ("Dtab", [128, 132], F32), ("Dtab3", [128, 132], F32), ("irow", [128, 512], F32),
                           ("identb", [128, 128], BF16),
                           ("hp", [128, 8], F32)):
            cst[n] = k.sb("a_" + n, shp, dt)
            k.dma("sp", cst[n][:], io[n], dst=cst[n])
        R0, R0c, Mc, Mw, Mcaus, Esel, Aext, identb, hp = (cst[n] for n in
                                                          ("R0", "R0c", "Mc", "Mw", "Mcaus", "Esel", "Aext", "identb", "hp"))
        Btab = k.sb("a_Btab", [128, 4, 132], F32)
        Btab3 = k.sb("a_Btab3", [128, 4, 132], F32)
        shrow = k.sb("a_shrow", [128, 4, 512], BF16)
        ones_b = k.sb("a_onesb", [128, 128], BF16)
        k.op("dve", lambda: V.memset(ones_b[:], 1.0), writes=[ones_b])
        for h in range(4):
            k.op("dve", lambda: V.tensor_scalar(out=Btab[:, h, :], in0=cst["Dtab"][:], scalar1=hp[:, 4 + h:5 + h], scalar2=None,
                                                op0=ALU.mult), reads=[cst["Dtab"], hp], parts=[Btab])
            k.op("dve", lambda: V.tensor_scalar(out=Btab3[:, h, :], in0=cst["Dtab3"][:], scalar1=hp[:, 4 + h:5 + h], scalar2=None,
                                                op0=ALU.mult), reads=[cst["Dtab3"], hp], parts=[Btab3])
            k.op("dve", lambda: V.tensor_scalar(out=shrow[:, h, :], in0=cst["irow"][:], scalar1=hp[:, h:h + 1], scalar2=None,
                                                op0=ALU.mult), reads=[cst["irow"], hp], parts=[shrow])
        pS = [k.ps(f"ap_S{i}", [128, 512], F32) for i in range(2)]
        pO = [k.ps(f"ap_O{i}", [128, 512], F32) for i in range(4)]
        pT = k.ps("ap_T", [128, 1024], BF16)
        pX = k.ps("ap_X", [128, 512], F32)
        ksT = k.sb("a_ksT", [128, L], BF16)
        vse = k.sb("a_vse", [128, NKT, 130], BF16)
        kccT = k.sb("a_kccT", [128, NCT * 128], BF16)
        vcce = k.sb("a_vcce", [128, NCT, 130], BF16)
        k.dma("sp", ksT[:], io["ksT"], dst=ksT)
        k.dma("sp", vse[:, :, 0:128], io["vs"].rearrange("(t p) d -> p t d", p=128), dst=vse)
        k.op("dve", lambda: V.memset(vse[:, :, 128:129], 1.0), parts=[vse])
        k.op("dve", lambda: V.memset(vcce[:, :, 128:129], 1.0), writes=[vcce])

        with ExitStack() as es3:
            k.es = es3
            src = k.sb("c_src", [128, L + 32], BF16)
            w1s = k.sb("c_w1s", [128, 8, 256], F32)
            w1b = k.sb("c_w1b", [128, 32, 256], BF16)
            w2s = k.sb("c_w2s", [128, 2, 128], F32)
            w2b = k.sb("c_w2b", [128, 2, 128], BF16)
            posf = k.sb("c_posf", [128, 32], F32)
            posb_ = k.sb("c_posb", [128, 32], BF16)
            pbias = k.sb("c_pbias", [128, 2], F32)
            xh = k.sb("c_xh", [128, 512], F32)
            t3 = k.sb("c_t3", [128, 512], F32)
            hidb = k.sb("c_hidb", [128, 2, NCT * 128], BF16)
            for which in ("k", "v"):
                k.op("dve", lambda: V.memset(src[:, L:L + 32], 0.0), writes=[src])
                k.dma("sp", src[:, 0:L], io["kcT" if which == "k" else "vcT"], dst=src, part=True)
                w1v = io["w1_" + which].rearrange("(i p) m -> p i m", p=128)
                for q4 in range(4):
                    k.dma("sp", w1s[:], w1v[:, q4 * 8:(q4 + 1) * 8, :], dst=w1s)
                    k.op("dve", lambda: V.tensor_copy(out=w1b[:, q4 * 8:(q4 + 1) * 8, :], in_=w1s[:]), reads=[w1s],
                         writes=[w1b] if q4 == 0 else [], parts=[w1b] if q4 else [])
                k.dma("sp", w2s[:], io["w2_" + which].rearrange("(t p) d -> p t d", p=128), dst=w2s)
                k.op("dve", lambda: V.tensor_copy(out=w2b[:], in_=w2s[:]), reads=[w2s], writes=[w2b])
                k.dma("sp", posf[:], io["posT_" + which], dst=posf)
                k.op("dve", lambda: V.tensor_copy(out=posb_[:], in_=posf[:]), reads=[posf], writes=[posb_])
                for mt in range(2):
                    for i in range(32):
                        k.op("pe", lambda: PE.matmul(pX[:, mt:mt + 1], lhsT=w1b[:, i, mt * 128:(mt + 1) * 128], rhs=posb_[:, i:i + 1],
                                                     start=(i == 0), stop=(i == 31)), reads=[w1b, posb_],
                             writes=[pX] if (i == 0 and mt == 0) else [], parts=[] if (i == 0 and mt == 0) else [pX])
                k.op("dve", lambda: V.tensor_copy(out=pbias[:], in_=pX[:, 0:2]), reads=[pX], writes=[pbias])
                CW = min(512, NCT * 128)
                for cc in range(NCT * 128 // CW):
                    for mt in range(2):
                        ps = pS[mt]
                        for i in range(32):
                            c0 = 16 * CW * cc + i
                            k.op("pe", lambda: PE.matmul(ps[:, 0:CW], lhsT=w1b[:, i, mt * 128:(mt + 1) * 128],
                                                         rhs=src[:, c0:c0 + 16 * (CW - 1) + 1:16], start=(i == 0), stop=(i == 31)),
                                 reads=[w1b, src], writes=[ps] if i == 0 else [], parts=[ps] if i else [])
                        k.op("dve", lambda: V.tensor_scalar(out=xh[:, 0:CW], in0=ps[:, 0:CW], scalar1=pbias[:, mt:mt + 1],
                                                            scalar2=None, op0=ALU.add), reads=[ps, pbias], writes=[xh])
                        k.op("dve", lambda: V.tensor_tensor(out=t3[:, 0:CW], in0=xh[:, 0:CW], in1=xh[:, 0:CW], op=ALU.mult),
                             reads=[xh], writes=[t3])
                        k.op("dve", lambda: V.tensor_tensor(out=t3[:, 0:CW], in0=t3[:, 0:CW], in1=xh[:, 0:CW], op=ALU.mult),
                             reads=[xh], writes=[t3])
                        k.op("dve", lambda: V.scalar_tensor_tensor(out=t3[:, 0:CW], in0=t3[:, 0:CW], scalar=0.044715,
                                                                   in1=xh[:, 0:CW], op0=ALU.mult, op1=ALU.add),
                             reads=[xh], writes=[t3])
                        k.op("act", lambda: S_.activation(out=t3[:, 0:CW], in_=t3[:, 0:CW], func=AF.Sigmoid,
                                                          scale=1.5957691216057308), reads=[t3], writes=[t3])
                        k.op("dve", lambda: V.tensor_tensor(out=hidb[:, mt, cc * CW:(cc + 1) * CW], in0=xh[:, 0:CW],
                                                            in1=t3[:, 0:CW], op=ALU.mult), reads=[xh, t3], parts=[hidb])
                if which == "k":
                    for cc in range(NCT * 128 // CW):
                        for mt in range(2):
                            k.op("pe", lambda: PE.matmul(pX[:, 0:CW], lhsT=w2b[:, mt, :], rhs=hidb[:, mt, cc * CW:(cc + 1) * CW],
                                                         start=(mt == 0), stop=(mt == 1)), reads=[w2b, hidb],
                                 writes=[pX] if mt == 0 else [], parts=[pX] if mt else [])
                        k.op("act", lambda: S_.copy(out=kccT[:, cc * CW:(cc + 1) * CW], in_=pX[:, 0:CW]), reads=[pX], parts=[kccT])
                else:
                    for ct in range(NCT):
                        for mt in range(2):
                            k.op("pe", lambda: PE.matmul(pX[:, 0:128], lhsT=hidb[:, mt, ct * 128:(ct + 1) * 128], rhs=w2b[:, mt, :],
                                                         start=(mt == 0), stop=(mt == 1)), reads=[w2b, hidb],
                                 writes=[pX] if mt == 0 else [], parts=[pX] if mt else [])
                        k.op("act", lambda: S_.copy(out=vcce[:, ct, 0:128], in_=pX[:, 0:128]), reads=[pX], parts=[vcce])
            k.release([src, w1s, w1b, w2s, w2b, posf, posb_, pbias, xh, t3, hidb])
            k.es = es2

        qT = [k.sb(f"a_qT{i}", [128, 4, W], BF16) for i in range(2)]
        glt = k.sb("a_gl", [128, 4, 6], F32)
        kwT = k.sb("a_kwT", [128, 1024], BF16)
        vwe = k.sb("a_vwe", [128, 8, 130], BF16)
        k.op("dve", lambda: V.memset(vwe[:, :, 128:129], 1.0), writes=[vwe])
        tmpS = [k.sb(f"a_tmpS{i}", [128, 512], F32) for i in range(3)]
        PT = [k.sb(f"a_PT{i}", [128, 512], BF16) for i in range(3)]
        imp = k.sb("a_imp", [128, NB], F32)
        imp2 = k.sb("a_imp2", [128, NB], F32)
        imp3 = k.sb("a_imp3", [128, NB], F32)
        m8 = k.sb("a_m8", [128, 16], F32)
        mneg = k.sb("a_mneg", [128, NB], BF16)
        maskT = k.sb("a_maskT", [NBR, NBH, W], BF16)
        maskTh2 = [k.sb(f"a_maskTh{i}", [NBR, NBH, W], BF16) for i in range(2)]
        den = k.sb("a_den", [128, 8], F32)
        oacc = [[k.sb(f"a_oacc{hh}_{i}", [128, 128], F32) for i in range(4)] for hh in range(2)]
        rot = [0]

        def score_tile(ps, lhs_k, rhs_q, ncols, extra, h, btab, bcol):
            ops = [(lhs_k, rhs_q)] + extra
            for oi, (l_, r_) in enumerate(ops):
                k.op("pe", lambda: PE.matmul(ps[:, 0:ncols], lhsT=l_[1], rhs=r_[1], start=(oi == 0), stop=(oi == len(ops) - 1)),
                     reads=[l_[0], r_[0]], writes=[ps] if oi == 0 else [], parts=[ps] if oi else [])
            pt_ = PT[rot[0] % 3]
            rot[0] += 1
            k.op("act", lambda: S_.activation(out=pt_[:, 0:ncols], in_=ps[:, 0:ncols], func=AF.Exp, scale=SCALE,
                                              bias=btab[:, h, bcol:bcol + 1]), reads=[ps, btab], writes=[pt_])
            return pt_

        def finish_branch(po, ncol, hh, i, b, first):
            dcol = den[:, 0:1]
            k.op("dve", lambda: V.tensor_scalar(out=den[:, 0:1], in0=po[:, ncol:ncol + 1], scalar1=TINY, scalar2=None,
                                                op0=ALU.max), reads=[po], writes=[den])
            k.op("dve", lambda: V.reciprocal(out=den[:, 1:2], in_=den[:, 0:1]), reads=[den], writes=[den])
            k.op("dve", lambda: V.tensor_tensor(out=den[:, 2:3], in0=den[:, 1:2], in1=glt[:, i, b * 2 + hh:b * 2 + hh + 1],
                                                op=ALU.mult), reads=[den, glt], writes=[den])
            oa = oacc[hh][i]
            if first:
                k.op("dve", lambda: V.tensor_scalar(out=oa[:], in0=po[:, 0:128], scalar1=den[:, 2:3], scalar2=None, op0=ALU.mult),
                     reads=[po, den], writes=[oa])
            else:
                k.op("dve", lambda: V.scalar_tensor_tensor(out=oa[:], in0=po[:, 0:128], scalar=den[:, 2:3], in1=oa[:],
                                                           op0=ALU.mult, op1=ALU.add), reads=[po, den], writes=[oa])

        for qc in range(L // W):
            t0 = qc * W
            q_ = qT[qc % 2]
            k.dma("sp", q_[:], io["qT"][:, t0:t0 + W].rearrange("(h p) n -> p h n", p=128), dst=q_)
            k.dma("sp", glt[:], io["gl"][t0:t0 + W, :].rearrange("(i p) c -> p i c", p=128), dst=glt)
            k.op("act", lambda: S_.activation(out=glt[:], in_=glt[:], func=AF.Sigmoid), writes=[glt])
            klo = max(0, t0 - 512)
            khi = min(L, t0 + 512)
            k.dma("sp", kwT[:, klo - (t0 - 512):khi - (t0 - 512)], io["kwT"][:, klo:khi], dst=kwT)
            k.dma("sp", vwe[:, (klo - (t0 - 512)) // 128:(khi - (t0 - 512)) // 128, 0:128],
                  io["vw"][klo:khi, :].rearrange("(t p) d -> p t d", p=128), dst=vwe, part=True)
            for i in range(4):
                t = t0 + 128 * i
                ctmax = min(NCT - 1, (t + 127) // 2048)
                for h in range(4):
                    own = h < 2
                    pU, pOc = (pO[0], pO[1]) if h % 2 == 0 else (pO[2], pO[3])
                    def c_front(ct):
                        dlt = t - 2048 * ct
                        extra = [((ones_b, ones_b[0:1, :]), (shrow, shrow[0:1, h, 0:128]))]
                        if dlt <= 2048:
                            extra.append(((identb, identb[:]), (Mc, Mc[:, dlt:dlt + 128])))
                        return score_tile(pS[rot[0] % 2], (kccT, kccT[:, ct * 128:(ct + 1) * 128]),
                                          (q_, q_[:, h, 128 * i:128 * i + 128]), 128, extra, h, Btab3, dlt // 128 + 3)

                    def c_back(ct, pt_):
                        k.op("pe", lambda: PE.matmul(pU[:, 0:NB + 1], lhsT=pt_[:, 0:128], rhs=Aext[:, ct, :], start=(ct == 0),
                                                     stop=(ct == ctmax)), reads=[pt_, Aext],
                             writes=[pU] if ct == 0 else [], parts=[pU] if ct else [])
                        if own:
                            k.op("pe", lambda: PE.matmul(pOc[:, 0:129], lhsT=pt_[:, 0:128], rhs=vcce[:, ct, 0:129], start=(ct == 0),
                                                         stop=(ct == ctmax)), reads=[pt_, vcce],
                                 writes=[pOc] if ct == 0 else [], parts=[pOc] if ct else [])
                    prev = None
                    for ct in range(ctmax + 1):
                        cur = (ct, c_front(ct))
                        if prev is not None:
                            c_back(*prev)
                        prev = cur
                    c_back(*prev)
                    k.op("dve", lambda: V.tensor_scalar(out=den[:, 4:5], in0=pU[:, NB:NB + 1], scalar1=TINY, scalar2=None,
                                                        op0=ALU.max), reads=[pU], writes=[den])
                    k.op("dve", lambda: V.reciprocal(out=den[:, 5:6], in_=den[:, 4:5]), reads=[den], writes=[den])
                    if h == 0:
                        k.op("dve", lambda: V.tensor_scalar(out=imp[:], in0=pU[:, 0:NB], scalar1=den[:, 5:6], scalar2=None,
                                                            op0=ALU.mult), reads=[pU, den], writes=[imp])
                    else:
                        k.op("dve", lambda: V.scalar_tensor_tensor(out=imp[:], in0=pU[:, 0:NB], scalar=den[:, 5:6], in1=imp[:],
                                                                   op0=ALU.mult, op1=ALU.add), reads=[pU, den], writes=[imp])
                    if own:
                        finish_branch(pOc, 128, h, i, 0, True)
                qt = t // 128
                x0 = NB - 2 * qt
                k.op("dve", lambda: V.tensor_tensor(out=imp2[:], in0=imp[:], in1=cst["cand"][:, x0:x0 + NB], op=ALU.mult),
                     reads=[imp, cst["cand"]], writes=[imp2])
                k.op("dve", lambda: V.tensor_tensor(out=imp2[:], in0=imp2[:], in1=cst["negm"][:, x0:x0 + NB], op=ALU.add),
                     reads=[cst["negm"]], writes=[imp2])
                k.op("dve", lambda: V.tensor_tensor(out=imp2[:], in0=imp2[:], in1=cst["force"][:, x0:x0 + NB], op=ALU.max),
                     reads=[cst["force"]], writes=[imp2])
                k.op("dve", lambda: V.memset(imp2[:, 0:1], 1e4), writes=[imp2])
                k.op("dve", lambda: V.max(out=m8[:, 0:8], in_=imp2[:]), reads=[imp2], writes=[m8])
                k.op("dve", lambda: V.match_replace(out=imp3[:], in_to_replace=m8[:, 0:8], in_values=imp2[:], imm_value=-3.0),
                     reads=[imp2, m8], writes=[imp3])
                k.op("dve", lambda: V.max(out=m8[:, 8:16], in_=imp3[:]), reads=[imp3], writes=[m8])
                k.op("dve", lambda: V.scalar_tensor_tensor(out=imp3[:], in0=imp2[:], scalar=m8[:, 15:16],
                                                           in1=cst["cand"][:, x0:x0 + NB], op0=ALU.is_ge, op1=ALU.mult),
                     reads=[imp2, m8, cst["cand"]], writes=[imp3])
                k.op("dve", lambda: V.tensor_scalar(out=mneg[:], in0=imp3[:], scalar1=-NEG, scalar2=NEG, op0=ALU.mult,
                                                    op1=ALU.add), reads=[imp3], writes=[mneg])
                for hf in range(NBH):
                    k.op("pe", lambda: PE.transpose(pT[0:NBR, hf * 128:(hf + 1) * 128], mneg[:, hf * NBR:(hf + 1) * NBR], identb[:]),
                         reads=[mneg, identb], writes=[pT] if hf == 0 else [], parts=[pT] if hf else [])
                for hf in range(NBH):
                    k.op("act", lambda: S_.copy(out=maskT[:, hf, 128 * i:128 * i + 128], in_=pT[0:NBR, hf * 128:(hf + 1) * 128]),
                         reads=[pT], writes=[maskT] if (i == 0 and hf == 0) else [], parts=[] if (i == 0 and hf == 0) else [maskT])
            for hh in range(2):
                maskTh = maskTh2[hh]
                for hf in range(NBH):
                    k.op("dve", lambda: V.tensor_tensor(out=maskTh[:, hf, :], in0=maskT[:, hf, :], in1=shrow[0:NBR, hh, :], op=ALU.add),
                         reads=[maskT, shrow], writes=[maskTh] if hf == 0 else [], parts=[maskTh] if hf else [])
                kts = list(range(0, (t0 + W) // 128))
                def s_front(kt):
                    k0 = kt * 128
                    extra = [((Esel, Esel[:, kt % 64, :]), (maskTh, maskTh[:, kt // 64, :]))]
                    if k0 >= t0:
                        xo = t0 - k0 + 384
                        extra.append(((identb, identb[:]), (Mcaus, Mcaus[:, xo:xo + W])))
                    return score_tile(pS[rot[0] % 2], (ksT, ksT[:, k0:k0 + 128]), (q_, q_[:, hh, :]), W, extra, hh, Btab,
                                      (t0 - k0) // 128 + 3)

                def s_back(ki, kt, pt_):
                    for i in range(4):
                        k.op("pe", lambda: PE.matmul(pO[i][:, 0:129], lhsT=pt_[:, 128 * i:128 * i + 128], rhs=vse[:, kt, 0:129],
                                                     start=(ki == 0), stop=(ki == len(kts) - 1)), reads=[pt_, vse],
                             writes=[pO[i]] if ki == 0 else [], parts=[pO[i]] if ki else [])
                prev = None
                for ki, kt in enumerate(kts):
                    cur = (ki, kt, s_front(kt))
                    if prev is not None:
                        s_back(*prev)
                    prev = cur
                s_back(*prev)
                for i in range(4):
                    finish_branch(pO[i], 128, hh, i, 1, False)
            for hh in range(2):
                kts = [m for m in range(8) if t0 - 512 + 128 * m >= 0 and t0 - 512 + 128 * m < L]
                def w_front(m):
                    k0 = t0 - 512 + 128 * m
                    xo = t0 - k0 + 384
                    extra = [((identb, identb[:]), (Mw, Mw[:, xo:xo + W])), ((ones_b, ones_b[0:1, :]), (shrow, shrow[0:1, hh, :]))]
                    return score_tile(pS[rot[0] % 2], (kwT, kwT[:, 128 * m:128 * m + 128]), (q_, q_[:, hh, :]), W, extra, hh, Btab,
                                      (t0 - k0) // 128 + 3)

                def w_back(ki, m, pt_):
                    for i in range(4):
                        k.op("pe", lambda: PE.matmul(pO[i][:, 0:129], lhsT=pt_[:, 128 * i:128 * i + 128], rhs=vwe[:, m, 0:129],
                                                     start=(ki == 0), stop=(ki == len(kts) - 1)), reads=[pt_, vwe],
                             writes=[pO[i]] if ki == 0 else [], parts=[pO[i]] if ki else [])
                prev = None
                for ki, m in enumerate(kts):
                    cur = (ki, m, w_front(m))
                    if prev is not None:
                        w_back(*prev)
                    prev = cur
                w_back(*prev)
                for i in range(4):
                    finish_branch(pO[i], 128, hh, i, 2, False)
            for hh in range(2):
                for i in range(4):
                    oa = oacc[hh][i]
                    k.dma("sp", io["ya"][t0 + 128 * i:t0 + 128 * i + 128, 128 * hh:128 * hh + 128], oa[:], src=oa)
        k.finish([oacc[hh][i] for hh in range(2) for i in range(4)])
        k.es = es_old


ATTN_IN = lambda L: {"qT": ([512, L], BF16), "kcT": ([128, L], BF16), "vcT": ([128, L], BF16), "ksT": ([128, L], BF16),
                     "kwT": ([128, L], BF16), "vs": ([L, 128], BF16), "vw": ([L, 128], BF16), "gl": ([L, 6], F32),
                     "posT_k": ([128, 32], F32), "w1_k": ([4096, 256], F32), "w2_k": ([256, 128], F32),
                     "posT_v": ([128, 32], F32), "w1_v": ([4096, 256], F32), "w2_v": ([256, 128], F32), "hp": ([128, 8], F32)}


def build_attn(L):
    nc = bass.Bass("TRN2", target_bir_lowering=False)
    io = {}
    cs = attn_consts(L)
    for n, (s, d) in ATTN_IN(L).items():
        io[n] = nc.dram_tensor(n, s, d, kind="ExternalInput").ap()
    for n, a in cs.items():
        d = BF16 if a.dtype == ml_dtypes.bfloat16 else F32
        io[n] = nc.dram_tensor(n, list(a.shape), d, kind="ExternalInput").ap()
    io["ya"] = nc.dram_tensor("ya", [L, 256], F32, kind="ExternalOutput").ap()
    with ExitStack() as es:
        k = KB(nc, es)
        attn_body(nc, k, L, io)
        print("attn kernel instructions:", k.n_ins)
    return nc


def proj_pack_w(w, blocks):
    D = w.shape[0]
    parts = [np.ascontiguousarray(w[:, c0:c0 + ncol].reshape(D // 128, 128, ncol).transpose(1, 0, 2)).reshape(-1)
             for (c0, ncol, _k, _n, _o) in blocks]
    return np.concatenate(parts) if parts else np.zeros(1, np.float32)


DFF = 5632
_bf = lambda a: np.ascontiguousarray(a).astype(ml_dtypes.bfloat16)
_col = lambda v: np.ascontiguousarray(np.asarray(v, np.float32).reshape(-1, 128).T)
_PROG = {}


def _run(nc, in_maps):
    res = run_bass_kernel_spmd(nc, in_maps, core_ids=list(range(NCORES)))
    return res.results


def kernel(**inp):
    L, NT = SEQ, SEQ // NCORES
    f32 = lambda a: np.ascontiguousarray(np.asarray(a, np.float32))
    x = f32(inp["x"])[0]
    identb = np.eye(128).astype(ml_dtypes.bfloat16)
    blocks, nfm = p1_blocks()
    spec = {"fm": ((nfm, NT), BF16), "dtT": ((32, NT), F32), "z": ((NT, 2048), F32), "vtm": ((NT, 1024), BF16),
            "gl": ((NT, 48), F32)}
    nc1 = build_proj(D_MODEL, NT, C_END, blocks, spec)
    w_in = proj_pack_w(f32(inp["w_in"])[0], blocks)
    nw1 = _col(inp["mix_norm_w"][0])
    r1 = _run(nc1, [{"x": x[c * NT:(c + 1) * NT], "nw": nw1, "w": w_in, "identb": identb} for c in range(NCORES)])
    FM = np.concatenate([np.asarray(r["fm"]) for r in r1], axis=1)
    DT = np.concatenate([np.asarray(r["dtT"]) for r in r1], axis=1)
    Z = np.concatenate([np.asarray(r["z"]) for r in r1], axis=0)
    VTM = np.concatenate([np.asarray(r["vtm"]) for r in r1], axis=0)
    GL = np.concatenate([np.asarray(r["gl"]) for r in r1], axis=0)
    del r1
    RQ, RKC, RVC, RKS, RKW = 3072, 5120, 5632, 6144, 6656
    nc2 = build_ssd(L)
    sc = ssd_consts()
    convw = f32(inp["ssm_conv_w"])[0]
    convb = f32(inp["ssm_conv_b"])[0]
    maps = []
    for c in range(NCORES):
        g = c // 2
        chs = np.concatenate([np.arange(256 * c, 256 * c + 256), np.arange(2048 + 128 * g, 2048 + 128 * g + 128),
                              np.arange(2560 + 128 * g, 2560 + 128 * g + 128)])
        xT = np.zeros((512, 3 + L), ml_dtypes.bfloat16)
        xT[:, 3:] = FM[chs]
        cw = convw[:, chs]
        cb = convb[chs]
        m = dict(sc)
        m.update({"xbcT": xT, "cw": np.ascontiguousarray(cw.T.reshape(4, 128, 4).transpose(1, 0, 2)),
                  "cb": np.ascontiguousarray(cb.reshape(4, 128).T), "cbrow": np.ascontiguousarray(cb[None, :]),
                  "dtT": np.ascontiguousarray(DT[4 * c:4 * c + 4]), "dtb": f32(inp["ssm_dt_bias"])[0, 4 * c:4 * c + 4, None],
                  "alog": f32(inp["ssm_a_log"])[0, 4 * c:4 * c + 4, None],
                  "drow": np.ascontiguousarray(np.tile(np.repeat(f32(inp["ssm_d"])[0, 4 * c:4 * c + 4], 64)[None, :], (128, 1))),
                  "z": np.ascontiguousarray(Z[:, 256 * c:256 * c + 256])})
        maps.append(m)
    r2 = _run(nc2, maps)
    YS = np.concatenate([np.asarray(r["ys"]) for r in r2], axis=1)
    del r2, maps
    nc3 = build_attn(L)
    ac = attn_consts(L)
    maps = []
    for c in range(NCORES):
        g = c // 2
        own = [2 * c, 2 * c + 1]
        heads = own + [h for h in range(4 * g, 4 * g + 4) if h not in own]
        qrows = np.concatenate([np.arange(RQ + 128 * h, RQ + 128 * h + 128) for h in heads])
        glc = np.array([b * 16 + h for b in range(3) for h in own])
        m = dict(ac)
        m.update({"qT": np.ascontiguousarray(FM[qrows]), "kcT": np.ascontiguousarray(FM[RKC + 128 * g:RKC + 128 * g + 128]),
                  "vcT": np.ascontiguousarray(FM[RVC + 128 * g:RVC + 128 * g + 128]),
                  "ksT": np.ascontiguousarray(FM[RKS + 128 * g:RKS + 128 * g + 128]),
                  "kwT": np.ascontiguousarray(FM[RKW + 128 * g:RKW + 128 * g + 128]),
                  "vs": np.ascontiguousarray(VTM[:, 128 * g:128 * g + 128]),
                  "vw": np.ascontiguousarray(VTM[:, 512 + 128 * g:512 + 128 * g + 128]),
                  "gl": np.ascontiguousarray(GL[:, glc]),
                  "posT_k": np.ascontiguousarray(f32(inp["cmp_pos_k"])[0].T), "w1_k": f32(inp["cmp_w1_k"])[0],
                  "w2_k": f32(inp["cmp_w2_k"])[0],
                  "posT_v": np.ascontiguousarray(f32(inp["cmp_pos_v"])[0].T), "w1_v": f32(inp["cmp_w1_v"])[0],
                  "w2_v": f32(inp["cmp_w2_v"])[0], "hp": attn_head_params(heads)})
        maps.append(m)
    r3 = _run(nc3, maps)
    YA = np.concatenate([np.asarray(r["ya"]) for r in r3], axis=1)
    del r3, maps
    nc4 = build_tail(D_MODEL, NT, DFF, GT=512)
    fcw = f32(inp["ffn_conv_w"])[0]
    common = {"nwa": _col(inp["attn_norm_w"][0]), "nws": _col(inp["ssm_norm_w"][0]), "nwf": _col(inp["ffn_norm_w"][0]),
              "fwb": np.ascontiguousarray(np.tile(f32(inp["final_norm_w"])[None, :], (128, 1))),
              "w_out": np.ascontiguousarray(f32(inp["w_out"])[0].reshape(32, 128, 8, 256).transpose(2, 1, 0, 3)),
              "w_up": np.ascontiguousarray(f32(inp["w_up"])[0].reshape(16, 128, 2, DFF // 128, 128).transpose(3, 1, 0, 2, 4)
                                           .reshape(DFF // 128, 128, 16, 256)),
              "w_down": np.ascontiguousarray(f32(inp["w_down"])[0].reshape(DFF // 128, 128, 8, 256).transpose(2, 1, 0, 3)),
              "cw": np.ascontiguousarray(fcw.T.reshape(2 * DFF // 128, 128, 3).transpose(1, 0, 2)),
              "cb": _col(inp["ffn_conv_b"][0]), "identb": identb}

    def halo(a, c):
        o = np.zeros((NT + 128, a.shape[1]), np.float32)
        lo = c * NT - 128
        if lo < 0:
            o[128:] = a[0:NT]
        else:
            o[:] = a[lo:lo + NT + 128]
        return o
    maps = []
    for c in range(NCORES):
        m = dict(common)
        m.update({"x": halo(x, c), "ya": halo(YA, c), "ys": halo(YS, c)})
        maps.append(m)
    r4 = _run(nc4, maps)
    out = np.concatenate([np.asarray(r["out"]) for r in r4], axis=0)
    return out[None].astype(np.float32)
```

```python
import numpy as np
from contextlib import ExitStack
import ml_dtypes
import concourse.bass as bass
import concourse.mybir as mybir
from concourse.bass_utils import run_bass_kernel_spmd

F32 = mybir.dt.float32
BF16 = mybir.dt.bfloat16
AF = mybir.ActivationFunctionType
ALU = mybir.AluOpType
AX = mybir.AxisListType

SAME_ENGINE_SYNC = True


class Tl:
    def __init__(self, k, name, shape, dt, space="sbuf"):
        self.k = k
        self.name = name
        if space == "sbuf":
            self.t = k.es.enter_context(k.nc.sbuf_tensor(name, list(shape), dt))
        else:
            self.t = k.es.enter_context(k.nc.psum_tensor(name, list(shape), dt))
        self.lw = {}
        self.rd = {}
        self.prd = {}
        self.dcnts = {}

    def __getitem__(self, idx):
        return self.t[idx]

    def dsem(self, q):
        kind = "sw" if q == "pool" else "hw"
        key = ("d", self.name, kind)
        if key not in self.k.sems:
            self.k.sems[key] = self.k.es.enter_context(self.k.nc.semaphore("d%s_%s" % (kind, self.name)))
            self.dcnts[key] = 0
        return key


class Alias:
    def __init__(self, base, dtype):
        self.base = base
        self.t = base.t.bitcast(dtype)
        self.name = base.name

    def __getitem__(self, idx):
        return self.t[idx]

    lw = property(lambda s: s.base.lw, lambda s, v: setattr(s.base, "lw", v))
    rd = property(lambda s: s.base.rd, lambda s, v: setattr(s.base, "rd", v))
    prd = property(lambda s: s.base.prd, lambda s, v: setattr(s.base, "prd", v))


class Dr:
    def __init__(self):
        self.lw = {}
        self.rd = {}


class KB:
    def __init__(self, nc, es):
        self.nc = nc
        self.es = es
        self.E = {"pe": nc.tensor, "act": nc.scalar, "dve": nc.vector, "pool": nc.gpsimd, "sp": nc.sync}
        self.sems = {}
        self.cnt = {}
        for e in ("pe", "act", "dve", "pool"):
            self.sems[e] = es.enter_context(nc.semaphore("s_" + e))
            self.cnt[e] = 0
        self.waited = {e: {} for e in self.E}
        self.n_ins = 0

    def sb(self, name, shape, dt):
        return Tl(self, name, shape, dt, "sbuf")

    def ps(self, name, shape, dt=F32):
        return Tl(self, name, shape, dt, "psum")

    def _wait(self, eng, deps):
        for key, v in deps.items():
            if key == eng and (eng == "pe" or not SAME_ENGINE_SYNC):
                continue
            if self.waited[eng].get(key, 0) < v:
                self.E[eng].wait_ge(self.sems[key], v)
                self.waited[eng][key] = v
                self.n_ins += 1

    @staticmethod
    def _addd(deps, d):
        for key, v in d.items():
            if deps.get(key, 0) < v:
                deps[key] = v

    def op(self, eng, fn, reads=(), writes=(), parts=()):
        deps = {}
        for r in reads:
            self._addd(deps, r.lw)
        for w in writes:
            self._addd(deps, w.lw)
            self._addd(deps, w.rd)
        for w in parts:
            self._addd(deps, w.rd)
            self._addd(deps, getattr(w, "prd", {}))
        self._wait(eng, deps)
        ins = fn()
        self.cnt[eng] += 1
        c = self.cnt[eng]
        ins.then_inc(self.sems[eng], 1)
        self.n_ins += 1
        for r in reads:
            if r.rd.get(eng, 0) < c:
                r.rd[eng] = c
        for w in writes:
            w.lw = {eng: c}
            w.prd = w.rd
            w.rd = {}
        for w in parts:
            w.lw[eng] = c
        return ins

    def dma(self, q, out, in_, dst=None, src=None, part=False, after=(), marks=(), **kw):
        deps = {}
        for d in after:
            self._addd(deps, d.lw)
        if src is not None:
            self._addd(deps, src.lw)
        if dst is not None:
            if not part:
                self._addd(deps, dst.lw)
            else:
                self._addd(deps, dst.prd)
            self._addd(deps, dst.rd)
        self._wait(q, deps)
        ins = self.E[q].dma_start(out=out, in_=in_, **kw)
        self.n_ins += 1
        assert not (dst is not None and src is not None)
        if dst is not None:
            key = dst.dsem(q)
            dst.dcnts[key] += 16
            ins.then_inc(self.sems[key], 16)
            if part:
                dst.lw[key] = dst.dcnts[key]
            else:
                dst.lw = {key: dst.dcnts[key]}
                dst.prd = dst.rd
                dst.rd = {}
        elif src is not None:
            key = src.dsem(q)
            src.dcnts[key] += 16
            ins.then_inc(self.sems[key], 16)
            src.rd[key] = src.dcnts[key]
            for d in marks:
                d.lw[key] = src.dcnts[key]
        return ins

    def release(self, tiles):
        deps = {}
        for t in tiles:
            self._addd(deps, t.lw)
            self._addd(deps, t.rd)
            self._addd(deps, t.prd)
        for e in ("pe", "act", "dve", "pool", "sp"):
            self._wait(e, deps)

    def finish(self, tiles):
        deps = {}
        for t in tiles:
            self._addd(deps, t.lw)
            self._addd(deps, t.rd)
        self._wait("sp", deps)


def rms_tile_to_hT(k, xt, nw_col, ident_b, hT, col0, D, eps, ps_tr, ss, xn):
    nc = k.nc
    KC = D // 128
    k.op("act", lambda: nc.scalar.activation(out=xn[:], in_=xt[:, 0:D], func=AF.Square, accum_out=ss[:, 0:1]),
         reads=[xt], writes=[xn, ss])
    k.op("dve", lambda: nc.vector.tensor_scalar(out=ss[:, 1:2], in0=ss[:, 0:1], scalar1=1.0 / D, scalar2=eps,
                                                op0=ALU.mult, op1=ALU.add), reads=[ss], writes=[ss])
    k.op("act", lambda: nc.scalar.activation(out=ss[:, 2:3], in_=ss[:, 1:2], func=AF.Sqrt), reads=[ss], writes=[ss])
    k.op("dve", lambda: nc.vector.reciprocal(out=ss[:, 3:4], in_=ss[:, 2:3]), reads=[ss], writes=[ss])
    k.op("dve", lambda: nc.vector.tensor_scalar(out=xn[:], in0=xt[:, 0:D], scalar1=ss[:, 3:4], scalar2=None,
                                                op0=ALU.mult), reads=[xt, ss], writes=[xn])
    for g in range(KC // 4):
        pt = ps_tr[g % 2]
        for j in range(4):
            kc = g * 4 + j
            k.op("pe", lambda: nc.tensor.transpose(pt[:, j * 128:(j + 1) * 128], xn[:, kc * 128:(kc + 1) * 128],
                                                   ident_b[:]), reads=[xn, ident_b],
                 parts=[pt] if j else [], writes=[] if j else [pt])
        for j in range(4):
            kc = g * 4 + j
            if g % 2 == 0:
                k.op("dve", lambda: nc.vector.tensor_scalar(
                    out=hT[:, kc, col0:col0 + 128], in0=pt[:, j * 128:(j + 1) * 128],
                    scalar1=nw_col[:, kc:kc + 1], scalar2=None, op0=ALU.mult), reads=[pt, nw_col], parts=[hT])
            else:
                k.op("act", lambda: nc.scalar.activation(
                    out=hT[:, kc, col0:col0 + 128], in_=pt[:, j * 128:(j + 1) * 128],
                    func=AF.Copy, scale=nw_col[:, kc:kc + 1]), reads=[pt, nw_col], parts=[hT])


def rmsnorm_to_hT(k, x_ap, nw_col, ident_b, hT, n_tok_tiles, D, eps, ps_tr, xt_tiles, tok0=0):
    for tt in range(n_tok_tiles):
        xt = xt_tiles["x"][tt % 2]
        k.dma("sp", xt[:], x_ap[tt * 128:(tt + 1) * 128, :], dst=xt)
        rms_tile_to_hT(k, xt, nw_col, ident_b, hT, tok0 + tt * 128, D, eps, ps_tr,
                       xt_tiles["ss"][tt % 2], xt_tiles["xn"][tt % 2])


D_MODEL = 2048
SEQ = 16384
NCORES = 8
EPS = 1e-6
C_Z, C_XBC, C_DT, C_Q, C_KC, C_VC, C_KS, C_VS, C_KW, C_VW, C_GL, C_END = (
    0, 2048, 5120, 5152, 7200, 7712, 8224, 8736, 9248, 9760, 10272, 10320)


def p1_blocks():
    blocks = []
    r = 0
    for c0, c1 in ((C_XBC, C_DT), (C_Q, C_KC), (C_KC, C_VC), (C_VC, C_KS), (C_KS, C_VS), (C_KW, C_VW)):
        for c in range(c0, c1, 128):
            blocks.append((c, 128, "f", "fm", r))
            r += 128
    nfm = r
    blocks.append((C_DT, 32, "f", "dtT", 0))
    for j in range(4):
        blocks.append((C_Z + 512 * j, 512, "t", "z", 512 * j))
    blocks.append((C_VS, 512, "t", "vtm", 0))
    blocks.append((C_VW, 512, "t", "vtm", 512))
    blocks.append((C_GL, 48, "t", "gl", 0))
    return blocks, nfm


def build_proj(D, NT, NCOLS, blocks, outs_spec):
    nc = bass.Bass("TRN2", target_bir_lowering=False)
    KC = D // 128
    x = nc.dram_tensor("x", [NT, D], F32, kind="ExternalInput").ap()
    nw = nc.dram_tensor("nw", [128, KC], F32, kind="ExternalInput").ap()
    wtot = sum(128 * KC * b[1] for b in blocks)
    w = nc.dram_tensor("w", [max(wtot, 1)], F32, kind="ExternalInput").ap()
    idb = nc.dram_tensor("identb", [128, 128], BF16, kind="ExternalInput").ap()
    outs = {n: nc.dram_tensor(n, list(s), d, kind="ExternalOutput").ap() for n, (s, d) in outs_spec.items()}
    woff = [0]
    with ExitStack() as es:
        k = KB(nc, es)
        ident_b = k.sb("ident_b", [128, 128], BF16)
        nw_col = k.sb("nw_col", [128, KC], F32)
        hT = k.sb("hT", [128, KC, NT], BF16)
        xt_tiles = {
            "x": [k.sb(f"xt{i}", [128, D], F32) for i in range(2)],
            "ss": [k.sb(f"ss{i}", [128, 4], F32) for i in range(2)],
            "xn": [k.sb(f"xn{i}", [128, D], BF16) for i in range(2)],
        }
        ps_tr = [k.ps(f"ps_tr{i}", [128, 512], BF16) for i in range(2)]
        ps_mm = [k.ps(f"ps_mm{i}", [128, 512], F32) for i in range(4)]
        wst = [k.sb(f"wst{i}", [128, KC, 512], F32) for i in range(2)]
        wbf = [k.sb(f"wbf{i}", [128, KC, 512], BF16) for i in range(2)]
        ob = [k.sb(f"ob{i}", [128, max(NT, 512)], F32) for i in range(2)]
        k.dma("sp", ident_b[:], idb[:, :], dst=ident_b)
        k.dma("sp", nw_col[:], nw[:, :], dst=nw_col)
        rmsnorm_to_hT(k, x, nw_col, ident_b, hT, NT // 128, D, EPS, ps_tr, xt_tiles)
        mmi = 0
        for bi, (c0, ncol, kind, oname, o0) in enumerate(blocks):
            ws, wb = wst[bi % 2], wbf[bi % 2]
            odt = outs_spec[oname][1]
            wblk = w[woff[0]:woff[0] + 128 * KC * ncol].rearrange("(p k c) -> p k c", p=128, k=KC)
            woff[0] += 128 * KC * ncol
            k.dma("pool" if bi % 2 else "sp", ws[:, :, 0:ncol], wblk, dst=ws)
            half = KC // 2
            k.op("dve", lambda: nc.vector.tensor_copy(out=wb[:, 0:half, 0:ncol], in_=ws[:, 0:half, 0:ncol]),
                 reads=[ws], writes=[wb])
            k.op("pool", lambda: nc.gpsimd.tensor_copy(out=wb[:, half:KC, 0:ncol], in_=ws[:, half:KC, 0:ncol]),
                 reads=[ws], parts=[wb])
            o = ob[bi % 2]
            if kind == "f":
                ov = o.t.bitcast(odt) if odt != F32 else o.t
                for ch in range(NT // 512):
                    pm = ps_mm[mmi % 4]
                    mmi += 1
                    for kc in range(KC):
                        k.op("pe", lambda: nc.tensor.matmul(pm[0:ncol, :], lhsT=wb[:, kc, 0:ncol],
                                                            rhs=hT[:, kc, ch * 512:(ch + 1) * 512],
                                                            start=(kc == 0), stop=(kc == KC - 1)),
                             reads=[wb, hT], writes=[pm] if kc == 0 else [], parts=[pm] if kc else [])
                    if ch % 2 == 0:
                        k.op("act", lambda: nc.scalar.copy(out=ov[0:ncol, ch * 512:(ch + 1) * 512], in_=pm[0:ncol, :]),
                             reads=[pm], writes=[o] if ch == 0 else [], parts=[o] if ch else [])
                    else:
                        k.op("dve", lambda: nc.vector.tensor_copy(out=ov[0:ncol, ch * 512:(ch + 1) * 512], in_=pm[0:ncol, :]),
                             reads=[pm], parts=[o])
                k.dma("pool", outs[oname][o0:o0 + ncol, :], ov[0:ncol, 0:NT], src=o)
            else:
                ov = o.t.bitcast(odt) if odt != F32 else o.t
                for tt in range(NT // 128):
                    pm = ps_mm[mmi % 4]
                    mmi += 1
                    for kc in range(KC):
                        k.op("pe", lambda: nc.tensor.matmul(pm[:, 0:ncol], lhsT=hT[:, kc, tt * 128:(tt + 1) * 128],
                                                            rhs=wb[:, kc, 0:ncol],
                                                            start=(kc == 0), stop=(kc == KC - 1)),
                             reads=[wb, hT], writes=[pm] if kc == 0 else [], parts=[pm] if kc else [])
                    o = ob[(bi + tt) % 2]
                    ov = o.t.bitcast(odt) if odt != F32 else o.t
                    if tt % 2 == 0:
                        k.op("act", lambda: nc.scalar.copy(out=ov[:, 0:ncol], in_=pm[:, 0:ncol]), reads=[pm], writes=[o])
                    else:
                        k.op("dve", lambda: nc.vector.tensor_copy(out=ov[:, 0:ncol], in_=pm[:, 0:ncol]), reads=[pm], writes=[o])
                    k.dma("pool", outs[oname][tt * 128:(tt + 1) * 128, o0:o0 + ncol], ov[:, 0:ncol], src=o)
        k.finish(ob)
        print("proj kernel instructions:", k.n_ins)
    return nc


def build_tail(D, NT, DFF, GT=512):
    nc = bass.Bass("TRN2", target_bir_lowering=False)
    KC = D // 128
    NTH = NT + 128
    NJ = DFF // 128
    x = nc.dram_tensor("x", [NTH, D], F32, kind="ExternalInput").ap()
    ya = nc.dram_tensor("ya", [NTH, D], F32, kind="ExternalInput").ap()
    ys = nc.dram_tensor("ys", [NTH, D], F32, kind="ExternalInput").ap()
    nws = {n: nc.dram_tensor(n, [128, KC], F32, kind="ExternalInput").ap() for n in ("nwa", "nws", "nwf")}
    fwb = nc.dram_tensor("fwb", [128, D], F32, kind="ExternalInput").ap()
    w_out = nc.dram_tensor("w_out", [D // 256, 128, 2 * KC, 256], F32, kind="ExternalInput").ap()
    w_up = nc.dram_tensor("w_up", [NJ, 128, KC, 256], F32, kind="ExternalInput").ap()
    w_down = nc.dram_tensor("w_down", [D // 256, 128, NJ, 256], F32, kind="ExternalInput").ap()
    cw = nc.dram_tensor("cw", [128, 2 * NJ, 3], F32, kind="ExternalInput").ap()
    cb = nc.dram_tensor("cb", [128, 2 * NJ], F32, kind="ExternalInput").ap()
    idb = nc.dram_tensor("identb", [128, 128], BF16, kind="ExternalInput").ap()
    out = nc.dram_tensor("out", [NT, D], F32, kind="ExternalOutput").ap()
    x1d = nc.dram_tensor("x1", [NTH, D], F32, kind="ExternalOutput").ap()
    h2d = nc.dram_tensor("h2T", [128, KC, NTH], BF16, kind="ExternalOutput").ap()
    with ExitStack() as es:
        k = KB(nc, es)
        ident_b = k.sb("ident_b", [128, 128], BF16)
        nwc = {n: k.sb(n + "_c", [128, KC], F32) for n in nws}
        cw_t = k.sb("cw_t", [128, 2 * NJ, 3], F32)
        cb_t = k.sb("cb_t", [128, 2 * NJ], F32)
        fw_t = k.sb("fw_t", [128, D], F32)
        big = [k.sb(f"big{i}", [128, D], F32) for i in range(4)]
        xn = [k.sb(f"xn{i}", [128, D], BF16) for i in range(2)]
        ss = [k.sb(f"ss{i}", [128, 4], F32) for i in range(3)]
        ps_tr = [k.ps(f"ps_tr{i}", [128, 512], BF16) for i in range(2)]
        ps_mm = [k.ps(f"ps_mm{i}", [128, 512], F32) for i in range(4)]
        ps_cv = [k.ps(f"ps_cv{i}", [128, 512], F32) for i in range(2)]
        ps_h2 = [Alias(t_, F32) for t_ in ps_tr]
        wst = [k.sb(f"wst{i}", [128, 16, 256], F32) for i in range(2)]
        wbf = [k.sb(f"wbf{i}", [128, 16, 256], BF16) for i in range(2)]
        k.dma("sp", ident_b[:], idb[:, :], dst=ident_b)
        for n in nws:
            k.dma("sp", nwc[n][:], nws[n][:, :], dst=nwc[n])
        k.dma("sp", cw_t[:], cw[:, :, :], dst=cw_t)
        k.dma("sp", cb_t[:], cb[:, :], dst=cb_t)
        k.dma("sp", fw_t[:], fwb[:, :], dst=fw_t)
        x1_reg = [Dr() for _ in range(NTH // 128)]
        h2_reg = [Dr() for _ in range(NTH // 128)]
        wload = [0]

        def load_w(view_ap, nk, ncol):
            i = wload[0] % 2
            wload[0] += 1
            ws, wb = wst[i], wbf[i]
            k.dma("pool" if i else "sp", ws[:, 0:nk, 0:ncol], view_ap, dst=ws)
            h = nk // 2
            k.op("dve", lambda: nc.vector.tensor_copy(out=wb[:, 0:h, 0:ncol], in_=ws[:, 0:h, 0:ncol]),
                 reads=[ws], writes=[wb])
            k.op("pool", lambda: nc.gpsimd.tensor_copy(out=wb[:, h:nk, 0:ncol], in_=ws[:, h:nk, 0:ncol]),
                 reads=[ws], parts=[wb])
            return wb

        TA = 4
        tiles = list(range(NTH // 128))
        esA = ExitStack()
        es_outer, k.es = k.es, esA
        hTy_all = k.sb("hTy_all", [128, 2 * KC, TA * 128], BF16)
        hT2t = k.sb("hT2t", [128, KC, 128], BF16)
        for g0 in range(0, len(tiles), TA):
            grp = tiles[g0:g0 + TA]
            for ti, tt in enumerate(grp):
                for half, (src, nwn) in enumerate(((ya, "nwa"), (ys, "nws"))):
                    yt = big[(2 * ti + half) % 4]
                    k.dma("sp", yt[:], src[tt * 128:(tt + 1) * 128, :], dst=yt)
                    hv = hTy_all
                    rms_tile_to_hT(k, yt, nwc[nwn], ident_b, _KView(hTy_all, half * KC), ti * 128, D, EPS, ps_tr,
                                   ss[half], xn[half])
            for ti, tt in enumerate(grp):
                xt = big[ti % 4]
                k.dma("sp", xt[:], x[tt * 128:(tt + 1) * 128, :], dst=xt)
            mmi = 0
            for dc in range(D // 256):
                wbs = []
                for pc in range(2 * KC // 16):
                    wbs.append((pc, load_w(w_out[dc, :, pc * 16:(pc + 1) * 16, :], 16, 256)))
                    pcc, wb = wbs[-1]
                    for ti, tt in enumerate(grp):
                        pm = ps_mm[ti]
                        for kk in range(16):
                            kc = pcc * 16 + kk
                            first = (pcc == 0 and kk == 0)
                            last = (pcc == 2 * KC // 16 - 1 and kk == 15)
                            k.op("pe", lambda: nc.tensor.matmul(pm[:, 0:256], lhsT=hTy_all[:, kc, ti * 128:(ti + 1) * 128],
                                                                rhs=wb[:, kk, 0:256], start=first, stop=last),
                                 reads=[wb, hTy_all], writes=[pm] if first else [], parts=[] if first else [pm])
                for ti, tt in enumerate(grp):
                    xt = big[ti % 4]
                    pm = ps_mm[ti]
                    k.op("dve", lambda: nc.vector.tensor_tensor(out=xt[:, dc * 256:(dc + 1) * 256],
                                                                in0=pm[:, 0:256], in1=xt[:, dc * 256:(dc + 1) * 256],
                                                                op=ALU.add), reads=[pm, xt], parts=[xt])
            for ti, tt in enumerate(grp):
                xt = big[ti % 4]
                k.dma("pool", x1d[tt * 128:(tt + 1) * 128, :], xt[:], src=xt, marks=[x1_reg[tt]])
                rms_tile_to_hT(k, xt, nwc["nwf"], ident_b, _KView(hT2t, 0), 0, D, EPS, ps_tr, ss[2], xn[ti % 2])
                k.dma("pool", h2d[:, :, tt * 128:(tt + 1) * 128], hT2t[:], src=hT2t, marks=[h2_reg[tt]])

        k.release([hTy_all, hT2t])
        esA.close()
        esB = ExitStack()
        k.es = esB
        hT2 = k.sb("hT2", [128, KC, GT + 2], BF16)
        gT = k.sb("gT", [128, NJ, GT], BF16)
        ub = [k.sb(f"ub{i}", [128, GT + 2], BF16) for i in range(4)]
        uhalo = k.sb("uhalo", [128, 2 * NJ, 2], BF16)
        dg = [k.sb(f"dg{i}", [128, 6, 128], BF16) for i in range(2)]
        ga = [k.sb(f"ga{i}", [128, GT], F32) for i in range(2)]
        for gi in range(NT // GT):
            t0 = 128 + gi * GT
            regs = [h2_reg[i] for i in range((t0 - 2) // 128, (t0 + GT - 1) // 128 + 1)]
            k.dma("sp", hT2[:], h2d[:, :, t0 - 2:t0 + GT], dst=hT2, after=regs)
            def ffn_front(j):
                i = wload[0] % 2
                wload[0] += 1
                ws, wb = wst[i], wbf[i]
                k.dma("sp", ws[:, 0:KC, :], w_up[j, :, :, :], dst=ws)
                h = KC // 2
                k.op("dve", lambda: nc.vector.tensor_copy(out=wb[:, 0:h, :], in_=ws[:, 0:h, :]), reads=[ws], writes=[wb])
                k.op("pool", lambda: nc.gpsimd.tensor_copy(out=wb[:, h:KC, :], in_=ws[:, h:KC, :]), reads=[ws], parts=[wb])
                dgt = dg[j % 2]
                for half in range(2):
                    cj = half * NJ + j
                    for kk in range(3):
                        k.op("pool", lambda: nc.gpsimd.tensor_scalar(out=dgt[:, half * 3 + kk, :], in0=ident_b[:],
                                                                     scalar1=cw_t[:, cj, kk:kk + 1], scalar2=None,
                                                                     op0=ALU.mult),
                             reads=[ident_b, cw_t], writes=[dgt] if (half == 0 and kk == 0) else [],
                             parts=[] if (half == 0 and kk == 0) else [dgt])
                for half in range(2):
                    cj = half * NJ + j
                    pm = ps_mm[(2 * j + half) % 4]
                    u = ub[2 * (j % 2) + half]
                    for kc in range(KC):
                        k.op("pe", lambda: nc.tensor.matmul(pm[:, 0:GT], lhsT=wb[:, kc, half * 128:(half + 1) * 128],
                                                            rhs=hT2[:, kc, 2:2 + GT], start=(kc == 0), stop=(kc == KC - 1)),
                             reads=[wb, hT2], writes=[pm] if kc == 0 else [], parts=[pm] if kc else [])
                    if gi == 0:
                        pm2 = ps_h2[half]
                        for kc in range(KC):
                            k.op("pe", lambda: nc.tensor.matmul(pm2[:, 0:2], lhsT=wb[:, kc, half * 128:(half + 1) * 128],
                                                                rhs=hT2[:, kc, 0:2], start=(kc == 0), stop=(kc == KC - 1)),
                                 reads=[wb, hT2], writes=[pm2] if kc == 0 else [], parts=[pm2] if kc else [])
                        k.op("act", lambda: nc.scalar.copy(out=u[:, 0:2], in_=pm2[:, 0:2]), reads=[pm2], writes=[u])
                    else:
                        k.op("act", lambda: nc.scalar.copy(out=u[:, 0:2], in_=uhalo[:, cj, :]), reads=[uhalo], writes=[u])
                    k.op("act", lambda: nc.scalar.copy(out=u[:, 2:2 + GT], in_=pm[:, 0:GT]), reads=[pm], parts=[u])
                    k.op("dve", lambda: nc.vector.tensor_copy(out=uhalo[:, cj, :], in_=u[:, GT:GT + 2]), reads=[u], parts=[uhalo])

            def ffn_back(j):
                dgt = dg[j % 2]
                for half in range(2):
                    u = ub[2 * (j % 2) + half]
                    pc = ps_cv[half]
                    for kk in range(3):
                        k.op("pe", lambda: nc.tensor.matmul(pc[:, 0:GT], lhsT=dgt[:, half * 3 + kk, :], rhs=u[:, kk:kk + GT],
                                                            start=(kk == 0), stop=(kk == 2)),
                             reads=[dgt, u], writes=[pc] if kk == 0 else [], parts=[pc] if kk else [])
                gat = ga[j % 2]
                k.op("act", lambda: nc.scalar.activation(out=gat[:], in_=ps_cv[0][:, 0:GT], func=AF.Silu,
                                                         bias=cb_t[:, j:j + 1]), reads=[ps_cv[0], cb_t], writes=[gat])
                k.op("dve", lambda: nc.vector.scalar_tensor_tensor(out=gT[:, j, :], in0=ps_cv[1][:, 0:GT],
                                                                   scalar=cb_t[:, NJ + j:NJ + j + 1], in1=gat[:],
                                                                   op0=ALU.add, op1=ALU.mult),
                     reads=[ps_cv[1], cb_t, gat], parts=[gT] if j else [], writes=[] if j else [gT])
            ffn_front(0)
            for j in range(NJ):
                if j + 1 < NJ:
                    ffn_front(j + 1)
                ffn_back(j)
            ntt = GT // 128
            for ti in range(ntt):
                xt = big[ti]
                r0 = t0 + ti * 128
                k.dma("sp", xt[:], x1d[r0:r0 + 128, :], dst=xt, after=[x1_reg[r0 // 128]])
            for dc in range(D // 256):
                pieces = [(p0, min(16, NJ - p0)) for p0 in range(0, NJ, 16)]
                for pi, (p0, npc) in enumerate(pieces):
                    wb = load_w(w_down[dc, :, p0:p0 + npc, :], npc, 256)
                    for ti in range(ntt):
                        pm = ps_mm[ti]
                        for kk in range(npc):
                            first = (pi == 0 and kk == 0)
                            last = (pi == len(pieces) - 1 and kk == npc - 1)
                            k.op("pe", lambda: nc.tensor.matmul(pm[:, 0:256], lhsT=gT[:, p0 + kk, ti * 128:(ti + 1) * 128],
                                                                rhs=wb[:, kk, 0:256], start=first, stop=last),
                                 reads=[wb, gT], writes=[pm] if first else [], parts=[] if first else [pm])
                for ti in range(ntt):
                    xt = big[ti]
                    pm = ps_mm[ti]
                    k.op("dve", lambda: nc.vector.tensor_tensor(out=xt[:, dc * 256:(dc + 1) * 256], in0=pm[:, 0:256],
                                                                in1=xt[:, dc * 256:(dc + 1) * 256], op=ALU.add),
                         reads=[pm, xt], parts=[xt])
            for ti in range(ntt):
                xt = big[ti]
                s_ = ss[ti % 3]
                xq = xn[ti % 2]
                k.op("act", lambda: nc.scalar.activation(out=xq[:], in_=xt[:], func=AF.Square, accum_out=s_[:, 0:1]),
                     reads=[xt], writes=[xq, s_])
                k.op("dve", lambda: nc.vector.tensor_scalar(out=s_[:, 1:2], in0=s_[:, 0:1], scalar1=1.0 / D, scalar2=EPS,
                                                            op0=ALU.mult, op1=ALU.add), reads=[s_], writes=[s_])
                k.op("act", lambda: nc.scalar.activation(out=s_[:, 2:3], in_=s_[:, 1:2], func=AF.Sqrt), reads=[s_], writes=[s_])
                k.op("dve", lambda: nc.vector.reciprocal(out=s_[:, 3:4], in_=s_[:, 2:3]), reads=[s_], writes=[s_])
                k.op("dve", lambda: nc.vector.scalar_tensor_tensor(out=xt[:], in0=xt[:], scalar=s_[:, 3:4], in1=fw_t[:],
                                                                   op0=ALU.mult, op1=ALU.mult),
                     reads=[xt, s_, fw_t], writes=[xt])
                r0 = gi * GT + ti * 128
                k.dma("pool", out[r0:r0 + 128, :], xt[:], src=xt)
        k.finish(big)
        k.release([hT2, gT, uhalo] + ub + dg + ga)
        esB.close()
        k.es = es_outer
        print("tail kernel instructions:", k.n_ins)
    return nc


class _KView:
    def __init__(self, tl, off):
        self.tl = tl
        self.off = off
        self.lw = tl.lw
        self.rd = tl.rd
        self.prd = tl.prd

    def __getitem__(self, idx):
        p, kc, c = idx
        return self.tl.t[p, kc + self.off, c]


NEG = -30000.0


def ssd_consts():
    c = {}
    c["identb"] = np.eye(128).astype(ml_dtypes.bfloat16)
    c["identf"] = np.eye(128).astype(np.float32)
    l = np.arange(256)[None, :]
    s = np.arange(128)[:, None]
    c["causneg"] = np.where(l >= s, 0.0, NEG).astype(np.float32)
    sel = np.zeros((4, 4, 128), np.float32)
    for h in range(4):
        sel[h, h, :] = 1.0
    c["sel4"] = sel
    c["ones4"] = np.ones((4, 128), np.float32)
    return c


def ssd_body(nc, k, L, io):
    NCH = L // 256
    with ExitStack() as es2:
        k.es, es_old = es2, k.es
        identb = k.sb("s_identb", [128, 128], BF16)
        identf = k.sb("s_identf", [128, 128], F32)
        causneg = k.sb("s_causneg", [128, 256], F32)
        sel4 = k.sb("s_sel4", [4, 4, 128], F32)
        ones4 = k.sb("s_ones4", [4, 128], F32)
        cw_t = k.sb("s_cw", [128, 4, 4], F32)
        cb_t = k.sb("s_cb", [128, 4], F32)
        cbrow = k.sb("s_cbrow", [1, 512], F32)
        drow = k.sb("s_drow", [128, 256], F32)
        dtb = k.sb("s_dtb", [4, 1], F32)
        alog = k.sb("s_alog", [4, 2], F32)
        for t, n in ((identb, "identb"), (identf, "identf"), (causneg, "causneg"), (sel4, "sel4"), (ones4, "ones4"),
                     (cw_t, "cw"), (cb_t, "cb"), (cbrow, "cbrow"), (drow, "drow"), (dtb, "dtb")):
            k.dma("sp", t[:], io[n], dst=t)
        k.dma("sp", alog[:, 0:1], io["alog"], dst=alog)
        dtt = k.sb("s_dtt", [4, L], F32)
        acum = k.sb("s_acum", [4, L], F32)
        PCS = min(L, 2048)
        t1 = k.sb("s_t1", [4, PCS], F32)
        t2 = k.sb("s_t2", [4, PCS], F32)
        rst = k.sb("s_rst", [4, PCS], F32)
        V, S_ = nc.vector, nc.scalar
        k.op("act", lambda: S_.activation(out=alog[:, 1:2], in_=alog[:, 0:1], func=AF.Exp), reads=[alog], writes=[alog])
        k.op("dve", lambda: V.memset(rst[:], 1.0), writes=[rst])
        k.op("dve", lambda: V.memset(rst[:].rearrange("p (c q) -> p c q", q=256)[:, :, 0:1], 0.0), writes=[rst])
        for pc in range(L // PCS):
            sl = slice(pc * PCS, (pc + 1) * PCS)
            k.dma("sp", t1[:], io["dtT"][:, sl], dst=t1)
            k.op("dve", lambda: V.tensor_scalar(out=t1[:], in0=t1[:], scalar1=dtb[:, 0:1], scalar2=None, op0=ALU.add),
                 reads=[dtb], writes=[t1])
            k.op("act", lambda: S_.activation(out=t2[:], in_=t1[:], func=AF.Abs), reads=[t1], writes=[t2])
            k.op("act", lambda: S_.activation(out=t2[:], in_=t2[:], func=AF.Exp, scale=-1.0), reads=[t2], writes=[t2])
            k.op("act", lambda: S_.activation(out=t2[:], in_=t2[:], func=AF.Ln, bias=1.0), reads=[t2], writes=[t2])
            k.op("dve", lambda: V.tensor_scalar(out=t1[:], in0=t1[:], scalar1=0.0, scalar2=None, op0=ALU.max), reads=[t1], writes=[t1])
            k.op("dve", lambda: V.tensor_tensor(out=dtt[:, sl], in0=t1[:], in1=t2[:], op=ALU.add), reads=[t1, t2], parts=[dtt])
            k.op("dve", lambda: V.tensor_scalar(out=t1[:], in0=dtt[:, sl], scalar1=alog[:, 1:2], scalar2=-1.0, op0=ALU.mult,
                                                op1=ALU.mult), reads=[dtt, alog], writes=[t1])
            k.op("dve", lambda: V.tensor_tensor_scan(out=acum[:, sl], data0=rst[:], data1=t1[:], initial=0.0, op0=ALU.mult,
                                                     op1=ALU.add), reads=[t1, rst], parts=[acum])
        dgc = k.sb("s_dgc", [128, 4, 4, 128], BF16)
        for t in range(4):
            for kk in range(4):
                k.op("dve", lambda: V.tensor_scalar(out=dgc[:, t, kk, :], in0=identb[:], scalar1=cw_t[:, t, kk:kk + 1],
                                                    scalar2=None, op0=ALU.mult), reads=[identb, cw_t], parts=[dgc])
        xbc = [k.sb(f"s_xbc{i}", [128, 4, 3 + 256], BF16) for i in range(2)]
        zt = [k.sb(f"s_z{i}", [128, 256], F32) for i in range(2)]
        BT = k.sb("s_BT", [128, 256], BF16)
        CT = k.sb("s_CT", [128, 256], BF16)
        xsf = [k.sb(f"s_xsf{i}", [128, 256], F32) for i in range(2)]
        Btok = k.sb("s_Btok", [128, 2, 128], BF16)
        xdt = [k.sb(f"s_xdt{i}", [128, 256], BF16) for i in range(2)]
        xdd = [k.sb(f"s_xdd{i}", [128, 256], BF16) for i in range(2)]
        sc = [k.sb(f"s_sc{i}", [128, 16], F32) for i in range(2)]
        cdt = k.sb("s_cd", [128, 8], F32)
        R4 = k.sb("s_R4", [4, 4], F32)
        tmpE = [k.sb(f"s_tmpE{i}", [128, 256], F32) for i in range(2)]
        MT = [[k.sb(f"s_MT{h}_{i}", [128, 256], BF16) for i in range(2)] for h in range(4)]
        H = k.sb("s_H", [128, 256], F32)
        Hb = k.sb("s_Hb", [128, 256], BF16)
        yos = k.sb("s_yos", [128, 256], F32)
        yv = [k.sb(f"s_yv{i}", [128, 256], F32) for i in range(2)]
        p_conv = k.ps("sp_conv", [128, 512], F32)
        p_small = k.ps("sp_small", [128, 512], F32)
        p_cb = [k.ps(f"sp_cb{i}", [128, 512], F32) for i in range(2)]
        p_D = [k.ps(f"sp_D{i}", [128, 512], F32) for i in range(2)]
        p_y = k.ps("sp_y", [128, 512], F32)
        p_st = k.ps("sp_st", [128, 512], F32)
        k.op("dve", lambda: V.memset(H[:], 0.0), writes=[H])
        k.op("dve", lambda: V.memset(Hb[:], 0.0), writes=[Hb])
        di = 0
        for c in range(NCH):
            T0 = c * 256
            xb = xbc[c % 2]
            k.dma("sp", xb[:], io["xbcT"][:, T0:T0 + 259].rearrange("(t p) n -> p t n", p=128), dst=xb)
            k.op("dve", lambda: V.tensor_scalar(out=R4[:], in0=identf[0:4, 0:4], scalar1=acum[:, T0 + 255:T0 + 256],
                                                scalar2=None, op0=ALU.mult), reads=[identf, acum], writes=[R4])
            k.op("pe", lambda: nc.tensor.matmul(p_small[:, 0:4], lhsT=ones4[:], rhs=R4[:], start=True, stop=True),
                 reads=[ones4, R4], writes=[p_small])
            k.op("dve", lambda: V.tensor_copy(out=cdt[:, 0:4], in_=p_small[:, 0:4]), reads=[p_small], writes=[cdt])
            k.op("act", lambda: S_.activation(out=cdt[:, 4:8], in_=cdt[:, 0:4], func=AF.Exp), reads=[cdt], parts=[cdt])
            for t, dst in ((2, BT), (3, CT)):
                for kk in range(4):
                    k.op("pe", lambda: nc.tensor.matmul(p_conv[:, 0:256], lhsT=dgc[:, t, kk, :], rhs=xb[:, t, kk:kk + 256],
                                                        start=(kk == 0), stop=(kk == 3)),
                         reads=[dgc, xb], writes=[p_conv] if kk == 0 else [], parts=[p_conv] if kk else [])
                k.op("act", lambda: S_.activation(out=dst[:], in_=p_conv[:, 0:256], func=AF.Silu, bias=cb_t[:, t:t + 1]),
                     reads=[p_conv, cb_t], writes=[dst])
            for i in range(2):
                s_ = sc[i]
                k.op("pe", lambda: nc.tensor.matmul(p_small[:, 8:12], lhsT=dtt[:, T0 + 128 * i:T0 + 128 * i + 128],
                                                    rhs=identf[0:4, 0:4], start=True, stop=True),
                     reads=[dtt, identf], writes=[p_small])
                k.op("pe", lambda: nc.tensor.matmul(p_small[:, 12:16], lhsT=acum[:, T0 + 128 * i:T0 + 128 * i + 128],
                                                    rhs=identf[0:4, 0:4], start=True, stop=True),
                     reads=[acum, identf], parts=[p_small])
                k.op("dve", lambda: V.tensor_copy(out=s_[:, 0:8], in_=p_small[:, 8:16]), reads=[p_small], writes=[s_])
                k.op("act", lambda: S_.activation(out=s_[:, 8:12], in_=s_[:, 4:8], func=AF.Exp), reads=[s_], parts=[s_])
                k.op("dve", lambda: V.tensor_tensor(out=s_[:, 12:16], in0=cdt[:, 0:4], in1=s_[:, 4:8], op=ALU.subtract),
                     reads=[cdt, s_], parts=[s_])
                k.op("act", lambda: S_.activation(out=s_[:, 12:16], in_=s_[:, 12:16], func=AF.Exp), reads=[s_], parts=[s_])
                k.op("dve", lambda: V.tensor_tensor(out=s_[:, 12:16], in0=s_[:, 12:16], in1=s_[:, 0:4], op=ALU.mult),
                     reads=[s_], parts=[s_])
                for t in range(3):
                    for kk in range(4):
                        k.op("pe", lambda: nc.tensor.matmul(p_conv[:, 256:384], lhsT=xb[:, t, kk + 128 * i:kk + 128 * i + 128],
                                                            rhs=dgc[:, t, kk, :], start=(kk == 0), stop=False),
                             reads=[dgc, xb], writes=[p_conv] if kk == 0 else [], parts=[p_conv] if kk else [])
                    k.op("pe", lambda: nc.tensor.matmul(p_conv[:, 256:384], lhsT=ones4[0:1, :], rhs=cbrow[0:1, t * 128:(t + 1) * 128],
                                                        start=False, stop=True), reads=[ones4, cbrow], parts=[p_conv])
                    if t < 2:
                        k.op("act", lambda: S_.activation(out=xsf[i][:, t * 128:(t + 1) * 128], in_=p_conv[:, 256:384],
                                                          func=AF.Silu), reads=[p_conv], writes=[xsf[i]] if t == 0 else [],
                             parts=[xsf[i]] if t else [])
                    else:
                        k.op("act", lambda: S_.activation(out=Btok[:, i, :], in_=p_conv[:, 256:384], func=AF.Silu),
                             reads=[p_conv], writes=[Btok] if i == 0 else [], parts=[Btok] if i else [])
                for h in range(4):
                    k.op("dve", lambda: V.tensor_scalar(out=xdt[i][:, 64 * h:64 * h + 64], in0=xsf[i][:, 64 * h:64 * h + 64],
                                                        scalar1=s_[:, h:h + 1], scalar2=None, op0=ALU.mult),
                         reads=[xsf[i], s_], writes=[xdt[i]] if h == 0 else [], parts=[xdt[i]] if h else [])
                    k.op("dve", lambda: V.tensor_scalar(out=xdd[i][:, 64 * h:64 * h + 64], in0=xsf[i][:, 64 * h:64 * h + 64],
                                                        scalar1=s_[:, 12 + h:13 + h], scalar2=None, op0=ALU.mult),
                         reads=[xsf[i], s_], writes=[xdd[i]] if h == 0 else [], parts=[xdd[i]] if h else [])
            for i in range(2):
                k.op("pe", lambda: nc.tensor.matmul(p_cb[i][:, 0:256], lhsT=BT[:, 128 * i:128 * i + 128], rhs=CT[:, :],
                                                    start=True, stop=True), reads=[BT, CT], writes=[p_cb[i]])
            for h in range(4):
                for i in range(2):
                    pd = p_D[di % 2]
                    te = tmpE[di % 2]
                    di += 1
                    l0 = 128 * i
                    k.op("pe", lambda: nc.tensor.matmul(pd[:, l0:256], lhsT=sel4[:, h, :], rhs=acum[:, T0 + l0:T0 + 256],
                                                        start=True, stop=True), reads=[sel4, acum], writes=[pd])
                    k.op("dve", lambda: V.scalar_tensor_tensor(out=te[:, l0:256], in0=pd[:, l0:256], scalar=sc[i][:, 4 + h:5 + h],
                                                               in1=causneg[:, 0:256 - l0], op0=ALU.subtract, op1=ALU.add),
                         reads=[pd, sc[i], causneg], writes=[te])
                    k.op("act", lambda: S_.activation(out=te[:, l0:256], in_=te[:, l0:256], func=AF.Exp), reads=[te], writes=[te])
                    k.op("dve", lambda: V.tensor_tensor(out=MT[h][i][:, l0:256], in0=p_cb[i][:, l0:256], in1=te[:, l0:256],
                                                        op=ALU.mult), reads=[p_cb[i], te], writes=[MT[h][i]])
            for li in range(2):
                z_ = zt[li]
                k.dma("sp", z_[:], io["z"][T0 + 128 * li:T0 + 128 * li + 128, :], dst=z_)
                for h in range(4):
                    for i in range(li + 1):
                        k.op("pe", lambda: nc.tensor.matmul(p_y[:, 64 * h:64 * h + 64], lhsT=MT[h][i][:, 128 * li:128 * li + 128],
                                                            rhs=xdt[i][:, 64 * h:64 * h + 64], start=(i == 0), stop=(i == li)),
                             reads=[MT[h][i], xdt[i]], writes=[p_y] if (h == 0 and i == 0) else [],
                             parts=[] if (h == 0 and i == 0) else [p_y])
                for h in range(4):
                    k.op("pe", lambda: nc.tensor.matmul(p_y[:, 256 + 64 * h:256 + 64 * h + 64], lhsT=CT[:, 128 * li:128 * li + 128],
                                                        rhs=Hb[:, 64 * h:64 * h + 64], start=True, stop=True),
                         reads=[CT, Hb], parts=[p_y])
                for h in range(4):
                    k.op("dve", lambda: V.tensor_scalar(out=yos[:, 64 * h:64 * h + 64], in0=p_y[:, 256 + 64 * h:256 + 64 * h + 64],
                                                        scalar1=sc[li][:, 8 + h:9 + h], scalar2=None, op0=ALU.mult),
                         reads=[p_y, sc[li]], writes=[yos] if h == 0 else [], parts=[yos] if h else [])
                y_ = yv[li]
                k.op("dve", lambda: V.tensor_tensor(out=y_[:], in0=p_y[:, 0:256], in1=yos[:], op=ALU.add),
                     reads=[p_y, yos], writes=[y_])
                k.op("dve", lambda: V.tensor_tensor(out=yos[:], in0=xsf[li][:], in1=drow[:], op=ALU.mult),
                     reads=[xsf[li], drow], writes=[yos])
                k.op("dve", lambda: V.tensor_tensor(out=y_[:], in0=y_[:], in1=yos[:], op=ALU.add), reads=[yos], writes=[y_])
                k.op("act", lambda: S_.activation(out=z_[:], in_=z_[:], func=AF.Silu), reads=[z_], writes=[z_])
                k.op("dve", lambda: V.tensor_tensor(out=y_[:], in0=y_[:], in1=z_[:], op=ALU.mult), reads=[z_], writes=[y_])
                k.dma("sp", io["ys"][T0 + 128 * li:T0 + 128 * li + 128, :], y_[:], src=y_)
            for i in range(2):
                k.op("pe", lambda: nc.tensor.matmul(p_st[:, 0:256], lhsT=Btok[:, i, :], rhs=xdd[i][:], start=(i == 0), stop=(i == 1)),
                     reads=[Btok, xdd[i]], writes=[p_st] if i == 0 else [], parts=[p_st] if i else [])
            for h in range(4):
                k.op("dve", lambda: V.scalar_tensor_tensor(out=H[:, 64 * h:64 * h + 64], in0=H[:, 64 * h:64 * h + 64],
                                                           scalar=cdt[:, 4 + h:5 + h], in1=p_st[:, 64 * h:64 * h + 64],
                                                           op0=ALU.mult, op1=ALU.add), reads=[cdt, p_st], writes=[H])
            k.op("dve", lambda: V.tensor_copy(out=Hb[:], in_=H[:]), reads=[H], writes=[Hb])
        k.finish(yv)
        k.es = es_old


def build_ssd(L):
    nc = bass.Bass("TRN2", target_bir_lowering=False)
    shapes = {"xbcT": ([512, 3 + L], BF16), "cw": ([128, 4, 4], F32), "cb": ([128, 4], F32), "cbrow": ([1, 512], F32),
              "dtT": ([4, L], F32), "dtb": ([4, 1], F32), "alog": ([4, 1], F32), "drow": ([128, 256], F32),
              "z": ([L, 256], F32), "identb": ([128, 128], BF16), "identf": ([128, 128], F32),
              "causneg": ([128, 256], F32), "sel4": ([4, 4, 128], F32), "ones4": ([4, 128], F32)}
    io = {}
    for n, (s, d) in shapes.items():
        t = nc.dram_tensor(n, s, d, kind="ExternalInput").ap()
        io[n] = t[tuple(slice(None) for _ in s)]
    io["xbcT"] = nc_ap(io["xbcT"])
    io["ys"] = nc.dram_tensor("ys", [L, 256], F32, kind="ExternalOutput").ap()
    with ExitStack() as es:
        k = KB(nc, es)
        ssd_body(nc, k, L, io)
        print("ssd kernel instructions:", k.n_ins)
    return nc


def nc_ap(a):
    return a


SCALE = 128 ** -0.5
TINY = 1e-30


def attn_consts(L):
    NB = L // 64
    NCT = max(1, (L // 16) // 128)
    c = {}
    j = np.arange(128)[:, None].astype(np.float64)
    i512 = np.arange(512)[None, :]
    c["R0"] = (i512 - j).astype(np.float32)
    c["R0c"] = (np.arange(128)[None, :] - 16 * j - 31).astype(np.float32)
    x = np.arange(2176)[None, :]
    c["Mc"] = np.where(x - 16 * j - 31 >= 0, 0.0, NEG).astype(ml_dtypes.bfloat16)
    x = np.arange(1408)[None, :]
    d = x - 384 - j
    c["Mw"] = np.where((d >= 0) & (d < 512), 0.0, NEG).astype(ml_dtypes.bfloat16)
    x = np.arange(896)[None, :]
    c["Mcaus"] = np.where(x - 384 - j >= 0, 0.0, NEG).astype(ml_dtypes.bfloat16)
    nbr = min(128, NB)
    es = np.zeros((nbr, 64, 128), np.float32)
    for kt in range(64):
        for jj in range(128):
            b = 2 * kt + jj // 64
            if b < nbr:
                es[b, kt, jj] = 1.0
    c["Esel"] = es.astype(ml_dtypes.bfloat16)
    A = np.zeros((NCT * 128, NB + 1), np.float32)
    for b in range(NB):
        for off, wgt in ((3, 1.0), (2, 2.0), (1, 2.0), (0, 2.0), (-1, 1.0)):
            r = 4 * b + off
            if 0 <= r < NCT * 128:
                A[r, b] += wgt
    A[:, NB] = 1.0
    c["Aext"] = A.reshape(NCT, 128, NB + 1).transpose(1, 0, 2).astype(ml_dtypes.bfloat16).copy()
    p = np.arange(128)[:, None]
    xx = np.arange(2 * NB)[None, :] - NB
    hi = (p >= 64).astype(np.int64)
    cand = (xx <= hi)
    c["cand"] = cand.astype(np.float32)
    c["negm"] = cand.astype(np.float32) - 1.0
    c["force"] = np.where((xx == hi) | (xx == hi - 1), 1e4, -2.0).astype(np.float32)
    mm = 128.0 * (np.arange(132) - 3)[None, :]
    c["Dtab"] = (mm - j).astype(np.float32)
    c["Dtab3"] = (mm - 16 * j - 31).astype(np.float32)
    c["irow"] = np.tile(np.arange(512, dtype=np.float32)[None, :], (128, 1))
    c["identb"] = np.eye(128).astype(ml_dtypes.bfloat16)
    return c


def attn_head_params(heads):
    sl = np.array([2.0 ** (-8.0 * (h + 1) / 16.0) for h in heads])
    row = np.concatenate([-sl / SCALE, -sl]).astype(np.float32)
    return np.tile(row[None, :], (128, 1))


def attn_body(nc, k, L, io):
    NB = L // 64
    NBR = min(128, NB)
    NBH = max(1, NB // 128)
    NCT = max(1, (L // 16) // 128)
    NKT = L // 128
    W = 512
    V, S_, PE = nc.vector, nc.scalar, nc.tensor
    with ExitStack() as es2:
        k.es, es_old = es2, k.es
        cst = {}
        for n, shp, dt in (("R0", [128, 512], F32), ("R0c", [128, 128], F32), ("Mc", [128, 2176], BF16),
                           ("Mw", [128, 1408], BF16), ("Mcaus", [128, 896], BF16), ("Esel", [NBR, 64, 128], BF16),
                           ("Aext", [128, NCT, NB + 1], BF16), ("cand", [128, 2 * NB], F32), ("negm", [128, 2 * NB], F32),
                           ("force", [128, 2 * NB], F32), ("Dtab", [128, 132], F32), ("Dtab3", [128, 132], F32), ("irow", [128, 512], F32),
                           ("identb", [128, 128], BF16),
                           ("hp", [128, 8], F32)):
            cst[n] = k.sb("a_" + n, shp, dt)
            k.dma("sp", cst[n][:], io[n], dst=cst[n])
        R0, R0c, Mc, Mw, Mcaus, Esel, Aext, identb, hp = (cst[n] for n in
                                                          ("R0", "R0c", "Mc", "Mw", "Mcaus", "Esel", "Aext", "identb", "hp"))
        Btab = k.sb("a_Btab", [128, 4, 132], F32)
        Btab3 = k.sb("a_Btab3", [128, 4, 132], F32)
        shrow = k.sb("a_shrow", [128, 4, 512], BF16)
        ones_b = k.sb("a_onesb", [128, 128], BF16)
        k.op("dve", lambda: V.memset(ones_b[:], 1.0), writes=[ones_b])
        for h in range(4):
            k.op("dve", lambda: V.tensor_scalar(out=Btab[:, h, :], in0=cst["Dtab"][:], scalar1=hp[:, 4 + h:5 + h], scalar2=None,
                                                op0=ALU.mult), reads=[cst["Dtab"], hp], parts=[Btab])
            k.op("dve", lambda: V.tensor_scalar(out=Btab3[:, h, :], in0=cst["Dtab3"][:], scalar1=hp[:, 4 + h:5 + h], scalar2=None,
                                                op0=ALU.mult), reads=[cst["Dtab3"], hp], parts=[Btab3])
            k.op("dve", lambda: V.tensor_scalar(out=shrow[:, h, :], in0=cst["irow"][:], scalar1=hp[:, h:h + 1], scalar2=None,
                                                op0=ALU.mult), reads=[cst["irow"], hp], parts=[shrow])
        pS = [k.ps(f"ap_S{i}", [128, 512], F32) for i in range(2)]
        pO = [k.ps(f"ap_O{i}", [128, 512], F32) for i in range(4)]
        pT = k.ps("ap_T", [128, 1024], BF16)
        pX = k.ps("ap_X", [128, 512], F32)
        ksT = k.sb("a_ksT", [128, L], BF16)
        vse = k.sb("a_vse", [128, NKT, 130], BF16)
        kccT = k.sb("a_kccT", [128, NCT * 128], BF16)
        vcce = k.sb("a_vcce", [128, NCT, 130], BF16)
        k.dma("sp", ksT[:], io["ksT"], dst=ksT)
        k.dma("sp", vse[:, :, 0:128], io["vs"].rearrange("(t p) d -> p t d", p=128), dst=vse)
        k.op("dve", lambda: V.memset(vse[:, :, 128:129], 1.0), parts=[vse])
        k.op("dve", lambda: V.memset(vcce[:, :, 128:129], 1.0), writes=[vcce])

        with ExitStack() as es3:
            k.es = es3
            src = k.sb("c_src", [128, L + 32], BF16)
            w1s = k.sb("c_w1s", [128, 8, 256], F32)
            w1b = k.sb("c_w1b", [128, 32, 256], BF16)
            w2s = k.sb("c_w2s", [128, 2, 128], F32)
            w2b = k.sb("c_w2b", [128, 2, 128], BF16)
            posf = k.sb("c_posf", [128, 32], F32)
            posb_ = k.sb("c_posb", [128, 32], BF16)
            pbias = k.sb("c_pbias", [128, 2], F32)
            xh = k.sb("c_xh", [128, 512], F32)
            t3 = k.sb("c_t3", [128, 512], F32)
            hidb = k.sb("c_hidb", [128, 2, NCT * 128], BF16)
            for which in ("k", "v"):
                k.op("dve", lambda: V.memset(src[:, L:L + 32], 0.0), writes=[src])
                k.dma("sp", src[:, 0:L], io["kcT" if which == "k" else "vcT"], dst=src, part=True)
                w1v = io["w1_" + which].rearrange("(i p) m -> p i m", p=128)
                for q4 in range(4):
                    k.dma("sp", w1s[:], w1v[:, q4 * 8:(q4 + 1) * 8, :], dst=w1s)
                    k.op("dve", lambda: V.tensor_copy(out=w1b[:, q4 * 8:(q4 + 1) * 8, :], in_=w1s[:]), reads=[w1s],
                         writes=[w1b] if q4 == 0 else [], parts=[w1b] if q4 else [])
                k.dma("sp", w2s[:], io["w2_" + which].rearrange("(t p) d -> p t d", p=128), dst=w2s)
                k.op("dve", lambda: V.tensor_copy(out=w2b[:], in_=w2s[:]), reads=[w2s], writes=[w2b])
                k.dma("sp", posf[:], io["posT_" + which], dst=posf)
                k.op("dve", lambda: V.tensor_copy(out=posb_[:], in_=posf[:]), reads=[posf], writes=[posb_])
                for mt in range(2):
                    for i in range(32):
                        k.op("pe", lambda: PE.matmul(pX[:, mt:mt + 1], lhsT=w1b[:, i, mt * 128:(mt + 1) * 128], rhs=posb_[:, i:i + 1],
                                                     start=(i == 0), stop=(i == 31)), reads=[w1b, posb_],
                             writes=[pX] if (i == 0 and mt == 0) else [], parts=[] if (i == 0 and mt == 0) else [pX])
                k.op("dve", lambda: V.tensor_copy(out=pbias[:], in_=pX[:, 0:2]), reads=[pX], writes=[pbias])
                CW = min(512, NCT * 128)
                for cc in range(NCT * 128 // CW):
                    for mt in range(2):
                        ps = pS[mt]
                        for i in range(32):
                            c0 = 16 * CW * cc + i
                            k.op("pe", lambda: PE.matmul(ps[:, 0:CW], lhsT=w1b[:, i, mt * 128:(mt + 1) * 128],
                                                         rhs=src[:, c0:c0 + 16 * (CW - 1) + 1:16], start=(i == 0), stop=(i == 31)),
                                 reads=[w1b, src], writes=[ps] if i == 0 else [], parts=[ps] if i else [])
                        k.op("dve", lambda: V.tensor_scalar(out=xh[:, 0:CW], in0=ps[:, 0:CW], scalar1=pbias[:, mt:mt + 1],
                                                            scalar2=None, op0=ALU.add), reads=[ps, pbias], writes=[xh])
                        k.op("dve", lambda: V.tensor_tensor(out=t3[:, 0:CW], in0=xh[:, 0:CW], in1=xh[:, 0:CW], op=ALU.mult),
                             reads=[xh], writes=[t3])
                        k.op("dve", lambda: V.tensor_tensor(out=t3[:, 0:CW], in0=t3[:, 0:CW], in1=xh[:, 0:CW], op=ALU.mult),
                             reads=[xh], writes=[t3])
                        k.op("dve", lambda: V.scalar_tensor_tensor(out=t3[:, 0:CW], in0=t3[:, 0:CW], scalar=0.044715,
                                                                   in1=xh[:, 0:CW], op0=ALU.mult, op1=ALU.add),
                             reads=[xh], writes=[t3])
                        k.op("act", lambda: S_.activation(out=t3[:, 0:CW], in_=t3[:, 0:CW], func=AF.Sigmoid,
                                                          scale=1.5957691216057308), reads=[t3], writes=[t3])
                        k.op("dve", lambda: V.tensor_tensor(out=hidb[:, mt, cc * CW:(cc + 1) * CW], in0=xh[:, 0:CW],
                                                            in1=t3[:, 0:CW], op=ALU.mult), reads=[xh, t3], parts=[hidb])
                if which == "k":
                    for cc in range(NCT * 128 // CW):
                        for mt in range(2):
                            k.op("pe", lambda: PE.matmul(pX[:, 0:CW], lhsT=w2b[:, mt, :], rhs=hidb[:, mt, cc * CW:(cc + 1) * CW],
                                                         start=(mt == 0), stop=(mt == 1)), reads=[w2b, hidb],
                                 writes=[pX] if mt == 0 else [], parts=[pX] if mt else [])
                        k.op("act", lambda: S_.copy(out=kccT[:, cc * CW:(cc + 1) * CW], in_=pX[:, 0:CW]), reads=[pX], parts=[kccT])
                else:
                    for ct in range(NCT):
                        for mt in range(2):
                            k.op("pe", lambda: PE.matmul(pX[:, 0:128], lhsT=hidb[:, mt, ct * 128:(ct + 1) * 128], rhs=w2b[:, mt, :],
                                                         start=(mt == 0), stop=(mt == 1)), reads=[w2b, hidb],
                                 writes=[pX] if mt == 0 else [], parts=[pX] if mt else [])
                        k.op("act", lambda: S_.copy(out=vcce[:, ct, 0:128], in_=pX[:, 0:128]), reads=[pX], parts=[vcce])
            k.release([src, w1s, w1b, w2s, w2b, posf, posb_, pbias, xh, t3, hidb])
            k.es = es2

        qT = [k.sb(f"a_qT{i}", [128, 4, W], BF16) for i in range(2)]
        glt = k.sb("a_gl", [128, 4, 6], F32)
        kwT = k.sb("a_kwT", [128, 1024], BF16)
        vwe = k.sb("a_vwe", [128, 8, 130], BF16)
        k.op("dve", lambda: V.memset(vwe[:, :, 128:129], 1.0), writes=[vwe])
        tmpS = [k.sb(f"a_tmpS{i}", [128, 512], F32) for i in range(3)]
        PT = [k.sb(f"a_PT{i}", [128, 512], BF16) for i in range(3)]
        imp = k.sb("a_imp", [128, NB], F32)
        imp2 = k.sb("a_imp2", [128, NB], F32)
        imp3 = k.sb("a_imp3", [128, NB], F32)
        m8 = k.sb("a_m8", [128, 16], F32)
        mneg = k.sb("a_mneg", [128, NB], BF16)
        maskT = k.sb("a_maskT", [NBR, NBH, W], BF16)
        maskTh2 = [k.sb(f"a_maskTh{i}", [NBR, NBH, W], BF16) for i in range(2)]
        den = k.sb("a_den", [128, 8], F32)
        oacc = [[k.sb(f"a_oacc{hh}_{i}", [128, 128], F32) for i in range(4)] for hh in range(2)]
        rot = [0]

        def score_tile(ps, lhs_k, rhs_q, ncols, extra, h, btab, bcol):
            ops = [(lhs_k, rhs_q)] + extra
            for oi, (l_, r_) in enumerate(ops):
                k.op("pe", lambda: PE.matmul(ps[:, 0:ncols], lhsT=l_[1], rhs=r_[1], start=(oi == 0), stop=(oi == len(ops) - 1)),
                     reads=[l_[0], r_[0]], writes=[ps] if oi == 0 else [], parts=[ps] if oi else [])
            pt_ = PT[rot[0] % 3]
            rot[0] += 1
            k.op("act", lambda: S_.activation(out=pt_[:, 0:ncols], in_=ps[:, 0:ncols], func=AF.Exp, scale=SCALE,
                                              bias=btab[:, h, bcol:bcol + 1]), reads=[ps, btab], writes=[pt_])
            return pt_

        def finish_branch(po, ncol, hh, i, b, first):
            dcol = den[:, 0:1]
            k.op("dve", lambda: V.tensor_scalar(out=den[:, 0:1], in0=po[:, ncol:ncol + 1], scalar1=TINY, scalar2=None,
                                                op0=ALU.max), reads=[po], writes=[den])
            k.op("dve", lambda: V.reciprocal(out=den[:, 1:2], in_=den[:, 0:1]), reads=[den], writes=[den])
            k.op("dve", lambda: V.tensor_tensor(out=den[:, 2:3], in0=den[:, 1:2], in1=glt[:, i, b * 2 + hh:b * 2 + hh + 1],
                                                op=ALU.mult), reads=[den, glt], writes=[den])
            oa = oacc[hh][i]
            if first:
                k.op("dve", lambda: V.tensor_scalar(out=oa[:], in0=po[:, 0:128], scalar1=den[:, 2:3], scalar2=None, op0=ALU.mult),
                     reads=[po, den], writes=[oa])
            else:
                k.op("dve", lambda: V.scalar_tensor_tensor(out=oa[:], in0=po[:, 0:128], scalar=den[:, 2:3], in1=oa[:],
                                                           op0=ALU.mult, op1=ALU.add), reads=[po, den], writes=[oa])

        for qc in range(L // W):
            t0 = qc * W
            q_ = qT[qc % 2]
            k.dma("sp", q_[:], io["qT"][:, t0:t0 + W].rearrange("(h p) n -> p h n", p=128), dst=q_)
            k.dma("sp", glt[:], io["gl"][t0:t0 + W, :].rearrange("(i p) c -> p i c", p=128), dst=glt)
            k.op("act", lambda: S_.activation(out=glt[:], in_=glt[:], func=AF.Sigmoid), writes=[glt])
            klo = max(0, t0 - 512)
            khi = min(L, t0 + 512)
            k.dma("sp", kwT[:, klo - (t0 - 512):khi - (t0 - 512)], io["kwT"][:, klo:khi], dst=kwT)
            k.dma("sp", vwe[:, (klo - (t0 - 512)) // 128:(khi - (t0 - 512)) // 128, 0:128],
                  io["vw"][klo:khi, :].rearrange("(t p) d -> p t d", p=128), dst=vwe, part=True)
            for i in range(4):
                t = t0 + 128 * i
                ctmax = min(NCT - 1, (t + 127) // 2048)
                for h in range(4):
                    own = h < 2
                    pU, pOc = (pO[0], pO[1]) if h % 2 == 0 else (pO[2], pO[3])
                    def c_front(ct):
                        dlt = t - 2048 * ct
                        extra = [((ones_b, ones_b[0:1, :]), (shrow, shrow[0:1, h, 0:128]))]
                        if dlt <= 2048:
                            extra.append(((identb, identb[:]), (Mc, Mc[:, dlt:dlt + 128])))
                        return score_tile(pS[rot[0] % 2], (kccT, kccT[:, ct * 128:(ct + 1) * 128]),
                                          (q_, q_[:, h, 128 * i:128 * i + 128]), 128, extra, h, Btab3, dlt // 128 + 3)

                    def c_back(ct, pt_):
                        k.op("pe", lambda: PE.matmul(pU[:, 0:NB + 1], lhsT=pt_[:, 0:128], rhs=Aext[:, ct, :], start=(ct == 0),
                                                     stop=(ct == ctmax)), reads=[pt_, Aext],
                             writes=[pU] if ct == 0 else [], parts=[pU] if ct else [])
                        if own:
                            k.op("pe", lambda: PE.matmul(pOc[:, 0:129], lhsT=pt_[:, 0:128], rhs=vcce[:, ct, 0:129], start=(ct == 0),
                                                         stop=(ct == ctmax)), reads=[pt_, vcce],
                                 writes=[pOc] if ct == 0 else [], parts=[pOc] if ct else [])
                    prev = None
                    for ct in range(ctmax + 1):
                        cur = (ct, c_front(ct))
                        if prev is not None:
                            c_back(*prev)
                        prev = cur
                    c_back(*prev)
                    k.op("dve", lambda: V.tensor_scalar(out=den[:, 4:5], in0=pU[:, NB:NB + 1], scalar1=TINY, scalar2=None,
                                                        op0=ALU.max), reads=[pU], writes=[den])
                    k.op("dve", lambda: V.reciprocal(out=den[:, 5:6], in_=den[:, 4:5]), reads=[den], writes=[den])
                    if h == 0:
                        k.op("dve", lambda: V.tensor_scalar(out=imp[:], in0=pU[:, 0:NB], scalar1=den[:, 5:6], scalar2=None,
                                                            op0=ALU.mult), reads=[pU, den], writes=[imp])
                    else:
                        k.op("dve", lambda: V.scalar_tensor_tensor(out=imp[:], in0=pU[:, 0:NB], scalar=den[:, 5:6], in1=imp[:],
                                                                   op0=ALU.mult, op1=ALU.add), reads=[pU, den], writes=[imp])
                    if own:
                        finish_branch(pOc, 128, h, i, 0, True)
                qt = t // 128
                x0 = NB - 2 * qt
                k.op("dve", lambda: V.tensor_tensor(out=imp2[:], in0=imp[:], in1=cst["cand"][:, x0:x0 + NB], op=ALU.mult),
                     reads=[imp, cst["cand"]], writes=[imp2])
                k.op("dve", lambda: V.tensor_tensor(out=imp2[:], in0=imp2[:], in1=cst["negm"][:, x0:x0 + NB], op=ALU.add),
                     reads=[cst["negm"]], writes=[imp2])
                k.op("dve", lambda: V.tensor_tensor(out=imp2[:], in0=imp2[:], in1=cst["force"][:, x0:x0 + NB], op=ALU.max),
                     reads=[cst["force"]], writes=[imp2])
                k.op("dve", lambda: V.memset(imp2[:, 0:1], 1e4), writes=[imp2])
                k.op("dve", lambda: V.max(out=m8[:, 0:8], in_=imp2[:]), reads=[imp2], writes=[m8])
                k.op("dve", lambda: V.match_replace(out=imp3[:], in_to_replace=m8[:, 0:8], in_values=imp2[:], imm_value=-3.0),
                     reads=[imp2, m8], writes=[imp3])
                k.op("dve", lambda: V.max(out=m8[:, 8:16], in_=imp3[:]), reads=[imp3], writes=[m8])
                k.op("dve", lambda: V.scalar_tensor_tensor(out=imp3[:], in0=imp2[:], scalar=m8[:, 15:16],
                                                           in1=cst["cand"][:, x0:x0 + NB], op0=ALU.is_ge, op1=ALU.mult),
                     reads=[imp2, m8, cst["cand"]], writes=[imp3])
                k.op("dve", lambda: V.tensor_scalar(out=mneg[:], in0=imp3[:], scalar1=-NEG, scalar2=NEG, op0=ALU.mult,
                                                    op1=ALU.add), reads=[imp3], writes=[mneg])
                for hf in range(NBH):
                    k.op("pe", lambda: PE.transpose(pT[0:NBR, hf * 128:(hf + 1) * 128], mneg[:, hf * NBR:(hf + 1) * NBR], identb[:]),
                         reads=[mneg, identb], writes=[pT] if hf == 0 else [], parts=[pT] if hf else [])
                for hf in range(NBH):
                    k.op("act", lambda: S_.copy(out=maskT[:, hf, 128 * i:128 * i + 128], in_=pT[0:NBR, hf * 128:(hf + 1) * 128]),
                         reads=[pT], writes=[maskT] if (i == 0 and hf == 0) else [], parts=[] if (i == 0 and hf == 0) else [maskT])
            for hh in range(2):
                maskTh = maskTh2[hh]
                for hf in range(NBH):
                    k.op("dve", lambda: V.tensor_tensor(out=maskTh[:, hf, :], in0=maskT[:, hf, :], in1=shrow[0:NBR, hh, :], op=ALU.add),
                         reads=[maskT, shrow], writes=[maskTh] if hf == 0 else [], parts=[maskTh] if hf else [])
                kts = list(range(0, (t0 + W) // 128))
                def s_front(kt):
                    k0 = kt * 128
                    extra = [((Esel, Esel[:, kt % 64, :]), (maskTh, maskTh[:, kt // 64, :]))]
                    if k0 >= t0:
                        xo = t0 - k0 + 384
                        extra.append(((identb, identb[:]), (Mcaus, Mcaus[:, xo:xo + W])))
                    return score_tile(pS[rot[0] % 2], (ksT, ksT[:, k0:k0 + 128]), (q_, q_[:, hh, :]), W, extra, hh, Btab,
                                      (t0 - k0) // 128 + 3)

                def s_back(ki, kt, pt_):
                    for i in range(4):
                        k.op("pe", lambda: PE.matmul(pO[i][:, 0:129], lhsT=pt_[:, 128 * i:128 * i + 128], rhs=vse[:, kt, 0:129],
                                                     start=(ki == 0), stop=(ki == len(kts) - 1)), reads=[pt_, vse],
                             writes=[pO[i]] if ki == 0 else [], parts=[pO[i]] if ki else [])
                prev = None
                for ki, kt in enumerate(kts):
                    cur = (ki, kt, s_front(kt))
                    if prev is not None:
                        s_back(*prev)
                    prev = cur
                s_back(*prev)
                for i in range(4):
                    finish_branch(pO[i], 128, hh, i, 1, False)
            for hh in range(2):
                kts = [m for m in range(8) if t0 - 512 + 128 * m >= 0 and t0 - 512 + 128 * m < L]
                def w_front(m):
                    k0 = t0 - 512 + 128 * m
                    xo = t0 - k0 + 384
                    extra = [((identb, identb[:]), (Mw, Mw[:, xo:xo + W])), ((ones_b, ones_b[0:1, :]), (shrow, shrow[0:1, hh, :]))]
                    return score_tile(pS[rot[0] % 2], (kwT, kwT[:, 128 * m:128 * m + 128]), (q_, q_[:, hh, :]), W, extra, hh, Btab,
                                      (t0 - k0) // 128 + 3)

                def w_back(ki, m, pt_):
                    for i in range(4):
                        k.op("pe", lambda: PE.matmul(pO[i][:, 0:129], lhsT=pt_[:, 128 * i:128 * i + 128], rhs=vwe[:, m, 0:129],
                                                     start=(ki == 0), stop=(ki == len(kts) - 1)), reads=[pt_, vwe],
                             writes=[pO[i]] if ki == 0 else [], parts=[pO[i]] if ki else [])
                prev = None
                for ki, m in enumerate(kts):
                    cur = (ki, m, w_front(m))
                    if prev is not None:
                        w_back(*prev)
                    prev = cur
                w_back(*prev)
                for i in range(4):
                    finish_branch(pO[i], 128, hh, i, 2, False)
            for hh in range(2):
                for i in range(4):
                    oa = oacc[hh][i]
                    k.dma("sp", io["ya"][t0 + 128 * i:t0 + 128 * i + 128, 128 * hh:128 * hh + 128], oa[:], src=oa)
        k.finish([oacc[hh][i] for hh in range(2) for i in range(4)])
        k.es = es_old


ATTN_IN = lambda L: {"qT": ([512, L], BF16), "kcT": ([128, L], BF16), "vcT": ([128, L], BF16), "ksT": ([128, L], BF16),
                     "kwT": ([128, L], BF16), "vs": ([L, 128], BF16), "vw": ([L, 128], BF16), "gl": ([L, 6], F32),
                     "posT_k": ([128, 32], F32), "w1_k": ([4096, 256], F32), "w2_k": ([256, 128], F32),
                     "posT_v": ([128, 32], F32), "w1_v": ([4096, 256], F32), "w2_v": ([256, 128], F32), "hp": ([128, 8], F32)}


def build_attn(L):
    nc = bass.Bass("TRN2", target_bir_lowering=False)
    io = {}
    cs = attn_consts(L)
    for n, (s, d) in ATTN_IN(L).items():
        io[n] = nc.dram_tensor(n, s, d, kind="ExternalInput").ap()
    for n, a in cs.items():
        d = BF16 if a.dtype == ml_dtypes.bfloat16 else F32
        io[n] = nc.dram_tensor(n, list(a.shape), d, kind="ExternalInput").ap()
    io["ya"] = nc.dram_tensor("ya", [L, 256], F32, kind="ExternalOutput").ap()
    with ExitStack() as es:
        k = KB(nc, es)
        attn_body(nc, k, L, io)
        print("attn kernel instructions:", k.n_ins)
    return nc


def proj_pack_w(w, blocks):
    D = w.shape[0]
    parts = [np.ascontiguousarray(w[:, c0:c0 + ncol].reshape(D // 128, 128, ncol).transpose(1, 0, 2)).reshape(-1)
             for (c0, ncol, _k, _n, _o) in blocks]
    return np.concatenate(parts) if parts else np.zeros(1, np.float32)


DFF = 5632
_bf = lambda a: np.ascontiguousarray(a).astype(ml_dtypes.bfloat16)
_col = lambda v: np.ascontiguousarray(np.asarray(v, np.float32).reshape(-1, 128).T)
_PROG = {}


def _run(nc, in_maps):
    res = run_bass_kernel_spmd(nc, in_maps, core_ids=list(range(NCORES)))
    return res.results


def kernel(**inp):
    L, NT = SEQ, SEQ // NCORES
    f32 = lambda a: np.ascontiguousarray(np.asarray(a, np.float32))
    x = f32(inp["x"])[0]
    identb = np.eye(128).astype(ml_dtypes.bfloat16)
    blocks, nfm = p1_blocks()
    spec = {"fm": ((nfm, NT), BF16), "dtT": ((32, NT), F32), "z": ((NT, 2048), F32), "vtm": ((NT, 1024), BF16),
            "gl": ((NT, 48), F32)}
    nc1 = build_proj(D_MODEL, NT, C_END, blocks, spec)
    w_in = proj_pack_w(f32(inp["w_in"])[0], blocks)
    nw1 = _col(inp["mix_norm_w"][0])
    r1 = _run(nc1, [{"x": x[c * NT:(c + 1) * NT], "nw": nw1, "w": w_in, "identb": identb} for c in range(NCORES)])
    FM = np.concatenate([np.asarray(r["fm"]) for r in r1], axis=1)
    DT = np.concatenate([np.asarray(r["dtT"]) for r in r1], axis=1)
    Z = np.concatenate([np.asarray(r["z"]) for r in r1], axis=0)
    VTM = np.concatenate([np.asarray(r["vtm"]) for r in r1], axis=0)
    GL = np.concatenate([np.asarray(r["gl"]) for r in r1], axis=0)
    del r1
    RQ, RKC, RVC, RKS, RKW = 3072, 5120, 5632, 6144, 6656
    nc2 = build_ssd(L)
    sc = ssd_consts()
    convw = f32(inp["ssm_conv_w"])[0]
    convb = f32(inp["ssm_conv_b"])[0]
    maps = []
    for c in range(NCORES):
        g = c // 2
        chs = np.concatenate([np.arange(256 * c, 256 * c + 256), np.arange(2048 + 128 * g, 2048 + 128 * g + 128),
                              np.arange(2560 + 128 * g, 2560 + 128 * g + 128)])
        xT = np.zeros((512, 3 + L), ml_dtypes.bfloat16)
        xT[:, 3:] = FM[chs]
        cw = convw[:, chs]
        cb = convb[chs]
        m = dict(sc)
        m.update({"xbcT": xT, "cw": np.ascontiguousarray(cw.T.reshape(4, 128, 4).transpose(1, 0, 2)),
                  "cb": np.ascontiguousarray(cb.reshape(4, 128).T), "cbrow": np.ascontiguousarray(cb[None, :]),
                  "dtT": np.ascontiguousarray(DT[4 * c:4 * c + 4]), "dtb": f32(inp["ssm_dt_bias"])[0, 4 * c:4 * c + 4, None],
                  "alog": f32(inp["ssm_a_log"])[0, 4 * c:4 * c + 4, None],
                  "drow": np.ascontiguousarray(np.tile(np.repeat(f32(inp["ssm_d"])[0, 4 * c:4 * c + 4], 64)[None, :], (128, 1))),
                  "z": np.ascontiguousarray(Z[:, 256 * c:256 * c + 256])})
        maps.append(m)
    r2 = _run(nc2, maps)
    YS = np.concatenate([np.asarray(r["ys"]) for r in r2], axis=1)
    del r2, maps
    nc3 = build_attn(L)
    ac = attn_consts(L)
    maps = []
    for c in range(NCORES):
        g = c // 2
        own = [2 * c, 2 * c + 1]
        heads = own + [h for h in range(4 * g, 4 * g + 4) if h not in own]
        qrows = np.concatenate([np.arange(RQ + 128 * h, RQ + 128 * h + 128) for h in heads])
        glc = np.array([b * 16 + h for b in range(3) for h in own])
        m = dict(ac)
        m.update({"qT": np.ascontiguousarray(FM[qrows]), "kcT": np.ascontiguousarray(FM[RKC + 128 * g:RKC + 128 * g + 128]),
                  "vcT": np.ascontiguousarray(FM[RVC + 128 * g:RVC + 128 * g + 128]),
                  "ksT": np.ascontiguousarray(FM[RKS + 128 * g:RKS + 128 * g + 128]),
                  "kwT": np.ascontiguousarray(FM[RKW + 128 * g:RKW + 128 * g + 128]),
                  "vs": np.ascontiguousarray(VTM[:, 128 * g:128 * g + 128]),
                  "vw": np.ascontiguousarray(VTM[:, 512 + 128 * g:512 + 128 * g + 128]),
                  "gl": np.ascontiguousarray(GL[:, glc]),
                  "posT_k": np.ascontiguousarray(f32(inp["cmp_pos_k"])[0].T), "w1_k": f32(inp["cmp_w1_k"])[0],
                  "w2_k": f32(inp["cmp_w2_k"])[0],
                  "posT_v": np.ascontiguousarray(f32(inp["cmp_pos_v"])[0].T), "w1_v": f32(inp["cmp_w1_v"])[0],
                  "w2_v": f32(inp["cmp_w2_v"])[0], "hp": attn_head_params(heads)})
        maps.append(m)
    r3 = _run(nc3, maps)
    YA = np.concatenate([np.asarray(r["ya"]) for r in r3], axis=1)
    del r3, maps
    nc4 = build_tail(D_MODEL, NT, DFF, GT=512)
    fcw = f32(inp["ffn_conv_w"])[0]
    common = {"nwa": _col(inp["attn_norm_w"][0]), "nws": _col(inp["ssm_norm_w"][0]), "nwf": _col(inp["ffn_norm_w"][0]),
              "fwb": np.ascontiguousarray(np.tile(f32(inp["final_norm_w"])[None, :], (128, 1))),
              "w_out": np.ascontiguousarray(f32(inp["w_out"])[0].reshape(32, 128, 8, 256).transpose(2, 1, 0, 3)),
              "w_up": np.ascontiguousarray(f32(inp["w_up"])[0].reshape(16, 128, 2, DFF // 128, 128).transpose(3, 1, 0, 2, 4)
                                           .reshape(DFF // 128, 128, 16, 256)),
              "w_down": np.ascontiguousarray(f32(inp["w_down"])[0].reshape(DFF // 128, 128, 8, 256).transpose(2, 1, 0, 3)),
              "cw": np.ascontiguousarray(fcw.T.reshape(2 * DFF // 128, 128, 3).transpose(1, 0, 2)),
              "cb": _col(inp["ffn_conv_b"][0]), "identb": identb}

    def halo(a, c):
        o = np.zeros((NT + 128, a.shape[1]), np.float32)
        lo = c * NT - 128
        if lo < 0:
            o[128:] = a[0:NT]
        else:
            o[:] = a[lo:lo + NT + 128]
        return o
    maps = []
    for c in range(NCORES):
        m = dict(common)
        m.update({"x": halo(x, c), "ya": halo(YA, c), "ys": halo(YS, c)})
        maps.append(m)
    r4 = _run(nc4, maps)
    out = np.concatenate([np.asarray(r["out"]) for r in r4], axis=0)
    return out[None].astype(np.float32)
```

```python
import numpy as np
from contextlib import ExitStack
import ml_dtypes
import concourse.bass as bass
import concourse.mybir as mybir
from concourse.bass_utils import run_bass_kernel_spmd

F32 = mybir.dt.float32
BF16 = mybir.dt.bfloat16
AF = mybir.ActivationFunctionType
ALU = mybir.AluOpType
AX = mybir.AxisListType

SAME_ENGINE_SYNC = True


class Tl:
    def __init__(self, k, name, shape, dt, space="sbuf"):
        self.k = k
        self.name = name
        if space == "sbuf":
            self.t = k.es.enter_context(k.nc.sbuf_tensor(name, list(shape), dt))
        else:
            self.t = k.es.enter_context(k.nc.psum_tensor(name, list(shape), dt))
        self.lw = {}
        self.rd = {}
        self.prd = {}
        self.dcnts = {}

    def __getitem__(self, idx):
        return self.t[idx]

    def dsem(self, q):
        kind = "sw" if q == "pool" else "hw"
        key = ("d", self.name, kind)
        if key not in self.k.sems:
            self.k.sems[key] = self.k.es.enter_context(self.k.nc.semaphore("d%s_%s" % (kind, self.name)))
            self.dcnts[key] = 0
        return key


class Alias:
    def __init__(self, base, dtype):
        self.base = base
        self.t = base.t.bitcast(dtype)
        self.name = base.name

    def __getitem__(self, idx):
        return self.t[idx]

    lw = property(lambda s: s.base.lw, lambda s, v: setattr(s.base, "lw", v))
    rd = property(lambda s: s.base.rd, lambda s, v: setattr(s.base, "rd", v))
    prd = property(lambda s: s.base.prd, lambda s, v: setattr(s.base, "prd", v))


class Dr:
    def __init__(self):
        self.lw = {}
        self.rd = {}


class KB:
    def __init__(self, nc, es):
        self.nc = nc
        self.es = es
        self.E = {"pe": nc.tensor, "act": nc.scalar, "dve": nc.vector, "pool": nc.gpsimd, "sp": nc.sync}
        self.sems = {}
        self.cnt = {}
        for e in ("pe", "act", "dve", "pool"):
            self.sems[e] = es.enter_context(nc.semaphore("s_" + e))
            self.cnt[e] = 0
        self.waited = {e: {} for e in self.E}
        self.n_ins = 0

    def sb(self, name, shape, dt):
        return Tl(self, name, shape, dt, "sbuf")

    def ps(self, name, shape, dt=F32):
        return Tl(self, name, shape, dt, "psum")

    def _wait(self, eng, deps):
        for key, v in deps.items():
            if key == eng and (eng == "pe" or not SAME_ENGINE_SYNC):
                continue
            if self.waited[eng].get(key, 0) < v:
                self.E[eng].wait_ge(self.sems[key], v)
                self.waited[eng][key] = v
                self.n_ins += 1

    @staticmethod
    def _addd(deps, d):
        for key, v in d.items():
            if deps.get(key, 0) < v:
                deps[key] = v

    def op(self, eng, fn, reads=(), writes=(), parts=()):
        deps = {}
        for r in reads:
            self._addd(deps, r.lw)
        for w in writes:
            self._addd(deps, w.lw)
            self._addd(deps, w.rd)
        for w in parts:
            self._addd(deps, w.rd)
            self._addd(deps, getattr(w, "prd", {}))
        self._wait(eng, deps)
        ins = fn()
        self.cnt[eng] += 1
        c = self.cnt[eng]
        ins.then_inc(self.sems[eng], 1)
        self.n_ins += 1
        for r in reads:
            if r.rd.get(eng, 0) < c:
                r.rd[eng] = c
        for w in writes:
            w.lw = {eng: c}
            w.prd = w.rd
            w.rd = {}
        for w in parts:
            w.lw[eng] = c
        return ins

    def dma(self, q, out, in_, dst=None, src=None, part=False, after=(), marks=(), **kw):
        deps = {}
        for d in after:
            self._addd(deps, d.lw)
        if src is not None:
            self._addd(deps, src.lw)
        if dst is not None:
            if not part:
                self._addd(deps, dst.lw)
            else:
                self._addd(deps, dst.prd)
            self._addd(deps, dst.rd)
        self._wait(q, deps)
        ins = self.E[q].dma_start(out=out, in_=in_, **kw)
        self.n_ins += 1
        assert not (dst is not None and src is not None)
        if dst is not None:
            key = dst.dsem(q)
            dst.dcnts[key] += 16
            ins.then_inc(self.sems[key], 16)
            if part:
                dst.lw[key] = dst.dcnts[key]
            else:
                dst.lw = {key: dst.dcnts[key]}
                dst.prd = dst.rd
                dst.rd = {}
        elif src is not None:
            key = src.dsem(q)
            src.dcnts[key] += 16
            ins.then_inc(self.sems[key], 16)
            src.rd[key] = src.dcnts[key]
            for d in marks:
                d.lw[key] = src.dcnts[key]
        return ins

    def release(self, tiles):
        deps = {}
        for t in tiles:
            self._addd(deps, t.lw)
            self._addd(deps, t.rd)
            self._addd(deps, t.prd)
        for e in ("pe", "act", "dve", "pool", "sp"):
            self._wait(e, deps)

    def finish(self, tiles):
        deps = {}
        for t in tiles:
            self._addd(deps, t.lw)
            self._addd(deps, t.rd)
        self._wait("sp", deps)


def rms_tile_to_hT(k, xt, nw_col, ident_b, hT, col0, D, eps, ps_tr, ss, xn):
    nc = k.nc
    KC = D // 128
    k.op("act", lambda: nc.scalar.activation(out=xn[:], in_=xt[:, 0:D], func=AF.Square, accum_out=ss[:, 0:1]),
         reads=[xt], writes=[xn, ss])
    k.op("dve", lambda: nc.vector.tensor_scalar(out=ss[:, 1:2], in0=ss[:, 0:1], scalar1=1.0 / D, scalar2=eps,
                                                op0=ALU.mult, op1=ALU.add), reads=[ss], writes=[ss])
    k.op("act", lambda: nc.scalar.activation(out=ss[:, 2:3], in_=ss[:, 1:2], func=AF.Sqrt), reads=[ss], writes=[ss])
    k.op("dve", lambda: nc.vector.reciprocal(out=ss[:, 3:4], in_=ss[:, 2:3]), reads=[ss], writes=[ss])
    k.op("dve", lambda: nc.vector.tensor_scalar(out=xn[:], in0=xt[:, 0:D], scalar1=ss[:, 3:4], scalar2=None,
                                                op0=ALU.mult), reads=[xt, ss], writes=[xn])
    for g in range(KC // 4):
        pt = ps_tr[g % 2]
        for j in range(4):
            kc = g * 4 + j
            k.op("pe", lambda: nc.tensor.transpose(pt[:, j * 128:(j + 1) * 128], xn[:, kc * 128:(kc + 1) * 128],
                                                   ident_b[:]), reads=[xn, ident_b],
                 parts=[pt] if j else [], writes=[] if j else [pt])
        for j in range(4):
            kc = g * 4 + j
            if g % 2 == 0:
                k.op("dve", lambda: nc.vector.tensor_scalar(
                    out=hT[:, kc, col0:col0 + 128], in0=pt[:, j * 128:(j + 1) * 128],
                    scalar1=nw_col[:, kc:kc + 1], scalar2=None, op0=ALU.mult), reads=[pt, nw_col], parts=[hT])
            else:
                k.op("act", lambda: nc.scalar.activation(
                    out=hT[:, kc, col0:col0 + 128], in_=pt[:, j * 128:(j + 1) * 128],
                    func=AF.Copy, scale=nw_col[:, kc:kc + 1]), reads=[pt, nw_col], parts=[hT])


def rmsnorm_to_hT(k, x_ap, nw_col, ident_b, hT, n_tok_tiles, D, eps, ps_tr, xt_tiles, tok0=0):
    for tt in range(n_tok_tiles):
        xt = xt_tiles["x"][tt % 2]
        k.dma("sp", xt[:], x_ap[tt * 128:(tt + 1) * 128, :], dst=xt)
        rms_tile_to_hT(k, xt, nw_col, ident_b, hT, tok0 + tt * 128, D, eps, ps_tr,
                       xt_tiles["ss"][tt % 2], xt_tiles["xn"][tt % 2])


D_MODEL = 2048
SEQ = 16384
NCORES = 8
EPS = 1e-6
C_Z, C_XBC, C_DT, C_Q, C_KC, C_VC, C_KS, C_VS, C_KW, C_VW, C_GL, C_END = (
    0, 2048, 5120, 5152, 7200, 7712, 8224, 8736, 9248, 9760, 10272, 10320)


def p1_blocks():
    blocks = []
    r = 0
    for c0, c1 in ((C_XBC, C_DT), (C_Q, C_KC), (C_KC, C_VC), (C_VC, C_KS), (C_KS, C_VS), (C_KW, C_VW)):
        for c in range(c0, c1, 128):
            blocks.append((c, 128, "f", "fm", r))
            r += 128
    nfm = r
    blocks.append((C_DT, 32, "f", "dtT", 0))
    for j in range(4):
        blocks.append((C_Z + 512 * j, 512, "t", "z", 512 * j))
    blocks.append((C_VS, 512, "t", "vtm", 0))
    blocks.append((C_VW, 512, "t", "vtm", 512))
    blocks.append((C_GL, 48, "t", "gl", 0))
    return blocks, nfm


def build_proj(D, NT, NCOLS, blocks, outs_spec):
    nc = bass.Bass("TRN2", target_bir_lowering=False)
    KC = D // 128
    x = nc.dram_tensor("x", [NT, D], F32, kind="ExternalInput").ap()
    nw = nc.dram_tensor("nw", [128, KC], F32, kind="ExternalInput").ap()
    wtot = sum(128 * KC * b[1] for b in blocks)
    w = nc.dram_tensor("w", [max(wtot, 1)], F32, kind="ExternalInput").ap()
    idb = nc.dram_tensor("identb", [128, 128], BF16, kind="ExternalInput").ap()
    outs = {n: nc.dram_tensor(n, list(s), d, kind="ExternalOutput").ap() for n, (s, d) in outs_spec.items()}
    woff = [0]
    with ExitStack() as es:
        k = KB(nc, es)
        ident_b = k.sb("ident_b", [128, 128], BF16)
        nw_col = k.sb("nw_col", [128, KC], F32)
        hT = k.sb("hT", [128, KC, NT], BF16)
        xt_tiles = {
            "x": [k.sb(f"xt{i}", [128, D], F32) for i in range(2)],
            "ss": [k.sb(f"ss{i}", [128, 4], F32) for i in range(2)],
            "xn": [k.sb(f"xn{i}", [128, D], BF16) for i in range(2)],
        }
        ps_tr = [k.ps(f"ps_tr{i}", [128, 512], BF16) for i in range(2)]
        ps_mm = [k.ps(f"ps_mm{i}", [128, 512], F32) for i in range(4)]
        wst = [k.sb(f"wst{i}", [128, KC, 512], F32) for i in range(2)]
        wbf = [k.sb(f"wbf{i}", [128, KC, 512], BF16) for i in range(2)]
        ob = [k.sb(f"ob{i}", [128, max(NT, 512)], F32) for i in range(2)]
        k.dma("sp", ident_b[:], idb[:, :], dst=ident_b)
        k.dma("sp", nw_col[:], nw[:, :], dst=nw_col)
        rmsnorm_to_hT(k, x, nw_col, ident_b, hT, NT // 128, D, EPS, ps_tr, xt_tiles)
        mmi = 0
        for bi, (c0, ncol, kind, oname, o0) in enumerate(blocks):
            ws, wb = wst[bi % 2], wbf[bi % 2]
            odt = outs_spec[oname][1]
            wblk = w[woff[0]:woff[0] + 128 * KC * ncol].rearrange("(p k c) -> p k c", p=128, k=KC)
            woff[0] += 128 * KC * ncol
            k.dma("pool" if bi % 2 else "sp", ws[:, :, 0:ncol], wblk, dst=ws)
            half = KC // 2
            k.op("dve", lambda: nc.vector.tensor_copy(out=wb[:, 0:half, 0:ncol], in_=ws[:, 0:half, 0:ncol]),
                 reads=[ws], writes=[wb])
            k.op("pool", lambda: nc.gpsimd.tensor_copy(out=wb[:, half:KC, 0:ncol], in_=ws[:, half:KC, 0:ncol]),
                 reads=[ws], parts=[wb])
            o = ob[bi % 2]
            if kind == "f":
                ov = o.t.bitcast(odt) if odt != F32 else o.t
                for ch in range(NT // 512):
                    pm = ps_mm[mmi % 4]
                    mmi += 1
                    for kc in range(KC):
                        k.op("pe", lambda: nc.tensor.matmul(pm[0:ncol, :], lhsT=wb[:, kc, 0:ncol],
                                                            rhs=hT[:, kc, ch * 512:(ch + 1) * 512],
                                                            start=(kc == 0), stop=(kc == KC - 1)),
                             reads=[wb, hT], writes=[pm] if kc == 0 else [], parts=[pm] if kc else [])
                    if ch % 2 == 0:
                        k.op("act", lambda: nc.scalar.copy(out=ov[0:ncol, ch * 512:(ch + 1) * 512], in_=pm[0:ncol, :]),
                             reads=[pm], writes=[o] if ch == 0 else [], parts=[o] if ch else [])
                    else:
                        k.op("dve", lambda: nc.vector.tensor_copy(out=ov[0:ncol, ch * 512:(ch + 1) * 512], in_=pm[0:ncol, :]),
                             reads=[pm], parts=[o])
                k.dma("pool", outs[oname][o0:o0 + ncol, :], ov[0:ncol, 0:NT], src=o)
            else:
                ov = o.t.bitcast(odt) if odt != F32 else o.t
                for tt in range(NT // 128):
                    pm = ps_mm[mmi % 4]
                    mmi += 1
                    for kc in range(KC):
                        k.op("pe", lambda: nc.tensor.matmul(pm[:, 0:ncol], lhsT=hT[:, kc, tt * 128:(tt + 1) * 128],
                                                            rhs=wb[:, kc, 0:ncol],
                                                            start=(kc == 0), stop=(kc == KC - 1)),
                             reads=[wb, hT], writes=[pm] if kc == 0 else [], parts=[pm] if kc else [])
                    o = ob[(bi + tt) % 2]
                    ov = o.t.bitcast(odt) if odt != F32 else o.t
                    if tt % 2 == 0:
                        k.op("act", lambda: nc.scalar.copy(out=ov[:, 0:ncol], in_=pm[:, 0:ncol]), reads=[pm], writes=[o])
                    else:
                        k.op("dve", lambda: nc.vector.tensor_copy(out=ov[:, 0:ncol], in_=pm[:, 0:ncol]), reads=[pm], writes=[o])
                    k.dma("pool", outs[oname][tt * 128:(tt + 1) * 128, o0:o0 + ncol], ov[:, 0:ncol], src=o)
        k.finish(ob)
        print("proj kernel instructions:", k.n_ins)
    return nc


def build_tail(D, NT, DFF, GT=512):
    nc = bass.Bass("TRN2", target_bir_lowering=False)
    KC = D // 128
    NTH = NT + 128
    NJ = DFF // 128
    x = nc.dram_tensor("x", [NTH, D], F32, kind="ExternalInput").ap()
    ya = nc.dram_tensor("ya", [NTH, D], F32, kind="ExternalInput").ap()
    ys = nc.dram_tensor("ys", [NTH, D], F32, kind="ExternalInput").ap()
    nws = {n: nc.dram_tensor(n, [128, KC], F32, kind="ExternalInput").ap() for n in ("nwa", "nws", "nwf")}
    fwb = nc.dram_tensor("fwb", [128, D], F32, kind="ExternalInput").ap()
    w_out = nc.dram_tensor("w_out", [D // 256, 128, 2 * KC, 256], F32, kind="ExternalInput").ap()
    w_up = nc.dram_tensor("w_up", [NJ, 128, KC, 256], F32, kind="ExternalInput").ap()
    w_down = nc.dram_tensor("w_down", [D // 256, 128, NJ, 256], F32, kind="ExternalInput").ap()
    cw = nc.dram_tensor("cw", [128, 2 * NJ, 3], F32, kind="ExternalInput").ap()
    cb = nc.dram_tensor("cb", [128, 2 * NJ], F32, kind="ExternalInput").ap()
    idb = nc.dram_tensor("identb", [128, 128], BF16, kind="ExternalInput").ap()
    out = nc.dram_tensor("out", [NT, D], F32, kind="ExternalOutput").ap()
    x1d = nc.dram_tensor("x1", [NTH, D], F32, kind="ExternalOutput").ap()
    h2d = nc.dram_tensor("h2T", [128, KC, NTH], BF16, kind="ExternalOutput").ap()
    with ExitStack() as es:
        k = KB(nc, es)
        ident_b = k.sb("ident_b", [128, 128], BF16)
        nwc = {n: k.sb(n + "_c", [128, KC], F32) for n in nws}
        cw_t = k.sb("cw_t", [128, 2 * NJ, 3], F32)
        cb_t = k.sb("cb_t", [128, 2 * NJ], F32)
        fw_t = k.sb("fw_t", [128, D], F32)
        big = [k.sb(f"big{i}", [128, D], F32) for i in range(4)]
        xn = [k.sb(f"xn{i}", [128, D], BF16) for i in range(2)]
        ss = [k.sb(f"ss{i}", [128, 4], F32) for i in range(3)]
        ps_tr = [k.ps(f"ps_tr{i}", [128, 512], BF16) for i in range(2)]
        ps_mm = [k.ps(f"ps_mm{i}", [128, 512], F32) for i in range(4)]
        ps_cv = [k.ps(f"ps_cv{i}", [128, 512], F32) for i in range(2)]
        ps_h2 = [Alias(t_, F32) for t_ in ps_tr]
        wst = [k.sb(f"wst{i}", [128, 16, 256], F32) for i in range(2)]
        wbf = [k.sb(f"wbf{i}", [128, 16, 256], BF16) for i in range(2)]
        k.dma("sp", ident_b[:], idb[:, :], dst=ident_b)
        for n in nws:
            k.dma("sp", nwc[n][:], nws[n][:, :], dst=nwc[n])
        k.dma("sp", cw_t[:], cw[:, :, :], dst=cw_t)
        k.dma("sp", cb_t[:], cb[:, :], dst=cb_t)
        k.dma("sp", fw_t[:], fwb[:, :], dst=fw_t)
        x1_reg = [Dr() for _ in range(NTH // 128)]
        h2_reg = [Dr() for _ in range(NTH // 128)]
        wload = [0]

        def load_w(view_ap, nk, ncol):
            i = wload[0] % 2
            wload[0] += 1
            ws, wb = wst[i], wbf[i]
            k.dma("pool" if i else "sp", ws[:, 0:nk, 0:ncol], view_ap, dst=ws)
            h = nk // 2
            k.op("dve", lambda: nc.vector.tensor_copy(out=wb[:, 0:h, 0:ncol], in_=ws[:, 0:h, 0:ncol]),
                 reads=[ws], writes=[wb])
            k.op("pool", lambda: nc.gpsimd.tensor_copy(out=wb[:, h:nk, 0:ncol], in_=ws[:, h:nk, 0:ncol]),
                 reads=[ws], parts=[wb])
            return wb

        TA = 4
        tiles = list(range(NTH // 128))
        esA = ExitStack()
        es_outer, k.es = k.es, esA
        hTy_all = k.sb("hTy_all", [128, 2 * KC, TA * 128], BF16)
        hT2t = k.sb("hT2t", [128, KC, 128], BF16)
        for g0 in range(0, len(tiles), TA):
            grp = tiles[g0:g0 + TA]
            for ti, tt in enumerate(grp):
                for half, (src, nwn) in enumerate(((ya, "nwa"), (ys, "nws"))):
                    yt = big[(2 * ti + half) % 4]
                    k.dma("sp", yt[:], src[tt * 128:(tt + 1) * 128, :], dst=yt)
                    hv = hTy_all
                    rms_tile_to_hT(k, yt, nwc[nwn], ident_b, _KView(hTy_all, half * KC), ti * 128, D, EPS, ps_tr,
                                   ss[half], xn[half])
            for ti, tt in enumerate(grp):
                xt = big[ti % 4]
                k.dma("sp", xt[:], x[tt * 128:(tt + 1) * 128, :], dst=xt)
            mmi = 0
            for dc in range(D // 256):
                wbs = []
                for pc in range(2 * KC // 16):
                    wbs.append((pc, load_w(w_out[dc, :, pc * 16:(pc + 1) * 16, :], 16, 256)))
                    pcc, wb = wbs[-1]
                    for ti, tt in enumerate(grp):
                        pm = ps_mm[ti]
                        for kk in range(16):
                            kc = pcc * 16 + kk
                            first = (pcc == 0 and kk == 0)
                            last = (pcc == 2 * KC // 16 - 1 and kk == 15)
                            k.op("pe", lambda: nc.tensor.matmul(pm[:, 0:256], lhsT=hTy_all[:, kc, ti * 128:(ti + 1) * 128],
                                                                rhs=wb[:, kk, 0:256], start=first, stop=last),
                                 reads=[wb, hTy_all], writes=[pm] if first else [], parts=[] if first else [pm])
                for ti, tt in enumerate(grp):
                    xt = big[ti % 4]
                    pm = ps_mm[ti]
                    k.op("dve", lambda: nc.vector.tensor_tensor(out=xt[:, dc * 256:(dc + 1) * 256],
                                                                in0=pm[:, 0:256], in1=xt[:, dc * 256:(dc + 1) * 256],
                                                                op=ALU.add), reads=[pm, xt], parts=[xt])
            for ti, tt in enumerate(grp):
                xt = big[ti % 4]
                k.dma("pool", x1d[tt * 128:(tt + 1) * 128, :], xt[:], src=xt, marks=[x1_reg[tt]])
                rms_tile_to_hT(k, xt, nwc["nwf"], ident_b, _KView(hT2t, 0), 0, D, EPS, ps_tr, ss[2], xn[ti % 2])
                k.dma("pool", h2d[:, :, tt * 128:(tt + 1) * 128], hT2t[:], src=hT2t, marks=[h2_reg[tt]])

        k.release([hTy_all, hT2t])
        esA.close()
        esB = ExitStack()
        k.es = esB
        hT2 = k.sb("hT2", [128, KC, GT + 2], BF16)
        gT = k.sb("gT", [128, NJ, GT], BF16)
        ub = [k.sb(f"ub{i}", [128, GT + 2], BF16) for i in range(4)]
        uhalo = k.sb("uhalo", [128, 2 * NJ, 2], BF16)
        dg = [k.sb(f"dg{i}", [128, 6, 128], BF16) for i in range(2)]
        ga = [k.sb(f"ga{i}", [128, GT], F32) for i in range(2)]
        for gi in range(NT // GT):
            t0 = 128 + gi * GT
            regs = [h2_reg[i] for i in range((t0 - 2) // 128, (t0 + GT - 1) // 128 + 1)]
            k.dma("sp", hT2[:], h2d[:, :, t0 - 2:t0 + GT], dst=hT2, after=regs)
            def ffn_front(j):
                i = wload[0] % 2
                wload[0] += 1
                ws, wb = wst[i], wbf[i]
                k.dma("sp", ws[:, 0:KC, :], w_up[j, :, :, :], dst=ws)
                h = KC // 2
                k.op("dve", lambda: nc.vector.tensor_copy(out=wb[:, 0:h, :], in_=ws[:, 0:h, :]), reads=[ws], writes=[wb])
                k.op("pool", lambda: nc.gpsimd.tensor_copy(out=wb[:, h:KC, :], in_=ws[:, h:KC, :]), reads=[ws], parts=[wb])
                dgt = dg[j % 2]
                for half in range(2):
                    cj = half * NJ + j
                    for kk in range(3):
                        k.op("act", lambda: nc.scalar.activation(out=dgt[:, half * 3 + kk, :], in_=ident_b[:], func=AF.Copy,
                                                                 scale=cw_t[:, cj, kk:kk + 1]),
                             reads=[ident_b, cw_t], writes=[dgt] if (half == 0 and kk == 0) else [],
                             parts=[] if (half == 0 and kk == 0) else [dgt])
                for half in range(2):
                    cj = half * NJ + j
                    pm = ps_mm[(2 * j + half) % 4]
                    u = ub[2 * (j % 2) + half]
                    for kc in range(KC):
                        k.op("pe", lambda: nc.tensor.matmul(pm[:, 0:GT], lhsT=wb[:, kc, half * 128:(half + 1) * 128],
                                                            rhs=hT2[:, kc, 2:2 + GT], start=(kc == 0), stop=(kc == KC - 1)),
                             reads=[wb, hT2], writes=[pm] if kc == 0 else [], parts=[pm] if kc else [])
                    if gi == 0:
                        pm2 = ps_h2[half]
                        for kc in range(KC):
                            k.op("pe", lambda: nc.tensor.matmul(pm2[:, 0:2], lhsT=wb[:, kc, half * 128:(half + 1) * 128],
                                                                rhs=hT2[:, kc, 0:2], start=(kc == 0), stop=(kc == KC - 1)),
                                 reads=[wb, hT2], writes=[pm2] if kc == 0 else [], parts=[pm2] if kc else [])
                        k.op("act", lambda: nc.scalar.copy(out=u[:, 0:2], in_=pm2[:, 0:2]), reads=[pm2], writes=[u])
                    else:
                        k.op("act", lambda: nc.scalar.copy(out=u[:, 0:2], in_=uhalo[:, cj, :]), reads=[uhalo], writes=[u])
                    k.op("act", lambda: nc.scalar.copy(out=u[:, 2:2 + GT], in_=pm[:, 0:GT]), reads=[pm], parts=[u])
                    k.op("dve", lambda: nc.vector.tensor_copy(out=uhalo[:, cj, :], in_=u[:, GT:GT + 2]), reads=[u], parts=[uhalo])

            def ffn_back(j):
                dgt = dg[j % 2]
                for half in range(2):
                    u = ub[2 * (j % 2) + half]
                    pc = ps_cv[half]
                    for kk in range(3):
                        k.op("pe", lambda: nc.tensor.matmul(pc[:, 0:GT], lhsT=dgt[:, half * 3 + kk, :], rhs=u[:, kk:kk + GT],
                                                            start=(kk == 0), stop=(kk == 2)),
                             reads=[dgt, u], writes=[pc] if kk == 0 else [], parts=[pc] if kk else [])
                gat = ga[j % 2]
                k.op("act", lambda: nc.scalar.activation(out=gat[:], in_=ps_cv[0][:, 0:GT], func=AF.Silu,
                                                         bias=cb_t[:, j:j + 1]), reads=[ps_cv[0], cb_t], writes=[gat])
                k.op("dve", lambda: nc.vector.scalar_tensor_tensor(out=gT[:, j, :], in0=ps_cv[1][:, 0:GT],
                                                                   scalar=cb_t[:, NJ + j:NJ + j + 1], in1=gat[:],
                                                                   op0=ALU.add, op1=ALU.mult),
                     reads=[ps_cv[1], cb_t, gat], parts=[gT] if j else [], writes=[] if j else [gT])
            ffn_front(0)
            for j in range(NJ):
                if j + 1 < NJ:
                    ffn_front(j + 1)
                ffn_back(j)
            ntt = GT // 128
            for ti in range(ntt):
                xt = big[ti]
                r0 = t0 + ti * 128
                k.dma("sp", xt[:], x1d[r0:r0 + 128, :], dst=xt, after=[x1_reg[r0 // 128]])
            for dc in range(D // 256):
                pieces = [(p0, min(16, NJ - p0)) for p0 in range(0, NJ, 16)]
                for pi, (p0, npc) in enumerate(pieces):
                    wb = load_w(w_down[dc, :, p0:p0 + npc, :], npc, 256)
                    for ti in range(ntt):
                        pm = ps_mm[ti]
                        for kk in range(npc):
                            first = (pi == 0 and kk == 0)
                            last = (pi == len(pieces) - 1 and kk == npc - 1)
                            k.op("pe", lambda: nc.tensor.matmul(pm[:, 0:256], lhsT=gT[:, p0 + kk, ti * 128:(ti + 1) * 128],
                                                                rhs=wb[:, kk, 0:256], start=first, stop=last),
                                 reads=[wb, gT], writes=[pm] if first else [], parts=[] if first else [pm])
                for ti in range(ntt):
                    xt = big[ti]
                    pm = ps_mm[ti]
                    k.op("dve", lambda: nc.vector.tensor_tensor(out=xt[:, dc * 256:(dc + 1) * 256], in0=pm[:, 0:256],
                                                                in1=xt[:, dc * 256:(dc + 1) * 256], op=ALU.add),
                         reads=[pm, xt], parts=[xt])
            for ti in range(ntt):
                xt = big[ti]
                s_ = ss[ti % 3]
                xq = xn[ti % 2]
                k.op("act", lambda: nc.scalar.activation(out=xq[:], in_=xt[:], func=AF.Square, accum_out=s_[:, 0:1]),
                     reads=[xt], writes=[xq, s_])
                k.op("dve", lambda: nc.vector.tensor_scalar(out=s_[:, 1:2], in0=s_[:, 0:1], scalar1=1.0 / D, scalar2=EPS,
                                                            op0=ALU.mult, op1=ALU.add), reads=[s_], writes=[s_])
                k.op("act", lambda: nc.scalar.activation(out=s_[:, 2:3], in_=s_[:, 1:2], func=AF.Sqrt), reads=[s_], writes=[s_])
                k.op("dve", lambda: nc.vector.reciprocal(out=s_[:, 3:4], in_=s_[:, 2:3]), reads=[s_], writes=[s_])
                k.op("dve", lambda: nc.vector.scalar_tensor_tensor(out=xt[:], in0=xt[:], scalar=s_[:, 3:4], in1=fw_t[:],
                                                                   op0=ALU.mult, op1=ALU.mult),
                     reads=[xt, s_, fw_t], writes=[xt])
                r0 = gi * GT + ti * 128
                k.dma("pool", out[r0:r0 + 128, :], xt[:], src=xt)
        k.finish(big)
        k.release([hT2, gT, uhalo] + ub + dg + ga)
        esB.close()
        k.es = es_outer
        print("tail kernel instructions:", k.n_ins)
    return nc


class _KView:
    def __init__(self, tl, off):
        self.tl = tl
        self.off = off
        self.lw = tl.lw
        self.rd = tl.rd
        self.prd = tl.prd

    def __getitem__(self, idx):
        p, kc, c = idx
        return self.tl.t[p, kc + self.off, c]


NEG = -30000.0


def ssd_consts():
    c = {}
    c["identb"] = np.eye(128).astype(ml_dtypes.bfloat16)
    c["identf"] = np.eye(128).astype(np.float32)
    l = np.arange(256)[None, :]
    s = np.arange(128)[:, None]
    c["causneg"] = np.where(l >= s, 0.0, NEG).astype(np.float32)
    sel = np.zeros((4, 4, 128), np.float32)
    for h in range(4):
        sel[h, h, :] = 1.0
    c["sel4"] = sel
    c["ones4"] = np.ones((4, 128), np.float32)
    return c


def ssd_body(nc, k, L, io):
    NCH = L // 256
    with ExitStack() as es2:
        k.es, es_old = es2, k.es
        identb = k.sb("s_identb", [128, 128], BF16)
        identf = k.sb("s_identf", [128, 128], F32)
        causneg = k.sb("s_causneg", [128, 256], F32)
        sel4 = k.sb("s_sel4", [4, 4, 128], F32)
        ones4 = k.sb("s_ones4", [4, 128], F32)
        cw_t = k.sb("s_cw", [128, 4, 4], F32)
        cb_t = k.sb("s_cb", [128, 4], F32)
        cbrow = k.sb("s_cbrow", [1, 512], F32)
        drow = k.sb("s_drow", [128, 256], F32)
        dtb = k.sb("s_dtb", [4, 1], F32)
        alog = k.sb("s_alog", [4, 2], F32)
        for t, n in ((identb, "identb"), (identf, "identf"), (causneg, "causneg"), (sel4, "sel4"), (ones4, "ones4"),
                     (cw_t, "cw"), (cb_t, "cb"), (cbrow, "cbrow"), (drow, "drow"), (dtb, "dtb")):
            k.dma("sp", t[:], io[n], dst=t)
        k.dma("sp", alog[:, 0:1], io["alog"], dst=alog)
        dtt = k.sb("s_dtt", [4, L], F32)
        acum = k.sb("s_acum", [4, L], F32)
        PCS = min(L, 2048)
        t1 = k.sb("s_t1", [4, PCS], F32)
        t2 = k.sb("s_t2", [4, PCS], F32)
        rst = k.sb("s_rst", [4, PCS], F32)
        V, S_ = nc.vector, nc.scalar
        k.op("act", lambda: S_.activation(out=alog[:, 1:2], in_=alog[:, 0:1], func=AF.Exp), reads=[alog], writes=[alog])
        k.op("dve", lambda: V.memset(rst[:], 1.0), writes=[rst])
        k.op("dve", lambda: V.memset(rst[:].rearrange("p (c q) -> p c q", q=256)[:, :, 0:1], 0.0), writes=[rst])
        for pc in range(L // PCS):
            sl = slice(pc * PCS, (pc + 1) * PCS)
            k.dma("sp", t1[:], io["dtT"][:, sl], dst=t1)
            k.op("dve", lambda: V.tensor_scalar(out=t1[:], in0=t1[:], scalar1=dtb[:, 0:1], scalar2=None, op0=ALU.add),
                 reads=[dtb], writes=[t1])
            k.op("act", lambda: S_.activation(out=t2[:], in_=t1[:], func=AF.Abs), reads=[t1], writes=[t2])
            k.op("act", lambda: S_.activation(out=t2[:], in_=t2[:], func=AF.Exp, scale=-1.0), reads=[t2], writes=[t2])
            k.op("act", lambda: S_.activation(out=t2[:], in_=t2[:], func=AF.Ln, bias=1.0), reads=[t2], writes=[t2])
            k.op("dve", lambda: V.tensor_scalar(out=t1[:], in0=t1[:], scalar1=0.0, scalar2=None, op0=ALU.max), reads=[t1], writes=[t1])
            k.op("dve", lambda: V.tensor_tensor(out=dtt[:, sl], in0=t1[:], in1=t2[:], op=ALU.add), reads=[t1, t2], parts=[dtt])
            k.op("dve", lambda: V.tensor_scalar(out=t1[:], in0=dtt[:, sl], scalar1=alog[:, 1:2], scalar2=-1.0, op0=ALU.mult,
                                                op1=ALU.mult), reads=[dtt, alog], writes=[t1])
            k.op("dve", lambda: V.tensor_tensor_scan(out=acum[:, sl], data0=rst[:], data1=t1[:], initial=0.0, op0=ALU.mult,
                                                     op1=ALU.add), reads=[t1, rst], parts=[acum])
        dgc = k.sb("s_dgc", [128, 4, 4, 128], BF16)
        for t in range(4):
            for kk in range(4):
                k.op("dve", lambda: V.tensor_scalar(out=dgc[:, t, kk, :], in0=identb[:], scalar1=cw_t[:, t, kk:kk + 1],
                                                    scalar2=None, op0=ALU.mult), reads=[identb, cw_t], parts=[dgc])
        xbc = [k.sb(f"s_xbc{i}", [128, 4, 3 + 256], BF16) for i in range(2)]
        zt = [k.sb(f"s_z{i}", [128, 256], F32) for i in range(2)]
        BT = k.sb("s_BT", [128, 256], BF16)
        CT = k.sb("s_CT", [128, 256], BF16)
        xsf = [k.sb(f"s_xsf{i}", [128, 256], F32) for i in range(2)]
        Btok = k.sb("s_Btok", [128, 2, 128], BF16)
        xdt = [k.sb(f"s_xdt{i}", [128, 256], BF16) for i in range(2)]
        xdd = [k.sb(f"s_xdd{i}", [128, 256], BF16) for i in range(2)]
        sc = [k.sb(f"s_sc{i}", [128, 16], F32) for i in range(2)]
        cdt = k.sb("s_cd", [128, 8], F32)
        R4 = k.sb("s_R4", [4, 4], F32)
        tmpE = [k.sb(f"s_tmpE{i}", [128, 256], F32) for i in range(2)]
        MT = [[k.sb(f"s_MT{h}_{i}", [128, 256], BF16) for i in range(2)] for h in range(4)]
        H = k.sb("s_H", [128, 256], F32)
        Hb = k.sb("s_Hb", [128, 256], BF16)
        yos = k.sb("s_yos", [128, 256], F32)
        yv = [k.sb(f"s_yv{i}", [128, 256], F32) for i in range(2)]
        p_conv = k.ps("sp_conv", [128, 512], F32)
        p_small = k.ps("sp_small", [128, 512], F32)
        p_cb = [k.ps(f"sp_cb{i}", [128, 512], F32) for i in range(2)]
        p_D = [k.ps(f"sp_D{i}", [128, 512], F32) for i in range(2)]
        p_y = k.ps("sp_y", [128, 512], F32)
        p_st = k.ps("sp_st", [128, 512], F32)
        k.op("dve", lambda: V.memset(H[:], 0.0), writes=[H])
        k.op("dve", lambda: V.memset(Hb[:], 0.0), writes=[Hb])
        di = 0
        for c in range(NCH):
            T0 = c * 256
            xb = xbc[c % 2]
            k.dma("sp", xb[:], io["xbcT"][:, T0:T0 + 259].rearrange("(t p) n -> p t n", p=128), dst=xb)
            k.op("dve", lambda: V.tensor_scalar(out=R4[:], in0=identf[0:4, 0:4], scalar1=acum[:, T0 + 255:T0 + 256],
                                                scalar2=None, op0=ALU.mult), reads=[identf, acum], writes=[R4])
            k.op("pe", lambda: nc.tensor.matmul(p_small[:, 0:4], lhsT=ones4[:], rhs=R4[:], start=True, stop=True),
                 reads=[ones4, R4], writes=[p_small])
            k.op("dve", lambda: V.tensor_copy(out=cdt[:, 0:4], in_=p_small[:, 0:4]), reads=[p_small], writes=[cdt])
            k.op("act", lambda: S_.activation(out=cdt[:, 4:8], in_=cdt[:, 0:4], func=AF.Exp), reads=[cdt], parts=[cdt])
            for t, dst in ((2, BT), (3, CT)):
                for kk in range(4):
                    k.op("pe", lambda: nc.tensor.matmul(p_conv[:, 0:256], lhsT=dgc[:, t, kk, :], rhs=xb[:, t, kk:kk + 256],
                                                        start=(kk == 0), stop=(kk == 3)),
                         reads=[dgc, xb], writes=[p_conv] if kk == 0 else [], parts=[p_conv] if kk else [])
                k.op("act", lambda: S_.activation(out=dst[:], in_=p_conv[:, 0:256], func=AF.Silu, bias=cb_t[:, t:t + 1]),
                     reads=[p_conv, cb_t], writes=[dst])
            for i in range(2):
                s_ = sc[i]
                k.op("pe", lambda: nc.tensor.matmul(p_small[:, 8:12], lhsT=dtt[:, T0 + 128 * i:T0 + 128 * i + 128],
                                                    rhs=identf[0:4, 0:4], start=True, stop=True),
                     reads=[dtt, identf], writes=[p_small])
                k.op("pe", lambda: nc.tensor.matmul(p_small[:, 12:16], lhsT=acum[:, T0 + 128 * i:T0 + 128 * i + 128],
                                                    rhs=identf[0:4, 0:4], start=True, stop=True),
                     reads=[acum, identf], parts=[p_small])
                k.op("dve", lambda: V.tensor_copy(out=s_[:, 0:8], in_=p_small[:, 8:16]), reads=[p_small], writes=[s_])
                k.op("act", lambda: S_.activation(out=s_[:, 8:12], in_=s_[:, 4:8], func=AF.Exp), reads=[s_], parts=[s_])
                k.op("dve", lambda: V.tensor_tensor(out=s_[:, 12:16], in0=cdt[:, 0:4], in1=s_[:, 4:8], op=ALU.subtract),
                     reads=[cdt, s_], parts=[s_])
                k.op("act", lambda: S_.activation(out=s_[:, 12:16], in_=s_[:, 12:16], func=AF.Exp), reads=[s_], parts=[s_])
                k.op("dve", lambda: V.tensor_tensor(out=s_[:, 12:16], in0=s_[:, 12:16], in1=s_[:, 0:4], op=ALU.mult),
                     reads=[s_], parts=[s_])
                for t in range(3):
                    for kk in range(4):
                        k.op("pe", lambda: nc.tensor.matmul(p_conv[:, 256:384], lhsT=xb[:, t, kk + 128 * i:kk + 128 * i + 128],
                                                            rhs=dgc[:, t, kk, :], start=(kk == 0), stop=False),
                             reads=[dgc, xb], writes=[p_conv] if kk == 0 else [], parts=[p_conv] if kk else [])
                    k.op("pe", lambda: nc.tensor.matmul(p_conv[:, 256:384], lhsT=ones4[0:1, :], rhs=cbrow[0:1, t * 128:(t + 1) * 128],
                                                        start=False, stop=True), reads=[ones4, cbrow], parts=[p_conv])
                    if t < 2:
                        k.op("act", lambda: S_.activation(out=xsf[i][:, t * 128:(t + 1) * 128], in_=p_conv[:, 256:384],
                                                          func=AF.Silu), reads=[p_conv], writes=[xsf[i]] if t == 0 else [],
                             parts=[xsf[i]] if t else [])
                    else:
                        k.op("act", lambda: S_.activation(out=Btok[:, i, :], in_=p_conv[:, 256:384], func=AF.Silu),
                             reads=[p_conv], writes=[Btok] if i == 0 else [], parts=[Btok] if i else [])
                for h in range(4):
                    k.op("dve", lambda: V.tensor_scalar(out=xdt[i][:, 64 * h:64 * h + 64], in0=xsf[i][:, 64 * h:64 * h + 64],
                                                        scalar1=s_[:, h:h + 1], scalar2=None, op0=ALU.mult),
                         reads=[xsf[i], s_], writes=[xdt[i]] if h == 0 else [], parts=[xdt[i]] if h else [])
                    k.op("dve", lambda: V.tensor_scalar(out=xdd[i][:, 64 * h:64 * h + 64], in0=xsf[i][:, 64 * h:64 * h + 64],
                                                        scalar1=s_[:, 12 + h:13 + h], scalar2=None, op0=ALU.mult),
                         reads=[xsf[i], s_], writes=[xdd[i]] if h == 0 else [], parts=[xdd[i]] if h else [])
            for i in range(2):
                k.op("pe", lambda: nc.tensor.matmul(p_cb[i][:, 0:256], lhsT=BT[:, 128 * i:128 * i + 128], rhs=CT[:, :],
                                                    start=True, stop=True), reads=[BT, CT], writes=[p_cb[i]])
            for h in range(4):
                for i in range(2):
                    pd = p_D[di % 2]
                    te = tmpE[di % 2]
                    di += 1
                    l0 = 128 * i
                    k.op("pe", lambda: nc.tensor.matmul(pd[:, l0:256], lhsT=sel4[:, h, :], rhs=acum[:, T0 + l0:T0 + 256],
                                                        start=True, stop=True), reads=[sel4, acum], writes=[pd])
                    k.op("dve", lambda: V.scalar_tensor_tensor(out=te[:, l0:256], in0=pd[:, l0:256], scalar=sc[i][:, 4 + h:5 + h],
                                                               in1=causneg[:, 0:256 - l0], op0=ALU.subtract, op1=ALU.add),
                         reads=[pd, sc[i], causneg], writes=[te])
                    k.op("act", lambda: S_.activation(out=te[:, l0:256], in_=te[:, l0:256], func=AF.Exp), reads=[te], writes=[te])
                    k.op("dve", lambda: V.tensor_tensor(out=MT[h][i][:, l0:256], in0=p_cb[i][:, l0:256], in1=te[:, l0:256],
                                                        op=ALU.mult), reads=[p_cb[i], te], writes=[MT[h][i]])
            for li in range(2):
                z_ = zt[li]
                k.dma("sp", z_[:], io["z"][T0 + 128 * li:T0 + 128 * li + 128, :], dst=z_)
                for h in range(4):
                    for i in range(li + 1):
                        k.op("pe", lambda: nc.tensor.matmul(p_y[:, 64 * h:64 * h + 64], lhsT=MT[h][i][:, 128 * li:128 * li + 128],
                                                            rhs=xdt[i][:, 64 * h:64 * h + 64], start=(i == 0), stop=(i == li)),
                             reads=[MT[h][i], xdt[i]], writes=[p_y] if (h == 0 and i == 0) else [],
                             parts=[] if (h == 0 and i == 0) else [p_y])
                for h in range(4):
                    k.op("pe", lambda: nc.tensor.matmul(p_y[:, 256 + 64 * h:256 + 64 * h + 64], lhsT=CT[:, 128 * li:128 * li + 128],
                                                        rhs=Hb[:, 64 * h:64 * h + 64], start=True, stop=True),
                         reads=[CT, Hb], parts=[p_y])
                for h in range(4):
                    k.op("dve", lambda: V.tensor_scalar(out=yos[:, 64 * h:64 * h + 64], in0=p_y[:, 256 + 64 * h:256 + 64 * h + 64],
                                                        scalar1=sc[li][:, 8 + h:9 + h], scalar2=None, op0=ALU.mult),
                         reads=[p_y, sc[li]], writes=[yos] if h == 0 else [], parts=[yos] if h else [])
                y_ = yv[li]
                k.op("dve", lambda: V.tensor_tensor(out=y_[:], in0=p_y[:, 0:256], in1=yos[:], op=ALU.add),
                     reads=[p_y, yos], writes=[y_])
                k.op("dve", lambda: V.tensor_tensor(out=yos[:], in0=xsf[li][:], in1=drow[:], op=ALU.mult),
                     reads=[xsf[li], drow], writes=[yos])
                k.op("dve", lambda: V.tensor_tensor(out=y_[:], in0=y_[:], in1=yos[:], op=ALU.add), reads=[yos], writes=[y_])
                k.op("act", lambda: S_.activation(out=z_[:], in_=z_[:], func=AF.Silu), reads=[z_], writes=[z_])
                k.op("dve", lambda: V.tensor_tensor(out=y_[:], in0=y_[:], in1=z_[:], op=ALU.mult), reads=[z_], writes=[y_])
                k.dma("sp", io["ys"][T0 + 128 * li:T0 + 128 * li + 128, :], y_[:], src=y_)
            for i in range(2):
                k.op("pe", lambda: nc.tensor.matmul(p_st[:, 0:256], lhsT=Btok[:, i, :], rhs=xdd[i][:], start=(i == 0), stop=(i == 1)),
                     reads=[Btok, xdd[i]], writes=[p_st] if i == 0 else [], parts=[p_st] if i else [])
            for h in range(4):
                k.op("dve", lambda: V.scalar_tensor_tensor(out=H[:, 64 * h:64 * h + 64], in0=H[:, 64 * h:64 * h + 64],
                                                           scalar=cdt[:, 4 + h:5 + h], in1=p_st[:, 64 * h:64 * h + 64],
                                                           op0=ALU.mult, op1=ALU.add), reads=[cdt, p_st], writes=[H])
            k.op("dve", lambda: V.tensor_copy(out=Hb[:], in_=H[:]), reads=[H], writes=[Hb])
        k.finish(yv)
        k.es = es_old


def build_ssd(L):
    nc = bass.Bass("TRN2", target_bir_lowering=False)
    shapes = {"xbcT": ([512, 3 + L], BF16), "cw": ([128, 4, 4], F32), "cb": ([128, 4], F32), "cbrow": ([1, 512], F32),
              "dtT": ([4, L], F32), "dtb": ([4, 1], F32), "alog": ([4, 1], F32), "drow": ([128, 256], F32),
              "z": ([L, 256], F32), "identb": ([128, 128], BF16), "identf": ([128, 128], F32),
              "causneg": ([128, 256], F32), "sel4": ([4, 4, 128], F32), "ones4": ([4, 128], F32)}
    io = {}
    for n, (s, d) in shapes.items():
        t = nc.dram_tensor(n, s, d, kind="ExternalInput").ap()
        io[n] = t[tuple(slice(None) for _ in s)]
    io["xbcT"] = nc_ap(io["xbcT"])
    io["ys"] = nc.dram_tensor("ys", [L, 256], F32, kind="ExternalOutput").ap()
    with ExitStack() as es:
        k = KB(nc, es)
        ssd_body(nc, k, L, io)
        print("ssd kernel instructions:", k.n_ins)
    return nc


def nc_ap(a):
    return a


SCALE = 128 ** -0.5
TINY = 1e-30


def attn_consts(L):
    NB = L // 64
    NCT = max(1, (L // 16) // 128)
    c = {}
    j = np.arange(128)[:, None].astype(np.float64)
    i512 = np.arange(512)[None, :]
    c["R0"] = (i512 - j).astype(np.float32)
    c["R0c"] = (np.arange(128)[None, :] - 16 * j - 31).astype(np.float32)
    x = np.arange(2176)[None, :]
    c["Mc"] = np.where(x - 16 * j - 31 >= 0, 0.0, NEG).astype(ml_dtypes.bfloat16)
    x = np.arange(1408)[None, :]
    d = x - 384 - j
    c["Mw"] = np.where((d >= 0) & (d < 512), 0.0, NEG).astype(ml_dtypes.bfloat16)
    x = np.arange(896)[None, :]
    c["Mcaus"] = np.where(x - 384 - j >= 0, 0.0, NEG).astype(ml_dtypes.bfloat16)
    nbr = min(128, NB)
    es = np.zeros((nbr, 64, 128), np.float32)
    for kt in range(64):
        for jj in range(128):
            b = 2 * kt + jj // 64
            if b < nbr:
                es[b, kt, jj] = 1.0
    c["Esel"] = es.astype(ml_dtypes.bfloat16)
    A = np.zeros((NCT * 128, NB + 1), np.float32)
    for b in range(NB):
        for off, wgt in ((3, 1.0), (2, 2.0), (1, 2.0), (0, 2.0), (-1, 1.0)):
            r = 4 * b + off
            if 0 <= r < NCT * 128:
                A[r, b] += wgt
    A[:, NB] = 1.0
    c["Aext"] = A.reshape(NCT, 128, NB + 1).transpose(1, 0, 2).astype(ml_dtypes.bfloat16).copy()
    p = np.arange(128)[:, None]
    xx = np.arange(2 * NB)[None, :] - NB
    hi = (p >= 64).astype(np.int64)
    cand = (xx <= hi)
    c["cand"] = cand.astype(np.float32)
    c["negm"] = cand.astype(np.float32) - 1.0
    c["force"] = np.where((xx == hi) | (xx == hi - 1), 1e4, -2.0).astype(np.float32)
    mm = 128.0 * (np.arange(132) - 3)[None, :]
    c["Dtab"] = (mm - j).astype(np.float32)
    c["Dtab3"] = (mm - 16 * j - 31).astype(np.float32)
    c["irow"] = np.tile(np.arange(512, dtype=np.float32)[None, :], (128, 1))
    c["identb"] = np.eye(128).astype(ml_dtypes.bfloat16)
    return c


def attn_head_params(heads):
    sl = np.array([2.0 ** (-8.0 * (h + 1) / 16.0) for h in heads])
    row = np.concatenate([-sl / SCALE, -sl]).astype(np.float32)
    return np.tile(row[None, :], (128, 1))


def attn_body(nc, k, L, io):
    NB = L // 64
    NBR = min(128, NB)
    NBH = max(1, NB // 128)
    NCT = max(1, (L // 16) // 128)
    NKT = L // 128
    W = 512
    V, S_, PE = nc.vector, nc.scalar, nc.tensor
    with ExitStack() as es2:
        k.es, es_old = es2, k.es
        cst = {}
        for n, shp, dt in (("R0", [128, 512], F32), ("R0c", [128, 128], F32), ("Mc", [128, 2176], BF16),
                           ("Mw", [128, 1408], BF16), ("Mcaus", [128, 896], BF16), ("Esel", [NBR, 64, 128], BF16),
                           ("Aext", [128, NCT, NB + 1], BF16), ("cand", [128, 2 * NB], F32), ("negm", [128, 2 * NB], F32),
                           ("force", [128, 2 * NB], F32), ("Dtab", [128, 132], F32), ("Dtab3", [128, 132], F32), ("irow", [128, 512], F32),
                           ("identb", [128, 128], BF16),
                           ("hp", [128, 8], F32)):
            cst[n] = k.sb("a_" + n, shp, dt)
            k.dma("sp", cst[n][:], io[n], dst=cst[n])
        R0, R0c, Mc, Mw, Mcaus, Esel, Aext, identb, hp = (cst[n] for n in
                                                          ("R0", "R0c", "Mc", "Mw", "Mcaus", "Esel", "Aext", "identb", "hp"))
        Btab = k.sb("a_Btab", [128, 4, 132], F32)
        Btab3 = k.sb("a_Btab3", [128, 4, 132], F32)
        shrow = k.sb("a_shrow", [128, 4, 512], BF16)
        ones_b = k.sb("a_onesb", [128, 128], BF16)
        k.op("dve", lambda: V.memset(ones_b[:], 1.0), writes=[ones_b])
        for h in range(4):
            k.op("dve", lambda: V.tensor_scalar(out=Btab[:, h, :], in0=cst["Dtab"][:], scalar1=hp[:, 4 + h:5 + h], scalar2=None,
                                                op0=ALU.mult), reads=[cst["Dtab"], hp], parts=[Btab])
            k.op("dve", lambda: V.tensor_scalar(out=Btab3[:, h, :], in0=cst["Dtab3"][:], scalar1=hp[:, 4 + h:5 + h], scalar2=None,
                                                op0=ALU.mult), reads=[cst["Dtab3"], hp], parts=[Btab3])
            k.op("dve", lambda: V.tensor_scalar(out=shrow[:, h, :], in0=cst["irow"][:], scalar1=hp[:, h:h + 1], scalar2=None,
                                                op0=ALU.mult), reads=[cst["irow"], hp], parts=[shrow])
        pS = [k.ps(f"ap_S{i}", [128, 512], F32) for i in range(2)]
        pO = [k.ps(f"ap_O{i}", [128, 512], F32) for i in range(4)]
        pT = k.ps("ap_T", [128, 1024], BF16)
        pX = k.ps("ap_X", [128, 512], F32)
        ksT = k.sb("a_ksT", [128, L], BF16)
        vse = k.sb("a_vse", [128, NKT, 130], BF16)
        kccT = k.sb("a_kccT", [128, NCT * 128], BF16)
        vcce = k.sb("a_vcce", [128, NCT, 130], BF16)
        k.dma("sp", ksT[:], io["ksT"], dst=ksT)
        k.dma("sp", vse[:, :, 0:128], io["vs"].rearrange("(t p) d -> p t d", p=128), dst=vse)
        k.op("dve", lambda: V.memset(vse[:, :, 128:129], 1.0), parts=[vse])
        k.op("dve", lambda: V.memset(vcce[:, :, 128:129], 1.0), writes=[vcce])

        with ExitStack() as es3:
            k.es = es3
            src = k.sb("c_src", [128, L + 32], BF16)
            w1s = k.sb("c_w1s", [128, 8, 256], F32)
            w1b = k.sb("c_w1b", [128, 32, 256], BF16)
            w2s = k.sb("c_w2s", [128, 2, 128], F32)
            w2b = k.sb("c_w2b", [128, 2, 128], BF16)
            posf = k.sb("c_posf", [128, 32], F32)
            posb_ = k.sb("c_posb", [128, 32], BF16)
            pbias = k.sb("c_pbias", [128, 2], F32)
            xh = k.sb("c_xh", [128, 512], F32)
            t3 = k.sb("c_t3", [128, 512], F32)
            hidb = k.sb("c_hidb", [128, 2, NCT * 128], BF16)
            for which in ("k", "v"):
                k.op("dve", lambda: V.memset(src[:, L:L + 32], 0.0), writes=[src])
                k.dma("sp", src[:, 0:L], io["kcT" if which == "k" else "vcT"], dst=src, part=True)
                w1v = io["w1_" + which].rearrange("(i p) m -> p i m", p=128)
                for q4 in range(4):
                    k.dma("sp", w1s[:], w1v[:, q4 * 8:(q4 + 1) * 8, :], dst=w1s)
                    k.op("dve", lambda: V.tensor_copy(out=w1b[:, q4 * 8:(q4 + 1) * 8, :], in_=w1s[:]), reads=[w1s],
                         writes=[w1b] if q4 == 0 else [], parts=[w1b] if q4 else [])
                k.dma("sp", w2s[:], io["w2_" + which].rearrange("(t p) d -> p t d", p=128), dst=w2s)
                k.op("dve", lambda: V.tensor_copy(out=w2b[:], in_=w2s[:]), reads=[w2s], writes=[w2b])
                k.dma("sp", posf[:], io["posT_" + which], dst=posf)
                k.op("dve", lambda: V.tensor_copy(out=posb_[:], in_=posf[:]), reads=[posf], writes=[posb_])
                for mt in range(2):
                    for i in range(32):
                        k.op("pe", lambda: PE.matmul(pX[:, mt:mt + 1], lhsT=w1b[:, i, mt * 128:(mt + 1) * 128], rhs=posb_[:, i:i + 1],
                                                     start=(i == 0), stop=(i == 31)), reads=[w1b, posb_],
                             writes=[pX] if (i == 0 and mt == 0) else [], parts=[] if (i == 0 and mt == 0) else [pX])
                k.op("dve", lambda: V.tensor_copy(out=pbias[:], in_=pX[:, 0:2]), reads=[pX], writes=[pbias])
                CW = min(512, NCT * 128)
                for cc in range(NCT * 128 // CW):
                    for mt in range(2):
                        ps = pS[mt]
                        for i in range(32):
                            c0 = 16 * CW * cc + i
                            k.op("pe", lambda: PE.matmul(ps[:, 0:CW], lhsT=w1b[:, i, mt * 128:(mt + 1) * 128],
                                                         rhs=src[:, c0:c0 + 16 * (CW - 1) + 1:16], start=(i == 0), stop=(i == 31)),
                                 reads=[w1b, src], writes=[ps] if i == 0 else [], parts=[ps] if i else [])
                        k.op("dve", lambda: V.tensor_scalar(out=xh[:, 0:CW], in0=ps[:, 0:CW], scalar1=pbias[:, mt:mt + 1],
                                                            scalar2=None, op0=ALU.add), reads=[ps, pbias], writes=[xh])
                        k.op("dve", lambda: V.tensor_tensor(out=t3[:, 0:CW], in0=xh[:, 0:CW], in1=xh[:, 0:CW], op=ALU.mult),
                             reads=[xh], writes=[t3])
                        k.op("dve", lambda: V.tensor_tensor(out=t3[:, 0:CW], in0=t3[:, 0:CW], in1=xh[:, 0:CW], op=ALU.mult),
                             reads=[xh], writes=[t3])
                        k.op("dve", lambda: V.scalar_tensor_tensor(out=t3[:, 0:CW], in0=t3[:, 0:CW], scalar=0.044715,
                                                                   in1=xh[:, 0:CW], op0=ALU.mult, op1=ALU.add),
                             reads=[xh], writes=[t3])
                        k.op("act", lambda: S_.activation(out=t3[:, 0:CW], in_=t3[:, 0:CW], func=AF.Sigmoid,
                                                          scale=1.5957691216057308), reads=[t3], writes=[t3])
                        k.op("dve", lambda: V.tensor_tensor(out=hidb[:, mt, cc * CW:(cc + 1) * CW], in0=xh[:, 0:CW],
                                                            in1=t3[:, 0:CW], op=ALU.mult), reads=[xh, t3], parts=[hidb])
                if which == "k":
                    for cc in range(NCT * 128 // CW):
                        for mt in range(2):
                            k.op("pe", lambda: PE.matmul(pX[:, 0:CW], lhsT=w2b[:, mt, :], rhs=hidb[:, mt, cc * CW:(cc + 1) * CW],
                                                         start=(mt == 0), stop=(mt == 1)), reads=[w2b, hidb],
                                 writes=[pX] if mt == 0 else [], parts=[pX] if mt else [])
                        k.op("act", lambda: S_.copy(out=kccT[:, cc * CW:(cc + 1) * CW], in_=pX[:, 0:CW]), reads=[pX], parts=[kccT])
                else:
                    for ct in range(NCT):
                        for mt in range(2):
                            k.op("pe", lambda: PE.matmul(pX[:, 0:128], lhsT=hidb[:, mt, ct * 128:(ct + 1) * 128], rhs=w2b[:, mt, :],
                                                         start=(mt == 0), stop=(mt == 1)), reads=[w2b, hidb],
                                 writes=[pX] if mt == 0 else [], parts=[pX] if mt else [])
                        k.op("act", lambda: S_.copy(out=vcce[:, ct, 0:128], in_=pX[:, 0:128]), reads=[pX], parts=[vcce])
            k.release([src, w1s, w1b, w2s, w2b, posf, posb_, pbias, xh, t3, hidb])
            k.es = es2

        qT = [k.sb(f"a_qT{i}", [128, 4, W], BF16) for i in range(2)]
        glt = k.sb("a_gl", [128, 4, 6], F32)
        kwT = k.sb("a_kwT", [128, 1024], BF16)
        vwe = k.sb("a_vwe", [128, 8, 130], BF16)
        k.op("dve", lambda: V.memset(vwe[:, :, 128:129], 1.0), writes=[vwe])
        tmpS = [k.sb(f"a_tmpS{i}", [128, 512], F32) for i in range(3)]
        PT = [k.sb(f"a_PT{i}", [128, 512], BF16) for i in range(3)]
        imp = k.sb("a_imp", [128, NB], F32)
        imp2 = k.sb("a_imp2", [128, NB], F32)
        imp3 = k.sb("a_imp3", [128, NB], F32)
        m8 = k.sb("a_m8", [128, 16], F32)
        mneg = k.sb("a_mneg", [128, NB], BF16)
        maskT = k.sb("a_maskT", [NBR, NBH, W], BF16)
        maskTh2 = [k.sb(f"a_maskTh{i}", [NBR, NBH, W], BF16) for i in range(2)]
        den = k.sb("a_den", [128, 8], F32)
        oacc = [[k.sb(f"a_oacc{hh}_{i}", [128, 128], F32) for i in range(4)] for hh in range(2)]
        rot = [0]

        def score_tile(ps, lhs_k, rhs_q, ncols, extra, h, btab, bcol):
            ops = [(lhs_k, rhs_q)] + extra
            for oi, (l_, r_) in enumerate(ops):
                k.op("pe", lambda: PE.matmul(ps[:, 0:ncols], lhsT=l_[1], rhs=r_[1], start=(oi == 0), stop=(oi == len(ops) - 1)),
                     reads=[l_[0], r_[0]], writes=[ps] if oi == 0 else [], parts=[ps] if oi else [])
            pt_ = PT[rot[0] % 3]
            rot[0] += 1
            k.op("act", lambda: S_.activation(out=pt_[:, 0:ncols], in_=ps[:, 0:ncols], func=AF.Exp, scale=SCALE,
                                              bias=btab[:, h, bcol:bcol + 1]), reads=[ps, btab], writes=[pt_])
            return pt_

        def finish_branch(po, ncol, hh, i, b, first):
            dcol = den[:, 0:1]
            k.op("dve", lambda: V.tensor_scalar(out=den[:, 0:1], in0=po[:, ncol:ncol + 1], scalar1=TINY, scalar2=None,
                                                op0=ALU.max), reads=[po], writes=[den])
            k.op("dve", lambda: V.reciprocal(out=den[:, 1:2], in_=den[:, 0:1]), reads=[den], writes=[den])
            k.op("dve", lambda: V.tensor_tensor(out=den[:, 2:3], in0=den[:, 1:2], in1=glt[:, i, b * 2 + hh:b * 2 + hh + 1],
                                                op=ALU.mult), reads=[den, glt], writes=[den])
            oa = oacc[hh][i]
            if first:
                k.op("dve", lambda: V.tensor_scalar(out=oa[:], in0=po[:, 0:128], scalar1=den[:, 2:3], scalar2=None, op0=ALU.mult),
                     reads=[po, den], writes=[oa])
            else:
                k.op("dve", lambda: V.scalar_tensor_tensor(out=oa[:], in0=po[:, 0:128], scalar=den[:, 2:3], in1=oa[:],
                                                           op0=ALU.mult, op1=ALU.add), reads=[po, den], writes=[oa])

        for qc in range(L // W):
            t0 = qc * W
            q_ = qT[qc % 2]
            k.dma("sp", q_[:], io["qT"][:, t0:t0 + W].rearrange("(h p) n -> p h n", p=128), dst=q_)
            k.dma("sp", glt[:], io["gl"][t0:t0 + W, :].rearrange("(i p) c -> p i c", p=128), dst=glt)
            k.op("act", lambda: S_.activation(out=glt[:], in_=glt[:], func=AF.Sigmoid), writes=[glt])
            klo = max(0, t0 - 512)
            khi = min(L, t0 + 512)
            k.dma("sp", kwT[:, klo - (t0 - 512):khi - (t0 - 512)], io["kwT"][:, klo:khi], dst=kwT)
            k.dma("sp", vwe[:, (klo - (t0 - 512)) // 128:(khi - (t0 - 512)) // 128, 0:128],
                  io["vw"][klo:khi, :].rearrange("(t p) d -> p t d", p=128), dst=vwe, part=True)
            for i in range(4):
                t = t0 + 128 * i
                ctmax = min(NCT - 1, (t + 127) // 2048)
                for h in range(4):
                    own = h < 2
                    pU, pOc = (pO[0], pO[1]) if h % 2 == 0 else (pO[2], pO[3])
                    def c_front(ct):
                        dlt = t - 2048 * ct
                        extra = [((ones_b, ones_b[0:1, :]), (shrow, shrow[0:1, h, 0:128]))]
                        if dlt <= 2048:
                            extra.append(((identb, identb[:]), (Mc, Mc[:, dlt:dlt + 128])))
                        return score_tile(pS[rot[0] % 2], (kccT, kccT[:, ct * 128:(ct + 1) * 128]),
                                          (q_, q_[:, h, 128 * i:128 * i + 128]), 128, extra, h, Btab3, dlt // 128 + 3)

                    def c_back(ct, pt_):
                        k.op("pe", lambda: PE.matmul(pU[:, 0:NB + 1], lhsT=pt_[:, 0:128], rhs=Aext[:, ct, :], start=(ct == 0),
                                                     stop=(ct == ctmax)), reads=[pt_, Aext],
                             writes=[pU] if ct == 0 else [], parts=[pU] if ct else [])
                        if own:
                            k.op("pe", lambda: PE.matmul(pOc[:, 0:129], lhsT=pt_[:, 0:128], rhs=vcce[:, ct, 0:129], start=(ct == 0),
                                                         stop=(ct == ctmax)), reads=[pt_, vcce],
                                 writes=[pOc] if ct == 0 else [], parts=[pOc] if ct else [])
                    prev = None
                    for ct in range(ctmax + 1):
                        cur = (ct, c_front(ct))
                        if prev is not None:
                            c_back(*prev)
                        prev = cur
                    c_back(*prev)
                    k.op("dve", lambda: V.tensor_scalar(out=den[:, 4:5], in0=pU[:, NB:NB + 1], scalar1=TINY, scalar2=None,
                                                        op0=ALU.max), reads=[pU], writes=[den])
                    k.op("dve", lambda: V.reciprocal(out=den[:, 5:6], in_=den[:, 4:5]), reads=[den], writes=[den])
                    if h == 0:
                        k.op("dve", lambda: V.tensor_scalar(out=imp[:], in0=pU[:, 0:NB], scalar1=den[:, 5:6], scalar2=None,
                                                            op0=ALU.mult), reads=[pU, den], writes=[imp])
                    else:
                        k.op("dve", lambda: V.scalar_tensor_tensor(out=imp[:], in0=pU[:, 0:NB], scalar=den[:, 5:6], in1=imp[:],
                                                                   op0=ALU.mult, op1=ALU.add), reads=[pU, den], writes=[imp])
                    if own:
                        finish_branch(pOc, 128, h, i, 0, True)
                qt = t // 128
                x0 = NB - 2 * qt
                k.op("dve", lambda: V.tensor_tensor(out=imp2[:], in0=imp[:], in1=cst["cand"][:, x0:x0 + NB], op=ALU.mult),
                     reads=[imp, cst["cand"]], writes=[imp2])
                k.op("dve", lambda: V.tensor_tensor(out=imp2[:], in0=imp2[:], in1=cst["negm"][:, x0:x0 + NB], op=ALU.add),
                     reads=[cst["negm"]], writes=[imp2])
                k.op("dve", lambda: V.tensor_tensor(out=imp2[:], in0=imp2[:], in1=cst["force"][:, x0:x0 + NB], op=ALU.max),
                     reads=[cst["force"]], writes=[imp2])
                k.op("dve", lambda: V.memset(imp2[:, 0:1], 1e4), writes=[imp2])
                k.op("dve", lambda: V.max(out=m8[:, 0:8], in_=imp2[:]), reads=[imp2], writes=[m8])
                k.op("dve", lambda: V.match_replace(out=imp3[:], in_to_replace=m8[:, 0:8], in_values=imp2[:], imm_value=-3.0),
                     reads=[imp2, m8], writes=[imp3])
                k.op("dve", lambda: V.max(out=m8[:, 8:16], in_=imp3[:]), reads=[imp3], writes=[m8])
                k.op("dve", lambda: V.scalar_tensor_tensor(out=imp3[:], in0=imp2[:], scalar=m8[:, 15:16],
                                                           in1=cst["cand"][:, x0:x0 + NB], op0=ALU.is_ge, op1=ALU.mult),
                     reads=[imp2, m8, cst["cand"]], writes=[imp3])
                k.op("dve", lambda: V.tensor_scalar(out=mneg[:], in0=imp3[:], scalar1=-NEG, scalar2=NEG, op0=ALU.mult,
                                                    op1=ALU.add), reads=[imp3], writes=[mneg])
                for hf in range(NBH):
                    k.op("pe", lambda: PE.transpose(pT[0:NBR, hf * 128:(hf + 1) * 128], mneg[:, hf * NBR:(hf + 1) * NBR], identb[:]),
                         reads=[mneg, identb], writes=[pT] if hf == 0 else [], parts=[pT] if hf else [])
                for hf in range(NBH):
                    k.op("act", lambda: S_.copy(out=maskT[:, hf, 128 * i:128 * i + 128], in_=pT[0:NBR, hf * 128:(hf + 1) * 128]),
                         reads=[pT], writes=[maskT] if (i == 0 and hf == 0) else [], parts=[] if (i == 0 and hf == 0) else [maskT])
            for hh in range(2):
                maskTh = maskTh2[hh]
                for hf in range(NBH):
                    k.op("dve", lambda: V.tensor_tensor(out=maskTh[:, hf, :], in0=maskT[:, hf, :], in1=shrow[0:NBR, hh, :], op=ALU.add),
                         reads=[maskT, shrow], writes=[maskTh] if hf == 0 else [], parts=[maskTh] if hf else [])
                kts = list(range(0, (t0 + W) // 128))
                def s_front(kt):
                    k0 = kt * 128
                    extra = [((Esel, Esel[:, kt % 64, :]), (maskTh, maskTh[:, kt // 64, :]))]
                    if k0 >= t0:
                        xo = t0 - k0 + 384
                        extra.append(((identb, identb[:]), (Mcaus, Mcaus[:, xo:xo + W])))
                    return score_tile(pS[rot[0] % 2], (ksT, ksT[:, k0:k0 + 128]), (q_, q_[:, hh, :]), W, extra, hh, Btab,
                                      (t0 - k0) // 128 + 3)

                def s_back(ki, kt, pt_):
                    for i in range(4):
                        k.op("pe", lambda: PE.matmul(pO[i][:, 0:129], lhsT=pt_[:, 128 * i:128 * i + 128], rhs=vse[:, kt, 0:129],
                                                     start=(ki == 0), stop=(ki == len(kts) - 1)), reads=[pt_, vse],
                             writes=[pO[i]] if ki == 0 else [], parts=[pO[i]] if ki else [])
                prev = None
                for ki, kt in enumerate(kts):
                    cur = (ki, kt, s_front(kt))
                    if prev is not None:
                        s_back(*prev)
                    prev = cur
                s_back(*prev)
                for i in range(4):
                    finish_branch(pO[i], 128, hh, i, 1, False)
            for hh in range(2):
                kts = [m for m in range(8) if t0 - 512 + 128 * m >= 0 and t0 - 512 + 128 * m < L]
                def w_front(m):
                    k0 = t0 - 512 + 128 * m
                    xo = t0 - k0 + 384
                    extra = [((identb, identb[:]), (Mw, Mw[:, xo:xo + W])), ((ones_b, ones_b[0:1, :]), (shrow, shrow[0:1, hh, :]))]
                    return score_tile(pS[rot[0] % 2], (kwT, kwT[:, 128 * m:128 * m + 128]), (q_, q_[:, hh, :]), W, extra, hh, Btab,
                                      (t0 - k0) // 128 + 3)

                def w_back(ki, m, pt_):
                    for i in range(4):
                        k.op("pe", lambda: PE.matmul(pO[i][:, 0:129], lhsT=pt_[:, 128 * i:128 * i + 128], rhs=vwe[:, m, 0:129],
                                                     start=(ki == 0), stop=(ki == len(kts) - 1)), reads=[pt_, vwe],
                             writes=[pO[i]] if ki == 0 else [], parts=[pO[i]] if ki else [])
                prev = None
                for ki, m in enumerate(kts):
                    cur = (ki, m, w_front(m))
                    if prev is not None:
                        w_back(*prev)
                    prev = cur
                w_back(*prev)
                for i in range(4):
                    finish_branch(pO[i], 128, hh, i, 2, False)
            for hh in range(2):
                for i in range(4):
                    oa = oacc[hh][i]
                    k.dma("sp", io["ya"][t0 + 128 * i:t0 + 128 * i + 128, 128 * hh:128 * hh + 128], oa[:], src=oa)
        k.finish([oacc[hh][i] for hh in range(2) for i in range(4)])
        k.es = es_old


ATTN_IN = lambda L: {"qT": ([512, L], BF16), "kcT": ([128, L], BF16), "vcT": ([128, L], BF16), "ksT": ([128, L], BF16),
                     "kwT": ([128, L], BF16), "vs": ([L, 128], BF16), "vw": ([L, 128], BF16), "gl": ([L, 6], F32),
                     "posT_k": ([128, 32], F32), "w1_k": ([4096, 256], F32), "w2_k": ([256, 128], F32),
                     "posT_v": ([128, 32], F32), "w1_v": ([4096, 256], F32), "w2_v": ([256, 128], F32), "hp": ([128, 8], F32)}


def build_attn(L):
    nc = bass.Bass("TRN2", target_bir_lowering=False)
    io = {}
    cs = attn_consts(L)
    for n, (s, d) in ATTN_IN(L).items():
        io[n] = nc.dram_tensor(n, s, d, kind="ExternalInput").ap()
    for n, a in cs.items():
        d = BF16 if a.dtype == ml_dtypes.bfloat16 else F32
        io[n] = nc.dram_tensor(n, list(a.shape), d, kind="ExternalInput").ap()
    io["ya"] = nc.dram_tensor("ya", [L, 256], F32, kind="ExternalOutput").ap()
    with ExitStack() as es:
        k = KB(nc, es)
        attn_body(nc, k, L, io)
        print("attn kernel instructions:", k.n_ins)
    return nc


def proj_pack_w(w, blocks):
    D = w.shape[0]
    parts = [np.ascontiguousarray(w[:, c0:c0 + ncol].reshape(D // 128, 128, ncol).transpose(1, 0, 2)).reshape(-1)
             for (c0, ncol, _k, _n, _o) in blocks]
    return np.concatenate(parts) if parts else np.zeros(1, np.float32)


DFF = 5632
_bf = lambda a: np.ascontiguousarray(a).astype(ml_dtypes.bfloat16)
_col = lambda v: np.ascontiguousarray(np.asarray(v, np.float32).reshape(-1, 128).T)
_PROG = {}


def _run(nc, in_maps):
    res = run_bass_kernel_spmd(nc, in_maps, core_ids=list(range(NCORES)))
    return res.results


def kernel(**inp):
    L, NT = SEQ, SEQ // NCORES
    f32 = lambda a: np.ascontiguousarray(np.asarray(a, np.float32))
    x = f32(inp["x"])[0]
    identb = np.eye(128).astype(ml_dtypes.bfloat16)
    blocks, nfm = p1_blocks()
    spec = {"fm": ((nfm, NT), BF16), "dtT": ((32, NT), F32), "z": ((NT, 2048), F32), "vtm": ((NT, 1024), BF16),
            "gl": ((NT, 48), F32)}
    nc1 = build_proj(D_MODEL, NT, C_END, blocks, spec)
    w_in = proj_pack_w(f32(inp["w_in"])[0], blocks)
    nw1 = _col(inp["mix_norm_w"][0])
    r1 = _run(nc1, [{"x": x[c * NT:(c + 1) * NT], "nw": nw1, "w": w_in, "identb": identb} for c in range(NCORES)])
    FM = np.concatenate([np.asarray(r["fm"]) for r in r1], axis=1)
    DT = np.concatenate([np.asarray(r["dtT"]) for r in r1], axis=1)
    Z = np.concatenate([np.asarray(r["z"]) for r in r1], axis=0)
    VTM = np.concatenate([np.asarray(r["vtm"]) for r in r1], axis=0)
    GL = np.concatenate([np.asarray(r["gl"]) for r in r1], axis=0)
    del r1
    RQ, RKC, RVC, RKS, RKW = 3072, 5120, 5632, 6144, 6656
    nc2 = build_ssd(L)
    sc = ssd_consts()
    convw = f32(inp["ssm_conv_w"])[0]
    convb = f32(inp["ssm_conv_b"])[0]
    maps = []
    for c in range(NCORES):
        g = c // 2
        chs = np.concatenate([np.arange(256 * c, 256 * c + 256), np.arange(2048 + 128 * g, 2048 + 128 * g + 128),
                              np.arange(2560 + 128 * g, 2560 + 128 * g + 128)])
        xT = np.zeros((512, 3 + L), ml_dtypes.bfloat16)
        xT[:, 3:] = FM[chs]
        cw = convw[:, chs]
        cb = convb[chs]
        m = dict(sc)
        m.update({"xbcT": xT, "cw": np.ascontiguousarray(cw.T.reshape(4, 128, 4).transpose(1, 0, 2)),
                  "cb": np.ascontiguousarray(cb.reshape(4, 128).T), "cbrow": np.ascontiguousarray(cb[None, :]),
                  "dtT": np.ascontiguousarray(DT[4 * c:4 * c + 4]), "dtb": f32(inp["ssm_dt_bias"])[0, 4 * c:4 * c + 4, None],
                  "alog": f32(inp["ssm_a_log"])[0, 4 * c:4 * c + 4, None],
                  "drow": np.ascontiguousarray(np.tile(np.repeat(f32(inp["ssm_d"])[0, 4 * c:4 * c + 4], 64)[None, :], (128, 1))),
                  "z": np.ascontiguousarray(Z[:, 256 * c:256 * c + 256])})
        maps.append(m)
    r2 = _run(nc2, maps)
    YS = np.concatenate([np.asarray(r["ys"]) for r in r2], axis=1)
    del r2, maps
    nc3 = build_attn(L)
    ac = attn_consts(L)
    maps = []
    for c in range(NCORES):
        g = c // 2
        own = [2 * c, 2 * c + 1]
        heads = own + [h for h in range(4 * g, 4 * g + 4) if h not in own]
        qrows = np.concatenate([np.arange(RQ + 128 * h, RQ + 128 * h + 128) for h in heads])
        glc = np.array([b * 16 + h for b in range(3) for h in own])
        m = dict(ac)
        m.update({"qT": np.ascontiguousarray(FM[qrows]), "kcT": np.ascontiguousarray(FM[RKC + 128 * g:RKC + 128 * g + 128]),
                  "vcT": np.ascontiguousarray(FM[RVC + 128 * g:RVC + 128 * g + 128]),
                  "ksT": np.ascontiguousarray(FM[RKS + 128 * g:RKS + 128 * g + 128]),
                  "kwT": np.ascontiguousarray(FM[RKW + 128 * g:RKW + 128 * g + 128]),
                  "vs": np.ascontiguousarray(VTM[:, 128 * g:128 * g + 128]),
                  "vw": np.ascontiguousarray(VTM[:, 512 + 128 * g:512 + 128 * g + 128]),
                  "gl": np.ascontiguousarray(GL[:, glc]),
                  "posT_k": np.ascontiguousarray(f32(inp["cmp_pos_k"])[0].T), "w1_k": f32(inp["cmp_w1_k"])[0],
                  "w2_k": f32(inp["cmp_w2_k"])[0],
                  "posT_v": np.ascontiguousarray(f32(inp["cmp_pos_v"])[0].T), "w1_v": f32(inp["cmp_w1_v"])[0],
                  "w2_v": f32(inp["cmp_w2_v"])[0], "hp": attn_head_params(heads)})
        maps.append(m)
    r3 = _run(nc3, maps)
    YA = np.concatenate([np.asarray(r["ya"]) for r in r3], axis=1)
    del r3, maps
    nc4 = build_tail(D_MODEL, NT, DFF, GT=512)
    fcw = f32(inp["ffn_conv_w"])[0]
    common = {"nwa": _col(inp["attn_norm_w"][0]), "nws": _col(inp["ssm_norm_w"][0]), "nwf": _col(inp["ffn_norm_w"][0]),
              "fwb": np.ascontiguousarray(np.tile(f32(inp["final_norm_w"])[None, :], (128, 1))),
              "w_out": np.ascontiguousarray(f32(inp["w_out"])[0].reshape(32, 128, 8, 256).transpose(2, 1, 0, 3)),
              "w_up": np.ascontiguousarray(f32(inp["w_up"])[0].reshape(16, 128, 2, DFF // 128, 128).transpose(3, 1, 0, 2, 4)
                                           .reshape(DFF // 128, 128, 16, 256)),
              "w_down": np.ascontiguousarray(f32(inp["w_down"])[0].reshape(DFF // 128, 128, 8, 256).transpose(2, 1, 0, 3)),
              "cw": np.ascontiguousarray(fcw.T.reshape(2 * DFF // 128, 128, 3).transpose(1, 0, 2)),
              "cb": _col(inp["ffn_conv_b"][0]), "identb": identb}

    def halo(a, c):
        o = np.zeros((NT + 128, a.shape[1]), np.float32)
        lo = c * NT - 128
        if lo < 0:
            o[128:] = a[0:NT]
        else:
            o[:] = a[lo:lo + NT + 128]
        return o
    maps = []
    for c in range(NCORES):
        m = dict(common)
        m.update({"x": halo(x, c), "ya": halo(YA, c), "ys": halo(YS, c)})
        maps.append(m)
    r4 = _run(nc4, maps)
    out = np.concatenate([np.asarray(r["out"]) for r in r4], axis=0)
    return out[None].astype(np.float32)
```
